# Optimizing a Trainium2 kernel written in Bass

```python
import jax, jax.numpy as jnp
from jax import lax
import numpy as np

D_MODEL = 1024
BATCH = 16
SEQ = 2048
DEPTH = 1

N_META = 16
BLOCK = 128
PAD_FRONT = BLOCK - N_META
WINDOW = 128

HEAD_DIM = 64
N_Q_HEADS = D_MODEL // HEAD_DIM
N_KV_HEADS = max(1, N_Q_HEADS // 8)
GQA_GROUP = N_Q_HEADS // N_KV_HEADS
ATT_WIDTH = N_Q_HEADS * HEAD_DIM
KV_WIDTH = N_KV_HEADS * HEAD_DIM
ROPE_THETA = 10000.0

RWKV_HEAD = 64
RWKV_HEADS = D_MODEL // RWKV_HEAD
RWKV_WIDTH = RWKV_HEADS * RWKV_HEAD
LORA_W = max(32, int(round(1.8 * D_MODEL ** 0.5 / 32)) * 32)
LORA_A = max(32, int(round(1.8 * D_MODEL ** 0.5 / 32)) * 32)
LORA_G = max(32, int(round(0.6 * D_MODEL ** 0.8 / 32)) * 32)
RWKV_COLS = 3 * RWKV_WIDTH + LORA_W + LORA_A + LORA_G
GN_EPS = 64e-5

GATE_COLS = 2 * D_MODEL
IN_COLS = ATT_WIDTH + 2 * KV_WIDTH + RWKV_COLS + GATE_COLS

D_FF = int(round(8 * D_MODEL / 3 / 256)) * 256
RMS_EPS = 1e-6
NEG = -1e30

kernel_name = "hybrid_swa_sink_rwkv7_macaron_meta"


def rmsnorm(x, g):
    xf = x.astype(jnp.float32)
    y = xf * lax.rsqrt(jnp.mean(xf * xf, axis=-1, keepdims=True) + RMS_EPS)
    return (y * g.astype(jnp.float32)).astype(x.dtype)


def swiglu(x, w_in, w_out):
    gate, up = jnp.split(x @ w_in, 2, axis=-1)
    return (jax.nn.silu(gate) * up) @ w_out


def rope(x, pos):
    half = HEAD_DIM // 2
    inv = ROPE_THETA ** (-jnp.arange(half, dtype=jnp.float32) / half)
    ang = pos.astype(jnp.float32)[:, None] * inv[None, :]
    cos = jnp.cos(ang)[None, :, None, :].astype(x.dtype)
    sin = jnp.sin(ang)[None, :, None, :].astype(x.dtype)
    x1, x2 = x[..., :half], x[..., half:]
    return jnp.concatenate([x1 * cos - x2 * sin, x2 * cos + x1 * sin], axis=-1)


def window_mask(nb):
    blk = jnp.arange(nb)[:, None, None]
    qi = blk * BLOCK + jnp.arange(BLOCK)[None, :, None]
    j = jnp.arange(3 * BLOCK)[None, None, :]
    slot, part = j % BLOCK, j // BLOCK
    k_idx = jnp.where(part == 0, slot, (blk + part - 2) * BLOCK + slot)
    meta_ok = (part == 0) & (k_idx >= PAD_FRONT) & (k_idx < BLOCK) & (k_idx <= qi)
    win_ok = (part > 0) & (k_idx >= BLOCK) & (qi - k_idx >= 0) & (qi - k_idx < WINDOW)
    return meta_ok | win_ok


def sliding_window_attention(q, k, v, sinks):
    B, T = q.shape[0], q.shape[1]
    nb = T // BLOCK
    qb = q.reshape(B, nb, BLOCK, N_KV_HEADS, GQA_GROUP, HEAD_DIM) * (HEAD_DIM ** -0.5)
    kb = k.reshape(B, nb, BLOCK, N_KV_HEADS, HEAD_DIM)
    vb = v.reshape(B, nb, BLOCK, N_KV_HEADS, HEAD_DIM)

    def band(t):
        prev = jnp.pad(t, ((0, 0), (1, 0), (0, 0), (0, 0), (0, 0)))[:, :-1]
        meta = jnp.broadcast_to(t[:, :1], t.shape)
        return jnp.concatenate([meta, prev, t], axis=2)

    keys, vals = band(kb), band(vb)
    s = jnp.einsum('bnqhgd,bnshd->bnhgqs', qb, keys).astype(jnp.float32)
    allowed = window_mask(nb)[None, :, None, None]
    s = jnp.where(allowed, s, NEG)
    sink = sinks.astype(jnp.float32).reshape(N_KV_HEADS, GQA_GROUP)[None, None, :, :, None, None]
    m = jnp.maximum(jnp.max(s, axis=-1, keepdims=True), sink)
    e = jnp.exp(s - m)
    p = e / (jnp.sum(e, axis=-1, keepdims=True) + jnp.exp(sink - m))
    o = jnp.einsum('bnhgqs,bnshd->bnqhgd', p.astype(v.dtype), vals)
    return o.reshape(B, T, ATT_WIDTH)


def token_shift_lerp(p, mu):
    prev = jnp.pad(p, ((0, 0), (1, 0), (0, 0)))[:, :-1]
    return p + (prev - p) * mu


def rwkv7_step(S, inp):
    r_t, w_t, k_t, v_t, a_t, b_t = inp
    sa = jnp.einsum('bhvk,bhk->bhv', S, a_t)
    S = S * w_t[:, :, None, :] + sa[..., None] * b_t[:, :, None, :] + v_t[..., None] * k_t[:, :, None, :]
    return S, jnp.einsum('bhvk,bhk->bhv', S, r_t)


def rwkv7_time_mix(r, k, v, wd, ad, gd, w0, w2, a0, a2, g2, k_k, k_a, r_k, lnx_w, lnx_b):
    B, T, C = r.shape
    f32 = jnp.float32
    w = -jax.nn.softplus(-(w0 + jnp.tanh(wd) @ w2)) - 0.5
    decay = jnp.exp(-jnp.exp(w.astype(f32)))
    a = jax.nn.sigmoid(a0 + ad @ a2)
    g = jax.nn.sigmoid(gd) @ g2
    kk = (k * k_k).astype(f32).reshape(B, T, RWKV_HEADS, RWKV_HEAD)
    kk = kk / jnp.maximum(jnp.sqrt(jnp.sum(kk * kk, axis=-1, keepdims=True)), 1e-12)
    k = k * (1.0 + (a - 1.0) * k_a)

    def heads(t):
        return t.astype(f32).reshape(B, T, RWKV_HEADS, RWKV_HEAD)

    rh, wh, kh, vh, ah = heads(r), heads(decay), heads(k), heads(v), heads(a)
    xs = tuple(jnp.moveaxis(t, 1, 0) for t in (rh, wh, kh, vh, -kk, kk * ah))
    S0 = jnp.zeros((B, RWKV_HEADS, RWKV_HEAD, RWKV_HEAD), f32)
    _, y = lax.scan(rwkv7_step, S0, xs)
    y = jnp.moveaxis(y, 0, 1)
    mu = jnp.mean(y, axis=-1, keepdims=True)
    var = jnp.mean(jnp.square(y - mu), axis=-1, keepdims=True)
    y = ((y - mu) * lax.rsqrt(var + GN_EPS)).reshape(B, T, C) * lnx_w + lnx_b
    bonus = jnp.sum(rh * kh * r_k, axis=-1, keepdims=True) * vh
    y = y + bonus.reshape(B, T, C)
    return (y * g.astype(f32)).astype(r.dtype)


def setup_inputs(seed: int = 0) -> dict:
    key = jax.random.key(seed)
    ks = jax.random.split(key, 32)
    f32 = jnp.float32
    L = DEPTH

    def nrm(k, shape, scale):
        return jax.random.normal(k, shape, f32) * scale

    def gain(k, shape):
        return 1.0 + 0.02 * jax.random.normal(k, shape, f32)

    return {
        "x": jax.random.normal(ks[0], (BATCH, SEQ, D_MODEL), f32),
        "meta_tokens": nrm(ks[1], (N_META, D_MODEL), 1.0),
        "norm_ffn1": gain(ks[2], (L, D_MODEL)),
        "ffn1_w_in": nrm(ks[3], (L, D_MODEL, 2 * D_FF), D_MODEL ** -0.5),
        "ffn1_w_out": nrm(ks[4], (L, D_FF, D_MODEL), D_FF ** -0.5),
        "norm_mix": gain(ks[5], (L, D_MODEL)),
        "w_in": nrm(ks[6], (L, D_MODEL, IN_COLS), D_MODEL ** -0.5),
        "rwkv_mu": jax.random.uniform(ks[7], (L, RWKV_COLS), f32),
        "sinks": nrm(ks[8], (L, N_Q_HEADS), 1.0),
        "w0": jax.random.uniform(ks[9], (L, RWKV_WIDTH), f32, -6.0, -1.0),
        "w2": nrm(ks[10], (L, LORA_W, RWKV_WIDTH), 0.1 * LORA_W ** -0.5),
        "a0": nrm(ks[11], (L, RWKV_WIDTH), 0.1),
        "a2": nrm(ks[12], (L, LORA_A, RWKV_WIDTH), 0.1 * LORA_A ** -0.5),
        "g2": nrm(ks[13], (L, LORA_G, RWKV_WIDTH), LORA_G ** -0.5),
        "k_k": 0.85 + nrm(ks[14], (L, RWKV_WIDTH), 0.02),
        "k_a": gain(ks[15], (L, RWKV_WIDTH)),
        "r_k": nrm(ks[16], (L, RWKV_HEADS, RWKV_HEAD), 0.1),
        "lnx_w": gain(ks[17], (L, RWKV_WIDTH)),
        "lnx_b": nrm(ks[18], (L, RWKV_WIDTH), 0.01),
        "w_attn_branch": nrm(ks[19], (L, ATT_WIDTH, D_MODEL), ATT_WIDTH ** -0.5),
        "w_rwkv_branch": nrm(ks[20], (L, RWKV_WIDTH, D_MODEL), RWKV_WIDTH ** -0.5),
        "w_out": nrm(ks[21], (L, D_MODEL, D_MODEL), D_MODEL ** -0.5),
        "norm_ffn2": gain(ks[22], (L, D_MODEL)),
        "ffn2_w_in": nrm(ks[23], (L, D_MODEL, 2 * D_FF), D_MODEL ** -0.5),
        "ffn2_w_out": nrm(ks[24], (L, D_FF, D_MODEL), D_FF ** -0.5),
        "norm_final": gain(ks[25], (D_MODEL,)),
    }


def reference(x, meta_tokens, norm_ffn1, ffn1_w_in, ffn1_w_out, norm_mix, w_in, rwkv_mu,
              sinks, w0, w2, a0, a2, g2, k_k, k_a, r_k, lnx_w, lnx_b,
              w_attn_branch, w_rwkv_branch, w_out, norm_ffn2, ffn2_w_in, ffn2_w_out,
              norm_final):
    B, S, D = x.shape
    dt = x.dtype
    meta = jnp.broadcast_to(meta_tokens.astype(dt)[None], (B, N_META, D))
    pad = jnp.zeros((B, PAD_FRONT, D), dt)
    h = jnp.concatenate([pad, meta, x], axis=1)
    T = h.shape[1]
    idx = jnp.arange(T)
    valid = (idx >= PAD_FRONT).astype(dt)[None, :, None]
    pos = idx - PAD_FRONT
    cols = list(np.cumsum([ATT_WIDTH, KV_WIDTH, KV_WIDTH, RWKV_COLS]))
    rcols = list(np.cumsum([RWKV_WIDTH, RWKV_WIDTH, RWKV_WIDTH, LORA_W, LORA_A]))

    for l in range(DEPTH):
        h = h + 0.5 * swiglu(rmsnorm(h, norm_ffn1[l]), ffn1_w_in[l], ffn1_w_out[l])

        u = rmsnorm(h, norm_mix[l]) * valid
        P = u @ w_in[l]
        q, kq, vq, rw, gates = jnp.split(P, cols, axis=-1)
        q = rope(q.reshape(B, T, N_Q_HEADS, HEAD_DIM), pos)
        kq = rope(kq.reshape(B, T, N_KV_HEADS, HEAD_DIM), pos)
        vq = vq.reshape(B, T, N_KV_HEADS, HEAD_DIM)
        att = sliding_window_attention(q, kq, vq, sinks[l])

        rw = token_shift_lerp(rw, rwkv_mu[l])
        r_, k_, v_, wd, ad, gd = jnp.split(rw, rcols, axis=-1)
        rwo = rwkv7_time_mix(r_, k_, v_, wd, ad, gd, w0[l], w2[l], a0[l], a2[l], g2[l],
                             k_k[l], k_a[l], r_k[l], lnx_w[l], lnx_b[l])

        g_att, g_rwkv = jnp.split(gates, 2, axis=-1)
        merged = (jax.nn.sigmoid(g_att) * (att @ w_attn_branch[l])
                  + jax.nn.sigmoid(g_rwkv) * (rwo @ w_rwkv_branch[l]))
        h = h + merged @ w_out[l]

        h = h + 0.5 * swiglu(rmsnorm(h, norm_ffn2[l]), ffn2_w_in[l], ffn2_w_out[l])

    return rmsnorm(h, norm_final)[:, BLOCK:]
```

```python
import contextlib
import numpy as np
import concourse.bass as bass
import concourse.mybir as mybir
from concourse.bass_utils import run_bass_kernel_spmd

F32 = mybir.dt.float32
BF16 = mybir.dt.bfloat16
AF = mybir.ActivationFunctionType
ALU = mybir.AluOpType

NCORES = 8
B, SEQ, DM = 16, 2048, 1024
SPC = B // NCORES
T = SEQ + 128
NBLK = T // 128
KC = DM // 128
FF = 2816
FC = FF // 128
NQH, NKVH, HD = 16, 2, 64
LW, LA, LG = 64, 64, 160
RWC = 3 * 1024 + LW + LA + LG
INC = 1024 + 256 + RWC + 2048
RMS_EPS = 1e-6
GN_EPS = 64e-5


_SBN = [0]


def _sbt(nc, name, shape, dt):
    _SBN[0] += 1
    return nc.sbuf_tensor("%s_u%d" % (name, _SBN[0]), list(shape), dt)


class Tok:
    __slots__ = ("w", "r")

    def __init__(self):
        self.w = None
        self.r = {}


class Sched:
    def __init__(self, nc, stack):
        self.nc = nc
        self.stack = stack
        self.E = {"pe": nc.tensor, "act": nc.scalar, "dve": nc.vector, "pool": nc.gpsimd, "sp": nc.sync}
        self.sem = {k: stack.enter_context(nc.semaphore("s_" + k)) for k in ("pe", "act", "dve", "pool")}
        self.cnt = {k: 0 for k in self.E}
        self.seen = {k: {} for k in self.E}
        self.toks = {}
        self.dsem = {}
        self.dcnt = {}
        self.nwaits = 0

    def tok(self, key):
        t = self.toks.get(key)
        if t is None:
            t = self.toks[key] = Tok()
        return t

    def _semof(self, e):
        if isinstance(e, tuple):
            return self.dsem[e[1]], 16
        return self.sem[e], 1

    def _need(self, r, w):
        need = {}
        for k in r:
            t = self.tok(k)
            if t.w is not None:
                if need.get(t.w[0], 0) < t.w[1]:
                    need[t.w[0]] = t.w[1]
        for k in w:
            t = self.tok(k)
            if t.w is not None:
                if need.get(t.w[0], 0) < t.w[1]:
                    need[t.w[0]] = t.w[1]
            for e2, n in t.r.items():
                if need.get(e2, 0) < n:
                    need[e2] = n
        return need

    def _waits(self, eng, need, myidx, is_dma=False):
        E = self.E[eng]
        seen = self.seen[eng]
        for e2, n in need.items():
            if e2 == eng and not is_dma:
                if eng == "pe" or (eng != "pool" and n < myidx - 3):
                    continue
            if n <= seen.get(e2, 0):
                continue
            seen[e2] = n
            sem, mult = self._semof(e2)
            E.wait_ge(sem, n * mult)
            self.nwaits += 1

    def op(self, eng, fn, r=(), w=(), inc=True):
        psr = [k for k in r if isinstance(k, tuple) and k[0] == "ps"]
        if psr:
            w = list(w) + [k for k in psr if k not in w]
        idx = self.cnt[eng] + 1
        need = self._need(r, w)
        self._waits(eng, need, idx)
        ins = fn(self.E[eng])
        if inc:
            ins.then_inc(self.sem[eng], 1)
            self.cnt[eng] = idx
        for k in r:
            t = self.tok(k)
            if t.r.get(eng, 0) < idx:
                t.r[eng] = idx
        for k in w:
            t = self.tok(k)
            t.w = (eng, idx)
            t.r = {}
        return ins

    def dma(self, q, out, in_, r=(), w=(), key=None, **kw):
        if key not in self.dsem:
            self.dsem[key] = self.stack.enter_context(self.nc.semaphore("d_%d" % len(self.dsem)))
            self.dcnt[key] = 0
        need = self._need(r, w)
        self._waits(q, need, 0, is_dma=True)
        self.dcnt[key] += 1
        k = self.dcnt[key]
        pe = ("dma", key)
        ins = self.E[q].dma_start(out=out, in_=in_, **kw)
        ins.then_inc(self.dsem[key], 16)
        for kk in r:
            self.tok(kk).r[pe] = k
        for kk in w:
            t = self.tok(kk)
            t.w = (pe, k)
            t.r = {}
        return ins

    def barrier(self):
        for e in ("pe", "act", "dve", "pool", "sp"):
            need = {e2: self.cnt[e2] for e2 in ("pe", "act", "dve", "pool") if self.cnt[e2] > 0}
            for key, k in self.dcnt.items():
                need[("dma", key)] = k
            self._waits(e, need, 1 << 60, is_dma=True)

    def finish(self):
        self.barrier()


def _mm(s, out, lhsT, rhs, start, stop, r, w, inc=None):
    if inc is None:
        inc = stop
    return s.op("pe", lambda e: e.matmul(out, lhsT=lhsT, rhs=rhs, start=start, stop=stop), r=r, w=w, inc=inc)


def token_tiles(t0, t1, n=512):
    out = []
    while t0 < t1:
        m = min(n, t1 - t0)
        out.append((t0, m))
        t0 += m
    return out


Q0, K0, V0, R0, RK0, RV0, WD0, GD0, GA0, GR0 = 0, 1024, 1152, 1280, 2304, 3328, 4352, 4480, 4640, 5664
LWC = -0.6065306597126334


def mixer(nc, s, D, W_IN, ps, uT, att_sl, rwo_sl, vcol, vecs, vecs2, VROW, C, dbg, sq, dbg_t, REAL, hsl, hall, hspill):
    ident, ones_b, perm_b, bo_b, bo64_b = C["ident"], C["ones_b"], C["perm_b"], C["bo_b"], C["bo64_b"]
    mask4, maskl, mask3, mask3f, resetm = C["mask4"], C["maskl"], C["mask3"], C["mask3f"], C["resetm"]
    RT = [(4, 0, 128)] + list(REAL)
    MU = VROW["mu"]

    def mucol(j):
        return vecs[:, MU + j:MU + j + 1]

    def wchunk_src(col0, ncols):
        return W_IN[:, col0:col0 + ncols].rearrange("(k p) f -> p k f", p=128)

    def proj(wt, wtok, t0, n, pb, m=128):
        for k in range(KC):
            _mm(s, ps[pb][0:m, 0:n], wt[:, k, 0:m], uT[:, k, t0:t0 + n], k == 0, k == KC - 1, r=[wtok], w=[("ps", pb)])

    with contextlib.ExitStack() as sm:
        def sbm(name, shape, dt):
            return sm.enter_context(_sbt(nc, name, list(shape), dt))

        la = sbm("la", [128, T], BF16)
        lg1 = sbm("lg1", [128, T], BF16)
        lg2 = sbm("lg2", [128, T], BF16)
        identb = sbm("identb", [128, 128], BF16)
        s.op("act", lambda e: e.activation(out=identb[:], in_=ident[:], func=AF.Copy), w=["identb"])

        import os
        with contextlib.ExitStack() as s1:
            wl = [s1.enter_context(_sbt(nc, "wl%d" % i, [128, KC, 128], BF16)) for i in range(3)]
            pt = s1.enter_context(_sbt(nc, "pt1", [128, 513], F32))
            dtmp = s1.enter_context(_sbt(nc, "dtmp1", [128, 512], F32))
            shf = s1.enter_context(_sbt(nc, "shf1", [128, 512], F32))
            specs = [(WD0, 128, 24, "la"), (GD0, 128, 25, "lg1"), (GD0 + 128, 32, 26, "lg2")]
            s.op("dve", lambda e: e.memset(wl[2][:, :, :], 0.0), w=[("wl", 2)])
            for i, (col0, m, muj, nm) in enumerate(specs):
                s.dma("pool", wl[i][:, :, 0:m], wchunk_src(col0, m), w=[("wl", i)], key=("wl", i))
            if os.environ.get("K_SKIPM1"):
                specs = []
            for i, (col0, m, muj, nm) in enumerate(specs):
                m = 128
                s.op("dve", lambda e: e.memset(pt[:, 0:1], 0.0), r=["pt"], w=["pt"])
                for (ti, t0, n) in RT:
                    pb = i % 2
                    proj(wl[i], ("wl", i), t0, n, pb, m)
                    s.op("act", lambda e: e.activation(out=pt[0:m, 1:n + 1], in_=ps[pb][0:m, 0:n], func=AF.Copy),
                         r=[("ps", pb), "pt"], w=["pt"])
                    s.op("dve", lambda e: e.tensor_tensor(out=dtmp[0:m, 0:n], in0=pt[0:m, 0:n], in1=pt[0:m, 1:n + 1], op=ALU.subtract),
                         r=["pt"], w=["dtmp"])
                    s.op("dve", lambda e: e.scalar_tensor_tensor(out=shf[0:m, 0:n], in0=dtmp[0:m, 0:n], scalar=vecs[0:m, MU + muj:MU + muj + 1],
                                                               in1=pt[0:m, 1:n + 1], op0=ALU.mult, op1=ALU.add),
                         r=["dtmp", "pt"], w=["shf"])
                    s.op("dve", lambda e: e.tensor_copy(out=pt[0:m, 0:1], in_=pt[0:m, n:n + 1]), r=["pt", "shf"], w=["pt"])
                    if nm == "la":
                        s.op("act", lambda e: e.activation(out=la[0:64, t0:t0 + n], in_=shf[0:64, 0:n], func=AF.Tanh), r=["shf"], w=["la"])
                        s.op("act", lambda e: e.activation(out=la[64:128, t0:t0 + n], in_=shf[64:128, 0:n], func=AF.Copy), r=["shf"], w=["la"])
                    elif nm == "lg1":
                        s.op("act", lambda e: e.activation(out=lg1[:, t0:t0 + n], in_=shf[:, 0:n], func=AF.Sigmoid), r=["shf"], w=["lg1"])
                    else:
                        s.op("act", lambda e: e.activation(out=lg2[:, t0:t0 + n], in_=shf[:, 0:n], func=AF.Sigmoid), r=["shf"], w=["lg2"])
            s.barrier()

        if dbg == "M1":
            return
        with contextlib.ExitStack() as s2:
            def sb2(name, shape, dt):
                return s2.enter_context(_sbt(nc, name, list(shape), dt))
            t1 = sb2("ropet1", [128, 512], F32)
            t2 = sb2("ropet2", [128, 512], F32)
            cosT = sb2("cosT", [128, T], F32)
            sinT = sb2("sinT", [128, T], F32)
            kTd = [sb2("kTd%d" % g, [128, T], BF16) for g in range(2)]
            Vt = sb2("Vt", [128, NBLK, 128], BF16)
            qT = [sb2("qT%d" % i, [128, SEQ], BF16) for i in range(2)]
            wq = [sb2("wq%d" % i, [128, KC, 128], BF16) for i in range(3)]
            pre = [sb2("pre%d" % i, [128, 512], BF16) for i in range(2)]
            Eb = [sb2("Eb%d" % i, [128, 384], BF16) for i in range(4)]
            rd = sb2("rd", [128, 512], F32)
            print("SBUF remaining at M2:", nc.sbuf_bytes_remaining)
            s.dma("sp", cosT[:], D["c_cos"][:, :], w=["cos"], key="cos")
            s.dma("sp", sinT[:], D["c_sin"][:, :], w=["sin"], key="sin")

            def rope_tile(wt, wtok, t0, n, dst, unit):
                import os
                RS = int(os.environ.get("K_RSTEP", "6"))
                pa, pb = 2 * (unit % 2), 2 * (unit % 2) + 1
                pr = pre[unit % 2]
                proj(wt, wtok, t0, n, pa)
                if RS < 2:
                    return
                s.op("act", lambda e: e.activation(out=pr[:, 0:n], in_=ps[pa][:, 0:n], func=AF.Copy), r=[("ps", pa)], w=[("pre", unit % 2)])
                if RS < 3:
                    return
                _mm(s, ps[pb][:, 0:n], perm_b[:], pr[:, 0:n], True, True, r=[("pre", unit % 2)], w=[("ps", pb)])
                if RS < 4:
                    return
                VV = os.environ.get("K_VAR", "")
                if VV == "1":
                    s.op("dve", lambda e: e.tensor_tensor(out=t1[:, 0:n], in0=cosT[:, t0:t0 + n], in1=t2[:, 0:n], op=ALU.mult),
                         r=[("ps", pa), "cos"], w=["t1"])
                elif VV == "2":
                    s.op("dve", lambda e: e.tensor_tensor(out=t1[:, 0:n], in0=t2[:, 0:n], in1=ps[pa][:, 0:n], op=ALU.mult),
                         r=[("ps", pa), "cos"], w=["t1"])
                elif VV == "3":
                    s.op("dve", lambda e: e.tensor_copy(out=t1[:, 0:n], in_=ps[pa][:, 0:n]),
                         r=[("ps", pa), "cos"], w=["t1"])
                elif VV == "4":
                    s.op("dve", lambda e: e.tensor_tensor(out=t1[:, 0:n], in0=cosT[:, t0:t0 + n], in1=ps[pa][:, 0:n], op=ALU.mult),
                         r=[("ps", pa), "cos", ("pre", unit % 2)], w=["t1"])
                else:
                    s.op("dve", lambda e: e.tensor_tensor(out=t1[:, 0:n], in0=cosT[:, t0:t0 + n], in1=ps[pa][:, 0:n], op=ALU.mult),
                         r=[("ps", pa), "cos"], w=["t1"])
                if RS < 5:
                    return
                s.op("dve", lambda e: e.tensor_tensor(out=t2[:, 0:n], in0=sinT[:, t0:t0 + n], in1=ps[pb][:, 0:n], op=ALU.mult),
                     r=[("ps", pb), "sin"], w=["t2"])
                if RS < 6:
                    return
                s.op("dve", lambda e: e.tensor_tensor(out=dst, in0=t1[:, 0:n], in1=t2[:, 0:n], op=ALU.add),
                     r=["t1", "t2"], w=[])

            unit = 0
            for g in range(2):
                wt = wq[g]
                for hf in range(2):
                    s.dma("pool", wt[:, :, hf * 64:(hf + 1) * 64], wchunk_src(K0 + g * 64, 64), w=[("wq", g)], key=("wq", g, hf))
            s.dma("pool", wq[2][:, :, :], wchunk_src(V0, 128), w=[("wq", 2)], key=("wq", 2))
            s.barrier()
            if dbg == "M2a":
                return
            import os
            NR = int(os.environ.get("K_NROPE", "100"))
            for g in range(2):
                for (ti, t0, n) in RT:
                    if unit >= NR or (os.environ.get("K_SKIP128") and n == 128):
                        continue
                    rope_tile(wq[g], ("wq", g), t0, n, kTd[g][:, t0:t0 + n], unit)
                    unit += 1
            if dbg == "M2b":
                s.barrier()
                return
            for blk in range(NBLK):
                pb = 4 + (blk // 4) % 2
                j = blk % 4
                for k in range(KC):
                    _mm(s, ps[pb][:, j * 128:(j + 1) * 128], uT[:, k, blk * 128:(blk + 1) * 128], wq[2][:, k, :], k == 0, k == KC - 1,
                        r=[("wq", 2)], w=[("ps", pb)])
                if j == 3 or blk == NBLK - 1:
                    b0 = blk - j
                    s.op("act", lambda e: e.activation(out=Vt[:, b0:blk + 1, :], in_=ps[pb][:, 0:(j + 1) * 128].rearrange("p (j t) -> p j t", j=j + 1),
                                                       func=AF.Copy), r=[("ps", pb)], w=["Vt"])
            s.barrier()

            if dbg == "M2":
                return
            for c in range(KC):
                g = c // 4
                ws = c % 3
                s.dma("pool", wq[ws][:, :, :], wchunk_src(Q0 + c * 128, 128), w=[("wq", ws)], key=("wq", ws))
                qt = qT[c % 2]
                for (ti, t0, n) in REAL:
                    rope_tile(wq[ws], ("wq", ws), t0, n, qt[:, t0 - 128:t0 - 128 + n], unit)
                    unit += 1
                s.barrier()
                for (ti, t0, n) in REAL:
                    ob, db = (4, 5) if ti % 2 == 0 else (6, 7)
                    for b4 in range(4):
                        nblk = 1 + 4 * ti + b4
                        qc = (nblk - 1) * 128
                        for hh in range(2):
                            R0_, R1_ = hh * 64, hh * 64 + 64
                            sbk = (2 * b4 + hh) % 4
                            E = Eb[sbk]
                            for j, kb in enumerate((0, nblk - 1, nblk)):
                                _mm(s, ps[sbk][:, j * 128:(j + 1) * 128], kTd[g][R0_:R1_, kb * 128:(kb + 1) * 128], qt[R0_:R1_, qc:qc + 128],
                                    True, True, r=[], w=[("ps", sbk)], inc=(j == 2))
                            s.op("act", lambda e: e.activation(out=E[:, :], in_=ps[sbk][:, 0:384], func=AF.Exp, scale=0.125),
                                 r=[("ps", sbk)], w=[("E", sbk)])
                            mk = mask3f if nblk == 1 else mask3
                            s.op("pool", lambda e: e.tensor_tensor(out=E[:, :], in0=E[:, :], in1=mk[:, :], op=ALU.mult),
                                 r=[("E", sbk)], w=[("E", sbk)])
                            for j, kb in enumerate((0, nblk - 1, nblk)):
                                _mm(s, ps[ob][R0_:R1_, b4 * 128:(b4 + 1) * 128], Vt[:, kb, g * 64:(g + 1) * 64], E[:, j * 128:(j + 1) * 128],
                                    j == 0, j == 2, r=[("E", sbk), "Vt"], w=[("ps", ob)])
                            for j in range(3):
                                _mm(s, ps[db][R0_:R1_, b4 * 128:(b4 + 1) * 128], ones_b[:, 0:64], E[:, j * 128:(j + 1) * 128],
                                    j == 0, j == 2, r=[("E", sbk)], w=[("ps", db)])
                    s.op("dve", lambda e: e.tensor_scalar_add(rd[:, :], ps[db][:, :], vecs2[:, 8 + c:9 + c]),
                         r=[("ps", db)], w=["rd"])
                    s.op("dve", lambda e: e.reciprocal(out=rd[:, :], in_=rd[:, :]), r=["rd"], w=["rd"])
                    s.op("dve", lambda e: e.tensor_tensor(out=att_sl(ti, c), in0=ps[ob][:, :], in1=rd[:, :], op=ALU.mult),
                         r=[("ps", ob), "rd"], w=[("att", ti, c)])
            s.barrier()
        if dbg in ("B", "BC") and sq == 0:
            for (ti, t0, n) in REAL:
                for c in range(KC):
                    s.dma("pool", dbg_t[0, :, c, t0:t0 + n], att_sl(ti, c), key="dbg")
            s.barrier()
            if dbg == "B":
                return
        if dbg == "K" and sq == 0:
            return

        with contextlib.ExitStack() as s4:
            def sb4(name, shape, dt):
                return s4.enter_context(_sbt(nc, name, list(shape), dt))
            wr1 = [sb4("wr_%d" % j, [128, KC, 128], BF16) for j in range(3)]
            wr = [wr1, wr1]
            lwt = [sb4("lwt%d" % i, [128, 128], BF16) for i in range(2)]
            g2a = [sb4("g2a%d" % i, [128, 128], BF16) for i in range(2)]
            g2b = [sb4("g2b%d" % i, [128, 128], BF16) for i in range(2)]
            for i in range(2):
                s.op("dve", lambda e: e.memset(g2b[i][:, :], 0.0), w=[("g2b", i)])
            ptx = [sb4("ptx%d" % i, [128, 513], F32) for i in range(3)]
            F = {nm: sb4("f_" + nm, [128, 512], F32) for nm in
                 ("r", "k", "v", "d", "lw", "a", "kk", "kmod", "b", "cw", "ep", "en", "epv", "ee", "g")}
            ALIAS = {"rn": "d", "t": "d", "kkn": "kk", "kh": "epv", "bh": "en", "y": "lw", "yc": "cw"}
            for a_, b_ in ALIAS.items():
                F[a_] = F[b_]
            sqk = sb4("sqk", [128, 512], BF16)
            AR = sb4("AR", [128, 4, 256], BF16)
            ktb = sb4("ktb", [128, 512], BF16)
            btb = sb4("btb", [128, 512], BF16)
            KBV = sb4("KBV", [128, 4, 384], BF16)
            MM = [sb4("MM%d" % hh, [128, 4, 512], BF16) for hh in range(2)]
            NN = sb4("NN", [128, 2, 512], BF16)
            MNb = sb4("MNb", [128, 4, 2, 512], BF16)
            Zb = sb4("Zb", [128, 4, 2, 256], BF16)
            RH = sb4("RH", [128, 128], BF16)
            UU = sb4("UU", [128, 128], BF16)
            Sst = sb4("Sst", [128, 64], F32)
            Sb = sb4("Sb", [128, 128], BF16)
            yb = sb4("yb", [128, 512], BF16)

            print("SBUF remaining at M4:", nc.sbuf_bytes_remaining)

            def v3(ap, nch):
                return ap.rearrange("p (j t) -> p j t", j=nch)

            M4S = int(os.environ.get("K_M4STEP", "0"))
            for c in range(KC):
                wsl = c % 2
                for j, col0 in enumerate((R0, RK0, RV0)):
                    s.dma("pool", wr[wsl][j][:, :, :], wchunk_src(col0 + c * 128, 128), w=[("wr", 0, j)], key=("wr", 0, j))
                s.dma("pool", lwt[wsl][0:64, :], D["w2"][0, :, c * 128:(c + 1) * 128], w=[("lwt", wsl)], key=("lwt", wsl, 0))
                s.dma("pool", lwt[wsl][64:128, :], D["a2"][0, :, c * 128:(c + 1) * 128], w=[("lwt", wsl)], key=("lwt", wsl, 1))
                s.dma("pool", g2a[wsl][:, :], D["g2"][0, 0:128, c * 128:(c + 1) * 128], w=[("g2a", wsl)], key=("g2a", wsl))
                s.dma("pool", g2b[wsl][0:32, :], D["g2"][0, 128:160, c * 128:(c + 1) * 128], w=[("g2b", wsl)], key=("g2b", wsl))
                s.op("dve", lambda e: e.memset(Sst[:], 0.0), r=["Sst"], w=["Sst"])
                s.op("dve", lambda e: e.memset(Sb[:], 0.0), r=["Sb"], w=["Sb"])
                for i in range(3):
                    s.op("dve", lambda e: e.memset(ptx[i][:, 0:1], 0.0), r=[("ptx", i)], w=[("ptx", i)])
                for (ti, t0, n) in RT:
                    nch = n // 128
                    for i, nm in enumerate(("r", "k", "v")):
                        proj(wr[wsl][i], ("wr", 0, i), t0, n, i)
                        p_ = ptx[i]
                        s.op("act", lambda e: e.activation(out=p_[:, 1:n + 1], in_=ps[i][:, 0:n], func=AF.Copy), r=[("ps", i), ("ptx", i)], w=[("ptx", i)])
                        s.op("dve", lambda e: e.tensor_tensor(out=F["d"][:, 0:n], in0=p_[:, 0:n], in1=p_[:, 1:n + 1], op=ALU.subtract),
                             r=[("ptx", i)], w=["f_d"])
                        s.op("dve", lambda e: e.scalar_tensor_tensor(out=F[nm][:, 0:n], in0=F["d"][:, 0:n], scalar=mucol(8 * i + c),
                                                                   in1=p_[:, 1:n + 1], op0=ALU.mult, op1=ALU.add),
                             r=["f_d", ("ptx", i)], w=["f_" + nm])
                        s.op("dve", lambda e: e.tensor_copy(out=p_[:, 0:1], in_=p_[:, n:n + 1]), r=[("ptx", i), "f_" + nm], w=[("ptx", i)])
                    if M4S == 1:
                        s.barrier()
                        return
                    _mm(s, ps[3][:, 0:n], lwt[wsl][0:64, :], la[0:64, t0:t0 + n], True, True, r=[("lwt", wsl)], w=[("ps", 3)])
                    s.op("act", lambda e: e.activation(out=F["lw"][:, 0:n], in_=ps[3][:, 0:n], func=AF.Sigmoid, bias=vcol("w0", c)),
                         r=[("ps", 3)], w=["f_lw"])
                    _mm(s, ps[4][:, 0:n], lwt[wsl][64:128, :], la[64:128, t0:t0 + n], True, True, r=[("lwt", wsl)], w=[("ps", 4)])
                    s.op("act", lambda e: e.activation(out=F["a"][:, 0:n], in_=ps[4][:, 0:n], func=AF.Sigmoid, bias=vcol("a0", c)),
                         r=[("ps", 4)], w=["f_a"])
                    _mm(s, ps[5][:, 0:n], g2a[wsl][:, :], lg1[:, t0:t0 + n], True, False, r=[("g2a", wsl)], w=[("ps", 5)], inc=False)
                    _mm(s, ps[5][:, 0:n], g2b[wsl][:, :], lg2[:, t0:t0 + n], False, True, r=[("g2b", wsl)], w=[("ps", 5)])
                    s.op("act", lambda e: e.activation(out=F["g"][:, 0:n], in_=ps[5][:, 0:n], func=AF.Copy), r=[("ps", 5)], w=["f_g"])
                    if M4S == 2:
                        s.barrier()
                        return
                    s.op("dve", lambda e: e.tensor_scalar_mul(F["kk"][:, 0:n], F["k"][:, 0:n], vcol("k_k", c)), r=["f_k"], w=["f_kk"])
                    s.op("act", lambda e: e.activation(out=sqk[:, 0:n], in_=F["kk"][:, 0:n], func=AF.Square), r=["f_kk"], w=["sqk"])
                    _mm(s, ps[6][:, 0:n], bo_b[:], sqk[:, 0:n], True, True, r=["sqk"], w=[("ps", 6)])
                    s.op("act", lambda e: e.activation(out=F["rn"][:, 0:n], in_=ps[6][:, 0:n], func=AF.Sqrt, bias=1e-24), r=[("ps", 6)], w=["f_d"])
                    s.op("dve", lambda e: e.reciprocal(out=F["rn"][:, 0:n], in_=F["rn"][:, 0:n]), r=["f_d"], w=["f_d"])
                    s.op("dve", lambda e: e.tensor_tensor(out=F["kkn"][:, 0:n], in0=F["kk"][:, 0:n], in1=F["rn"][:, 0:n], op=ALU.mult),
                         r=["f_kk", "f_d"], w=["f_kk"])
                    if M4S == 3:
                        s.barrier()
                        return
                    s.op("dve", lambda e: e.tensor_scalar(F["t"][:, 0:n], F["a"][:, 0:n], vcol("k_a", c), vecs2[:, c:c + 1], ALU.mult, ALU.add),
                         r=["f_a"], w=["f_d"])
                    s.op("dve", lambda e: e.tensor_tensor(out=F["kmod"][:, 0:n], in0=F["k"][:, 0:n], in1=F["t"][:, 0:n], op=ALU.mult),
                         r=["f_k", "f_d"], w=["f_kmod"])
                    s.op("pool", lambda e: e.tensor_tensor(out=F["b"][:, 0:n], in0=F["kkn"][:, 0:n], in1=F["a"][:, 0:n], op=ALU.mult),
                         r=["f_kk", "f_a"], w=["f_b"])
                    if M4S == 4:
                        s.barrier()
                        return
                    s.op("dve", lambda e: e.tensor_scalar_mul(F["lw"][:, 0:n], F["lw"][:, 0:n], LWC), r=["f_lw"], w=["f_lw"])
                    s.op("dve", lambda e: e.tensor_tensor_scan(out=F["cw"][:, 0:n], data0=resetm[:, 0:n], data1=F["lw"][:, 0:n], initial=0.0,
                                                             op0=ALU.mult, op1=ALU.add), r=["f_lw"], w=["f_cw"])
                    s.op("act", lambda e: e.activation(out=F["ep"][:, 0:n], in_=F["cw"][:, 0:n], func=AF.Exp), r=["f_cw"], w=["f_ep"])
                    s.op("act", lambda e: e.activation(out=F["en"][:, 0:n], in_=F["cw"][:, 0:n], func=AF.Exp, scale=-1.0), r=["f_cw"], w=["f_en"])
                    s.op("pool", lambda e: e.tensor_tensor(out=F["t"][:, 0:n], in0=F["cw"][:, 0:n], in1=F["lw"][:, 0:n], op=ALU.subtract),
                         r=["f_cw", "f_lw", "f_d"], w=["f_d"])
                    s.op("act", lambda e: e.activation(out=F["epv"][:, 0:n], in_=F["t"][:, 0:n], func=AF.Exp), r=["f_d"], w=["f_epv"])
                    for j in range(nch):
                        s.op("act", lambda e: e.activation(out=F["ee"][:, j * 128:(j + 1) * 128], in_=F["cw"][:, j * 128:(j + 1) * 128], func=AF.Exp,
                                                           scale=-1.0, bias=F["cw"][:, j * 128 + 127:j * 128 + 128]), r=["f_cw"], w=["f_ee"])
                    if M4S == 5:
                        s.barrier()
                        return
                    s.op("dve", lambda e: e.scalar_tensor_tensor(out=AR[:, 0:nch, 0:128], in0=v3(F["kkn"][:, 0:n], nch), scalar=-1.0,
                                                               in1=v3(F["epv"][:, 0:n], nch), op0=ALU.mult, op1=ALU.mult),
                         r=["f_kk", "f_epv", "AR"], w=["AR"])
                    s.op("dve", lambda e: e.tensor_tensor(out=AR[:, 0:nch, 128:256], in0=v3(F["r"][:, 0:n], nch), in1=v3(F["ep"][:, 0:n], nch), op=ALU.mult),
                         r=["f_r", "f_ep", "AR"], w=["AR"])
                    s.op("pool", lambda e: e.tensor_tensor(out=ktb[:, 0:n], in0=F["kmod"][:, 0:n], in1=F["en"][:, 0:n], op=ALU.mult),
                         r=["f_kmod", "f_en"], w=["ktb"])
                    s.op("pool", lambda e: e.tensor_tensor(out=btb[:, 0:n], in0=F["b"][:, 0:n], in1=F["en"][:, 0:n], op=ALU.mult),
                         r=["f_b", "f_en"], w=["btb"])
                    s.op("pool", lambda e: e.tensor_tensor(out=F["kh"][:, 0:n], in0=F["kmod"][:, 0:n], in1=F["ee"][:, 0:n], op=ALU.mult),
                         r=["f_kmod", "f_ee"], w=["f_epv"])
                    s.op("pool", lambda e: e.tensor_tensor(out=F["bh"][:, 0:n], in0=F["b"][:, 0:n], in1=F["ee"][:, 0:n], op=ALU.mult),
                         r=["f_b", "f_ee"], w=["f_en"])
                    if M4S == 6:
                        s.barrier()
                        return
                    for j in range(nch):
                        pb = 6 + j % 2
                        cs = slice(j * 128, (j + 1) * 128)
                        for q_, nm in enumerate(("kh", "bh", "v")):
                            s.op("pe", lambda e: e.transpose(ps[pb][:, q_ * 128:(q_ + 1) * 128], F[nm][:, cs], ident[:]),
                                 r=["f_" + {"kh": "epv", "bh": "en", "v": "v"}[nm]], w=[("ps", pb)], inc=(q_ == 2))
                        s.op("act", lambda e: e.activation(out=KBV[:, j, :], in_=ps[pb][:, 0:384], func=AF.Copy), r=[("ps", pb), "KBV"], w=["KBV"])
                    if M4S == 7:
                        s.barrier()
                        return
                    for j in range(nch):
                        cs = slice(j * 128, (j + 1) * 128)
                        for hh in range(2):
                            R = slice(hh * 64, hh * 64 + 64)
                            px = (2 * j + hh) % 4
                            _mm(s, ps[px][:, 0:256], btb[R, cs], AR[R, j, :], True, True, r=["btb", "AR"], w=[("ps", px)], inc=False)
                            _mm(s, ps[px][:, 256:512], ktb[R, cs], AR[R, j, :], True, True, r=["ktb", "AR"], w=[("ps", px)])
                            s.op("dve", lambda e: e.tensor_tensor(out=MM[hh][:, j, :], in0=ps[px][:, :], in1=mask4[:, :], op=ALU.mult),
                                 r=[("ps", px), ("MM", hh)], w=[("MM", hh)])
                            _mm(s, ps[4 + hh][:, cs], AR[R, j, 0:128], btb[R, cs], True, True, r=["btb", "AR"], w=[("ps", 4 + hh)])
                    for hh in range(2):
                        s.op("dve", lambda e: e.tensor_tensor(out=NN[:, hh, 0:n], in0=ps[4 + hh][:, 0:n], in1=maskl[:, 0:n], op=ALU.mult),
                             r=[("ps", 4 + hh), "NN"], w=["NN"])
                    if M4S == 8:
                        s.barrier()
                        return
                    Mc = {}
                    Nc = {}
                    Zc = {}
                    for j in range(nch):
                        for hh in range(2):
                            Mc[j, hh] = MM[hh][:, j, 0:128]
                            Nc[j, hh] = NN[:, hh, j * 128:(j + 1) * 128]
                            s.op("pool", lambda e: e.tensor_tensor(out=Zb[:, j, 0, hh * 128:(hh + 1) * 128], in0=MM[hh][:, j, 0:128], in1=identb[:, :], op=ALU.add),
                                 r=[("MM", hh), "identb", ("Zb", j)], w=[("Zb", j)])
                            Zc[j, hh] = Zb[:, j, 0, hh * 128:(hh + 1) * 128]
                    for p in range(1, 7):
                        pp = p % 2
                        for j in range(nch):
                            for hh in range(2):
                                if p < 6:
                                    _mm(s, ps[j][:, hh * 128:(hh + 1) * 128], Nc[j, hh], Mc[j, hh], True, True,
                                        r=[("MM", hh), "NN", ("MNb", j)], w=[("ps", j)], inc=False)
                                _mm(s, ps[j][:, 256 + hh * 128:256 + (hh + 1) * 128], Mc[j, hh], Nc[j, hh], True, True,
                                    r=[("MM", hh), "NN", ("MNb", j)], w=[("ps", j)], inc=(hh == 1))
                        for j in range(nch):
                            lo = 0 if p < 6 else 256
                            s.op("act", lambda e: e.activation(out=MNb[:, j, pp, lo:512], in_=ps[j][:, lo:512], func=AF.Copy),
                                 r=[("ps", j), ("MNb", j)], w=[("MNb", j)])
                            for hh in range(2):
                                Mc[j, hh] = MNb[:, j, pp, hh * 128:(hh + 1) * 128]
                                Nc[j, hh] = MNb[:, j, pp, 256 + hh * 128:256 + (hh + 1) * 128]
                        for j in range(nch):
                            pz = 4 + j // 2
                            zo = (j % 2) * 256
                            for hh in range(2):
                                _mm(s, ps[pz][:, zo + hh * 128:zo + (hh + 1) * 128], Nc[j, hh], Zc[j, hh], True, True,
                                    r=[("MNb", j), ("Zb", j)], w=[("ps", pz)], inc=(hh == 1))
                        for j in range(nch):
                            pz = 4 + j // 2
                            zo = (j % 2) * 256
                            s.op("dve", lambda e: e.tensor_tensor(out=Zb[:, j, pp, :], in0=ps[pz][:, zo:zo + 256], in1=Zb[:, j, 1 - pp, :], op=ALU.add),
                                 r=[("ps", pz), ("Zb", j)], w=[("Zb", j)])
                            for hh in range(2):
                                Zc[j, hh] = Zb[:, j, pp, hh * 128:(hh + 1) * 128]
                    if M4S == 9:
                        s.barrier()
                        return
                    for j in range(nch):
                        cs = slice(j * 128, (j + 1) * 128)
                        Vh = [KBV[:, j, 256 + hh * 64:256 + (hh + 1) * 64] for hh in range(2)]
                        _mm(s, ps[6][:, 0:128], AR[:, j, 0:128], Sb[:, :], True, False, r=["AR", "Sb"], w=[("ps", 6)], inc=False)
                        for hh in range(2):
                            _mm(s, ps[6][:, hh * 64:(hh + 1) * 64], MM[hh][:, j, 256:384], Vh[hh], False, hh == 1, r=[("MM", hh), "KBV"], w=[("ps", 6)],
                                inc=(hh == 1))
                        s.op("act", lambda e: e.activation(out=RH[:, :], in_=ps[6][:, 0:128], func=AF.Copy), r=[("ps", 6), "RH"], w=["RH"])
                        for hh in range(2):
                            _mm(s, ps[6][:, 128 + hh * 64:128 + (hh + 1) * 64], Zc[j, hh], RH[:, hh * 64:(hh + 1) * 64], True, True,
                                r=[("Zb", j), "RH"], w=[("ps", 6)], inc=(hh == 1))
                        s.op("act", lambda e: e.activation(out=UU[:, :], in_=ps[6][:, 128:256], func=AF.Copy), r=[("ps", 6), "UU"], w=["UU"])
                        if ti != 4:
                            _mm(s, ps[7][:, cs], Sb[:, :], AR[:, j, 128:256], True, False, r=["Sb", "AR"], w=[("ps", 7)], inc=False)
                            for hh in range(2):
                                R = slice(hh * 64, hh * 64 + 64)
                                _mm(s, ps[7][R, cs], UU[:, hh * 64:(hh + 1) * 64], MM[hh][:, j, 128:256], False, False, r=["UU", ("MM", hh)], w=[("ps", 7)], inc=False)
                                _mm(s, ps[7][R, cs], Vh[hh], MM[hh][:, j, 384:512], False, True, r=["KBV", ("MM", hh)], w=[("ps", 7)], inc=(hh == 1))
                        for hh in range(2):
                            R = slice(hh * 64, hh * 64 + 64)
                            _mm(s, ps[6][R, 256:320], KBV[:, j, 128 + hh * 64:128 + (hh + 1) * 64], UU[:, hh * 64:(hh + 1) * 64], True, False,
                                r=["KBV", "UU"], w=[("ps", 6)], inc=False)
                            _mm(s, ps[6][R, 256:320], KBV[:, j, hh * 64:(hh + 1) * 64], Vh[hh], False, True, r=["KBV"], w=[("ps", 6)], inc=(hh == 1))
                        s.op("dve", lambda e: e.scalar_tensor_tensor(out=Sst[:, :], in0=Sst[:, :], scalar=F["ep"][:, j * 128 + 127:j * 128 + 128],
                                                                   in1=ps[6][:, 256:320], op0=ALU.mult, op1=ALU.add),
                             r=[("ps", 6), "Sst", "f_ep"], w=["Sst"])
                        for hh in range(2):
                            R = slice(hh * 64, hh * 64 + 64)
                            s.op("act", lambda e: e.activation(out=Sb[R, hh * 64:(hh + 1) * 64], in_=Sst[R, :], func=AF.Copy), r=["Sst", "Sb"], w=["Sb"])
                    if M4S == 10:
                        s.barrier()
                        return
                    if ti == 4:
                        continue
                    s.op("act", lambda e: e.activation(out=F["y"][:, 0:n], in_=ps[7][:, 0:n], func=AF.Copy), r=[("ps", 7)], w=["f_lw"])
                    s.op("act", lambda e: e.activation(out=yb[:, 0:n], in_=ps[7][:, 0:n], func=AF.Copy), r=[("ps", 7), "yb"], w=["yb"])
                    _mm(s, ps[0][:, 0:n], bo64_b[:], yb[:, 0:n], True, True, r=["yb"], w=[("ps", 0)])
                    s.op("dve", lambda e: e.tensor_tensor(out=F["yc"][:, 0:n], in0=F["y"][:, 0:n], in1=ps[0][:, 0:n], op=ALU.subtract),
                         r=["f_lw", ("ps", 0)], w=["f_cw"])
                    s.op("act", lambda e: e.activation(out=yb[:, 0:n], in_=F["yc"][:, 0:n], func=AF.Square), r=["f_cw", "yb"], w=["yb"])
                    _mm(s, ps[1][:, 0:n], bo64_b[:], yb[:, 0:n], True, True, r=["yb"], w=[("ps", 1)])
                    s.op("act", lambda e: e.activation(out=F["y"][:, 0:n], in_=ps[1][:, 0:n], func=AF.Sqrt, bias=GN_EPS), r=[("ps", 1), "f_lw"], w=["f_lw"])
                    s.op("dve", lambda e: e.reciprocal(out=F["y"][:, 0:n], in_=F["y"][:, 0:n]), r=["f_lw"], w=["f_lw"])
                    s.op("dve", lambda e: e.tensor_tensor(out=F["yc"][:, 0:n], in0=F["yc"][:, 0:n], in1=F["y"][:, 0:n], op=ALU.mult),
                         r=["f_cw", "f_lw"], w=["f_cw"])
                    s.op("dve", lambda e: e.tensor_scalar(F["yc"][:, 0:n], F["yc"][:, 0:n], vcol("lnx_w", c), vcol("lnx_b", c), ALU.mult, ALU.add),
                         r=["f_cw"], w=["f_cw"])
                    s.op("pool", lambda e: e.tensor_tensor(out=F["t"][:, 0:n], in0=F["r"][:, 0:n], in1=F["kmod"][:, 0:n], op=ALU.mult),
                         r=["f_r", "f_kmod", "f_d"], w=["f_d"])
                    s.op("dve", lambda e: e.tensor_scalar_mul(yb[:, 0:n], F["t"][:, 0:n], vcol("r_k", c)), r=["f_d", "yb"], w=["yb"])
                    _mm(s, ps[2][:, 0:n], bo_b[:], yb[:, 0:n], True, True, r=["yb"], w=[("ps", 2)])
                    s.op("dve", lambda e: e.tensor_tensor(out=F["y"][:, 0:n], in0=ps[2][:, 0:n], in1=F["v"][:, 0:n], op=ALU.mult),
                         r=[("ps", 2), "f_v", "f_lw"], w=["f_lw"])
                    s.op("dve", lambda e: e.tensor_tensor(out=F["yc"][:, 0:n], in0=F["yc"][:, 0:n], in1=F["y"][:, 0:n], op=ALU.add),
                         r=["f_cw", "f_lw"], w=["f_cw"])
                    s.op("dve", lambda e: e.tensor_tensor(out=rwo_sl(ti, c), in0=F["yc"][:, 0:n], in1=F["g"][:, 0:n], op=ALU.mult),
                         r=["f_cw", "f_g"], w=[("rwo", ti, c)])
            s.barrier()
        if dbg in ("C", "BC") and sq == 0:
            for (ti, t0, n) in REAL:
                for c in range(KC):
                    s.dma("pool", dbg_t[1, :, c, t0:t0 + n], rwo_sl(ti, c), key="dbg")
            s.barrier()
            return

        with contextlib.ExitStack() as s5:
            def sb5(name, shape, dt):
                return s5.enter_context(_sbt(nc, name, list(shape), dt))
            mg = sb5("mg", [128, KC, SEQ], BF16)
            w5 = [[sb5("w5_%d_%d" % (i, j), [128, KC, 128], BF16) for j in range(4)] for i in range(2)]
            sga = [sb5("sga%d" % i, [128, 512], F32) for i in range(2)]
            sgr = [sb5("sgr%d" % i, [128, 512], F32) for i in range(2)]
            WA, WR, WO = D["w_attn_branch"][0], D["w_rwkv_branch"][0], D["w_out"][0]
            unit = 0
            for oc in range(KC):
                wsl = oc % 2
                cs = slice(oc * 128, (oc + 1) * 128)
                srcs = [WA[:, cs].rearrange("(k p) f -> p k f", p=128), WR[:, cs].rearrange("(k p) f -> p k f", p=128),
                        wchunk_src(GA0 + oc * 128, 128), wchunk_src(GR0 + oc * 128, 128)]
                for j in range(4):
                    s.dma("pool", w5[wsl][j][:, :, :], srcs[j], w=[("w5", wsl, j)], key=("w5", wsl, j))
                for (ti, t0, n) in REAL:
                    b0 = 4 * (unit % 2)
                    u2 = unit % 2
                    for k in range(KC):
                        _mm(s, ps[b0][:, :], w5[wsl][0][:, k, :], att_sl(ti, k), k == 0, k == KC - 1, r=[("w5", wsl, 0), ("att", ti, k)], w=[("ps", b0)])
                    for k in range(KC):
                        _mm(s, ps[b0 + 1][:, :], w5[wsl][1][:, k, :], rwo_sl(ti, k), k == 0, k == KC - 1, r=[("w5", wsl, 1), ("rwo", ti, k)], w=[("ps", b0 + 1)])
                    proj(w5[wsl][2], ("w5", wsl, 2), t0, n, b0 + 2)
                    proj(w5[wsl][3], ("w5", wsl, 3), t0, n, b0 + 3)
                    s.op("act", lambda e: e.activation(out=sga[u2][:, :], in_=ps[b0 + 2][:, :], func=AF.Sigmoid), r=[("ps", b0 + 2)], w=[("sga", u2)])
                    s.op("act", lambda e: e.activation(out=sgr[u2][:, :], in_=ps[b0 + 3][:, :], func=AF.Sigmoid), r=[("ps", b0 + 3)], w=[("sgr", u2)])
                    s.op("dve", lambda e: e.tensor_tensor(out=sga[u2][:, :], in0=ps[b0][:, :], in1=sga[u2][:, :], op=ALU.mult),
                         r=[("ps", b0), ("sga", u2)], w=[("sga", u2)])
                    s.op("dve", lambda e: e.tensor_tensor(out=sgr[u2][:, :], in0=ps[b0 + 1][:, :], in1=sgr[u2][:, :], op=ALU.mult),
                         r=[("ps", b0 + 1), ("sgr", u2)], w=[("sgr", u2)])
                    s.op("pool", lambda e: e.tensor_tensor(out=mg[:, oc, ti * 512:(ti + 1) * 512], in0=sga[u2][:, :], in1=sgr[u2][:, :], op=ALU.add),
                         r=[("sga", u2), ("sgr", u2)], w=[("mg", ti, oc)])
                    unit += 1
            s.barrier()
            for (ti, t0, n) in REAL:
                s.dma("sp", hall(ti), hspill[:, ti * KC * 512:(ti + 1) * KC * 512].rearrange("p (c t) -> p c t", c=KC),
                      w=[("hT", ti, cc) for cc in range(KC)], key=("hre", ti))
            for oc in range(KC):
                wsl = oc % 2
                s.dma("pool", w5[wsl][0][:, :, :], WO[:, oc * 128:(oc + 1) * 128].rearrange("(k p) f -> p k f", p=128),
                      w=[("w5", wsl, 0)], key=("w5", wsl, 0))
                for (ti, t0, n) in REAL:
                    pb = unit % 4
                    for k in range(KC):
                        _mm(s, ps[pb][:, :], w5[wsl][0][:, k, :], mg[:, k, ti * 512:(ti + 1) * 512], k == 0, k == KC - 1,
                            r=[("w5", wsl, 0), ("mg", ti, k)], w=[("ps", pb)])
                    s.op("dve", lambda e: e.tensor_tensor(out=hsl(ti, oc), in0=ps[pb][:, :], in1=hsl(ti, oc), op=ALU.add),
                         r=[("ps", pb), ("hT", ti, oc)], w=[("hT", ti, oc)])
                    unit += 1
            s.barrier()

def host_consts():
    c = {}
    c["c_ident"] = np.eye(128, dtype=np.float32)
    p = np.arange(128)
    perm = np.zeros((128, 128), np.float32)
    partner = (p // 64) * 64 + ((p % 64) + 32) % 64
    perm[partner, p] = 1.0
    c["c_perm"] = perm
    i = np.arange(128)[:, None]
    t = np.arange(128)[None, :]
    su = (t > i).astype(np.float32)
    iu = (t >= i).astype(np.float32)
    c["c_mask4"] = np.concatenate([su, iu, su, iu], axis=1)
    sl = (t < i).astype(np.float32)
    c["c_maskl"] = np.concatenate([sl, sl, sl, sl], axis=1)
    meta = np.broadcast_to((i >= 112), (128, 128)).astype(np.float32)
    prev = (i > t).astype(np.float32)
    cur = (i <= t).astype(np.float32)
    c["c_mask3"] = np.concatenate([meta, prev, cur], axis=1)
    c["c_mask3f"] = np.concatenate([meta, np.zeros_like(prev), cur], axis=1)
    bo = np.zeros((128, 128), np.float32)
    bo[:64, :64] = 1.0
    bo[64:, 64:] = 1.0
    c["c_bo"] = bo
    rs = np.ones((128, 512), np.float32)
    rs[:, 0::128] = 0.0
    c["c_reset"] = rs
    half = 32
    inv = (10000.0 ** (-np.arange(half, dtype=np.float32) / half)).astype(np.float32)
    pos = (np.arange(T) - 112).astype(np.float32)
    ang = pos[None, :] * inv[(p % 32)][:, None]
    c["c_cos"] = np.cos(ang).astype(np.float32)
    sgn = np.where((p % 64) < 32, -1.0, 1.0).astype(np.float32)[:, None]
    c["c_sin"] = (np.sin(ang) * sgn).astype(np.float32)
    return c


CONST_SHAPES = {"c_ident": (128, 128), "c_perm": (128, 128), "c_mask4": (128, 512), "c_maskl": (128, 512),
                "c_mask3": (128, 384), "c_mask3f": (128, 384), "c_bo": (128, 128), "c_reset": (128, 512),
                "c_cos": (128, T), "c_sin": (128, T)}


def build_nc(dbg=None, nseq=SPC, skip=False):
    nc = bass.Bass("TRN2", target_bir_lowering=False)
    D = {}

    def inp(name, shape):
        D[name] = nc.dram_tensor(name, list(shape), F32, kind="ExternalInput").ap()

    inp("x", (SPC, SEQ, DM))
    inp("meta_tokens", (16, DM))
    inp("norm_ffn1", (1, DM))
    inp("ffn1_w_in", (1, DM, 2 * FF))
    inp("ffn1_w_out", (1, FF, DM))
    inp("norm_mix", (1, DM))
    inp("w_in", (1, DM, INC))
    inp("rwkv_mu", (1, RWC))
    inp("sinks", (1, NQH))
    inp("w0", (1, 1024))
    inp("w2", (1, LW, 1024))
    inp("a0", (1, 1024))
    inp("a2", (1, LA, 1024))
    inp("g2", (1, LG, 1024))
    inp("k_k", (1, 1024))
    inp("k_a", (1, 1024))
    inp("r_k", (1, 16, 64))
    inp("lnx_w", (1, 1024))
    inp("lnx_b", (1, 1024))
    inp("w_attn_branch", (1, 1024, DM))
    inp("w_rwkv_branch", (1, 1024, DM))
    inp("w_out", (1, DM, DM))
    inp("norm_ffn2", (1, DM))
    inp("ffn2_w_in", (1, DM, 2 * FF))
    inp("ffn2_w_out", (1, FF, DM))
    inp("norm_final", (DM,))
    for k, shp in CONST_SHAPES.items():
        inp(k, shp)
    out = nc.dram_tensor("out", [SPC, SEQ, DM], F32, kind="ExternalOutput").ap()
    hspill = nc.dram_tensor("hspill", [128, 4 * KC * 512], F32, kind="Internal").ap()
    if dbg:
        dbg_t = nc.dram_tensor("dbg", [2, 128, KC, T], F32, kind="ExternalOutput").ap()

    W_IN = D["w_in"][0]

    with contextlib.ExitStack() as st:
        s = Sched(nc, st)

        def sb(name, shape, dt):
            return st.enter_context(_sbt(nc, name, list(shape), dt))

        ident = sb("ident", [128, 128], F32)
        ones_b = sb("ones_b", [128, 128], BF16)
        perm_b = sb("perm_b", [128, 128], BF16)
        bo_b = sb("bo_b", [128, 128], BF16)
        bo64_b = sb("bo64_b", [128, 128], BF16)
        mask4 = sb("mask4", [128, 512], BF16)
        maskl = sb("maskl", [128, 512], BF16)
        mask3 = sb("mask3", [128, 384], BF16)
        mask3f = sb("mask3f", [128, 384], BF16)
        resetm = sb("resetm", [128, 512], F32)
        vecs = sb("vecs", [128, 128], F32)
        vecs2 = sb("vecs2", [128, 32], F32)
        vstage = sb("vstage", [128, 128], F32)
        ps = [st.enter_context(nc.psum_tensor("ps%d" % i, [128, 512], F32)) for i in range(8)]

        s.dma("sp", ident[:], D["c_ident"][:, :], key="c0")
        s.dma("sp", resetm[:], D["c_reset"][:, :], key="c0")
        s.dma("pool", perm_b[:], D["c_perm"][:, :], key="c1")
        s.dma("pool", bo_b[:], D["c_bo"][:, :], key="c1")
        s.dma("pool", mask4[:], D["c_mask4"][:, :], key="c1")
        s.dma("pool", maskl[:], D["c_maskl"][:, :], key="c1")
        s.dma("pool", mask3[:], D["c_mask3"][:, :], key="c1")
        s.dma("pool", mask3f[:], D["c_mask3f"][:, :], key="c1")
        s.op("dve", lambda e: e.memset(ones_b[:], 1.0))
        s.op("dve", lambda e: e.memset(vstage[:], 0.0))
        s.barrier()
        VROW = {}
        row = 0
        for nm in ("norm_ffn1", "norm_mix", "norm_ffn2", "norm_final", "w0", "a0", "k_k", "k_a",
                   "r_k", "lnx_w", "lnx_b"):
            ap = D[nm]
            if nm == "norm_final":
                src = ap.rearrange("(c p) -> c p", p=128)
            elif nm == "r_k":
                src = ap[0].rearrange("(c h) k -> c (h k)", h=2)
            else:
                src = ap[0].rearrange("(c p) -> c p", p=128)
            s.dma("sp", vstage[row:row + 8, :], src, key="c0")
            VROW[nm] = row
            row += 8
        mu = D["rwkv_mu"][0]
        s.dma("sp", vstage[row:row + 26, :], mu[0:26 * 128].rearrange("(c p) -> c p", p=128), key="c0")
        s.dma("sp", vstage[row + 26:row + 27, 0:32], mu[26 * 128:RWC].rearrange("(c p) -> c p", p=32), key="c0")
        VROW["mu"] = row
        row += 27
        assert row <= 128
        sk = D["sinks"][0].rearrange("(c h) -> h c", h=2)
        import os
        for hf in range(2):
            if os.environ.get("K_SKIPC"):
                continue
            s.dma("sp", vecs2[hf * 64:(hf + 1) * 64, 8:16], sk[hf:hf + 1, :].broadcast_to([64, 8]), key="c0",
                  allow_slow_non_contiguous=True)
        s.barrier()
        s.op("pe", lambda e: e.transpose(ps[0][:, 0:128], vstage[:], ident[:]))
        s.barrier()
        s.op("dve", lambda e: e.tensor_copy(out=vecs[:], in_=ps[0][:, 0:128]))
        s.op("act", lambda e: e.activation(out=vecs2[:, 8:16], in_=vecs2[:, 8:16], func=AF.Exp))
        s.op("act", lambda e: e.mul(out=bo64_b[:], in_=bo_b[:], mul=1.0 / 64))
        s.barrier()

        def vcol(nm, c):
            j = VROW[nm] + c
            return vecs[:, j:j + 1]

        s.op("dve", lambda e: e.tensor_scalar(vecs2[:, 0:8], vecs[:, VROW["k_a"]:VROW["k_a"] + 8], -1.0, 1.0, ALU.mult, ALU.add))
        s.barrier()

        hTf = sb("hTf", [128, 4 * KC * 512 + KC * 128], F32)
        hTb = hTf.bitcast(BF16)
        xnT = sb("xnT", [128, KC, T], BF16)

        REAL = [(i, 128 + 512 * i, 512) for i in range(4)]
        META = (4, 0, 128)
        ALLT = REAL + [META]

        def hsl(ti, c, a=0, b=None):
            if ti == 4:
                base = 4 * KC * 512 + c * 128
                n = 128
            else:
                base = (ti * KC + c) * 512
                n = 512
            if b is None:
                b = n
            return hTf[:, base + a:base + b]

        def hall(ti):
            if ti == 4:
                return hTf[:, 4 * KC * 512:4 * KC * 512 + KC * 128].rearrange("p (c t) -> p c t", c=KC)
            return hTf[:, ti * KC * 512:(ti + 1) * KC * 512].rearrange("p (c t) -> p c t", c=KC)

        def att_sl(ti, c, a=0, b=512):
            base = ti * 8192 + c * 512
            return hTb[:, base + a:base + b]

        def rwo_sl(ti, c, a=0, b=512):
            base = ti * 8192 + 4096 + c * 512
            return hTb[:, base + a:base + b]

        def hT_toks(ti):
            return [("hT", ti, c) for c in range(KC)]

        for sq in range(nseq):
            with contextlib.ExitStack() as st1:
                xs = [st1.enter_context(_sbt(nc, "xs%d" % i, [128, DM], F32)) for i in range(2)]
                import os
                for n in range(NBLK):
                    if os.environ.get("K_SKIPA"):
                        continue
                    slot = n % 2
                    xt = xs[slot]
                    if n == 0:
                        s.op("dve", lambda e: e.memset(xt[:], 0.0), w=[("xs", slot)])
                        s.dma("sp", xt[112:128, :], D["meta_tokens"][:, :], w=[("xs", slot)], key=("xs", slot))
                        ti, off = 4, 0
                    else:
                        s.dma("sp", xt[:], D["x"][sq, (n - 1) * 128:n * 128, :], w=[("xs", slot)], key=("xs", slot))
                        ti, off = (n - 1) // 4, ((n - 1) % 4) * 128
                    for half in range(2):
                        pb = (2 * n + half) % 4
                        for j in range(4):
                            c = half * 4 + j
                            s.op("pe", lambda e: e.transpose(ps[pb][:, j * 128:(j + 1) * 128], xt[:, c * 128:(c + 1) * 128], ident[:]),
                                 r=[("xs", slot)], w=[("ps", pb)], inc=(j == 3))
                        s.op("act", lambda e: e.activation(out=hall(ti)[:, half * 4:half * 4 + 4, off:off + 128],
                                                           in_=ps[pb][:].rearrange("p (j t) -> p j t", j=4), func=AF.Copy),
                             r=[("ps", pb)], w=[("hT", ti, half * 4 + jj) for jj in range(4)])
                s.barrier()

            def rmsnorm_tiles(gname, tiles, bufs, write):
                sqb, rstd = bufs
                for (ti, t0, n) in tiles:
                    s.op("act", lambda e: e.activation(out=sqb[:, :, 0:n], in_=hall(ti), func=AF.Square),
                         r=hT_toks(ti), w=["sqb"])
                    for c in range(KC):
                        _mm(s, ps[0][:, 0:n], ones_b[:], sqb[:, c, 0:n], c == 0, c == KC - 1, r=["sqb"], w=[("ps", 0)])
                    s.op("act", lambda e: e.activation(out=rstd[:, 0:n], in_=ps[0][:, 0:n], func=AF.Sqrt, bias=RMS_EPS, scale=1.0 / DM),
                         r=[("ps", 0)], w=["rstd"])
                    s.op("dve", lambda e: e.reciprocal(out=rstd[:, 0:n], in_=rstd[:, 0:n]), r=["rstd"], w=["rstd"])
                    for c in range(KC):
                        write(ti, t0, n, c, rstd, gname)

            def write_xn(ti, t0, n, c, rstd, gname):
                s.op("dve", lambda e: e.scalar_tensor_tensor(out=xnT[:, c, t0:t0 + n], in0=hsl(ti, c), scalar=vcol(gname, c),
                                                           in1=rstd[:, 0:n], op0=ALU.mult, op1=ALU.mult),
                     r=[("hT", ti, c), "rstd"], w=[("xn", ti, c)])

            def ffn(w_in_ap, w_out_ap, tiles):
                NF = 4
                blocks = []
                f0 = 0
                while f0 < FC:
                    nf = min(NF, FC - f0)
                    blocks.append((f0, nf))
                    f0 += nf
                with contextlib.ExitStack() as st2:
                    wg = [st2.enter_context(_sbt(nc, "wg%d" % i, [128, KC, NF * 128], BF16)) for i in range(2)]
                    wu = [st2.enter_context(_sbt(nc, "wu%d" % i, [128, KC, NF * 128], BF16)) for i in range(2)]
                    wd = [st2.enter_context(_sbt(nc, "wd%d" % i, [128, NF, DM], BF16)) for i in range(2)]
                    h1 = [st2.enter_context(_sbt(nc, "h1%d" % i, [128, NF, 512], BF16)) for i in range(2)]
                    sg = [st2.enter_context(_sbt(nc, "sg%d" % i, [128, 512], BF16)) for i in range(2)]
                    pending = [None]
                    unit = 0
                    for bi, (f0, nf) in enumerate(blocks):
                        ws = bi % 2
                        s.dma("pool", wg[ws][:, :, 0:nf * 128],
                              w_in_ap[:, f0 * 128:(f0 + nf) * 128].rearrange("(k p) f -> p k f", p=128),
                              w=[("wg", ws)], key=("wg", ws))
                        s.dma("pool", wu[ws][:, :, 0:nf * 128],
                              w_in_ap[:, FF + f0 * 128:FF + (f0 + nf) * 128].rearrange("(k p) f -> p k f", p=128),
                              w=[("wu", ws)], key=("wu", ws))
                        s.dma("pool", wd[ws][:, 0:nf, :],
                              w_out_ap[f0 * 128:(f0 + nf) * 128, :].rearrange("(c p) d -> p c d", p=128),
                              w=[("wd", ws)], key=("wd", ws))
                        for (ti, t0, n) in tiles:
                            hs = unit % 2
                            for j in range(nf):
                                pa = 2 * (j % 2)
                                for k in range(KC):
                                    _mm(s, ps[pa][:, 0:n], wg[ws][:, k, j * 128:(j + 1) * 128], xnT[:, k, t0:t0 + n],
                                        k == 0, k == KC - 1, r=[("wg", ws), ("xn", ti, k)], w=[("ps", pa)])
                                for k in range(KC):
                                    _mm(s, ps[pa + 1][:, 0:n], wu[ws][:, k, j * 128:(j + 1) * 128], xnT[:, k, t0:t0 + n],
                                        k == 0, k == KC - 1, r=[("wu", ws), ("xn", ti, k)], w=[("ps", pa + 1)])
                                sgj = sg[j % 2]
                                s.op("act", lambda e: e.activation(out=sgj[:, 0:n], in_=ps[pa][:, 0:n], func=AF.Silu),
                                     r=[("ps", pa)], w=[("sg", j % 2)])
                                s.op("dve", lambda e: e.tensor_tensor(out=h1[hs][:, j, 0:n], in0=ps[pa + 1][:, 0:n], in1=sgj[:, 0:n], op=ALU.mult),
                                     r=[("ps", pa + 1), ("sg", j % 2)], w=[("h1", hs)])

                            def down(ws=ws, hs=hs, nf=nf, ti=ti, n=n):
                                for dc in range(KC):
                                    pb = 4 + dc % 4
                                    for j in range(nf):
                                        _mm(s, ps[pb][:, 0:n], wd[ws][:, j, dc * 128:(dc + 1) * 128], h1[hs][:, j, 0:n],
                                            j == 0, j == nf - 1, r=[("wd", ws), ("h1", hs)], w=[("ps", pb)])
                                    s.op("dve", lambda e: e.scalar_tensor_tensor(out=hsl(ti, dc), in0=ps[pb][:, 0:n], scalar=0.5,
                                                                               in1=hsl(ti, dc), op0=ALU.mult, op1=ALU.add),
                                         r=[("ps", pb), ("hT", ti, dc)], w=[("hT", ti, dc)])
                            if pending[0] is not None:
                                pending[0]()
                            pending[0] = down
                            unit += 1
                    pending[0]()
                    s.barrier()

            with contextlib.ExitStack() as st3:
                bufs = (st3.enter_context(_sbt(nc, "sqb", [128, KC, 512], BF16)),
                        st3.enter_context(_sbt(nc, "rstd", [128, 512], F32)))
                if not os.environ.get("K_SKIPA"):
                    rmsnorm_tiles("norm_ffn1", ALLT, bufs, write_xn)
                if not skip:
                    ffn(D["ffn1_w_in"][0], D["ffn1_w_out"][0], ALLT)
                if not os.environ.get("K_SKIPA"):
                    rmsnorm_tiles("norm_mix", ALLT, bufs, write_xn)
                s.barrier()
            if dbg == "A" and sq == 0:
                for (ti, t0, n) in ALLT:
                    s.dma("sp", dbg_t[0, :, :, t0:t0 + n], hall(ti), key="dbg")
                s.barrier()
                break
            for ti_ in range(4):
                if os.environ.get("K_SKIPSP"):
                    continue
                s.dma("sp", hspill[:, ti_ * 4096:(ti_ + 1) * 4096], hTf[:, ti_ * 4096:(ti_ + 1) * 4096], key=("spill", ti_))
            s.barrier()

            uT = xnT
            mixer(nc, s, D, W_IN, ps, uT, att_sl, rwo_sl, vcol, vecs, vecs2, VROW,
                  dict(ident=ident, ones_b=ones_b, perm_b=perm_b, bo_b=bo_b, bo64_b=bo64_b, mask4=mask4, maskl=maskl,
                       mask3=mask3, mask3f=mask3f, resetm=resetm), dbg, sq, dbg_t if dbg else None, REAL, hsl, hall, hspill)
            if dbg in ("B", "C", "K", "BC", "M1", "M2", "M2b", "M2a") and sq == 0:
                break

            with contextlib.ExitStack() as st3:
                bufs = (st3.enter_context(_sbt(nc, "sqb", [128, KC, 512], BF16)),
                        st3.enter_context(_sbt(nc, "rstd", [128, 512], F32)))
                rmsnorm_tiles("norm_ffn2", REAL, bufs, write_xn)
                ffn(D["ffn2_w_in"][0], D["ffn2_w_out"][0], REAL)
                ot = [st3.enter_context(_sbt(nc, "ot%d" % i, [128, DM], F32)) for i in range(2)]
                yf = st3.enter_context(_sbt(nc, "yf", [128, KC, 512], F32))

                def write_y(ti, t0, n, c, rstd, gname):
                    s.op("dve", lambda e: e.scalar_tensor_tensor(out=yf[:, c, 0:n], in0=hsl(ti, c), scalar=vcol(gname, c),
                                                               in1=rstd[:, 0:n], op0=ALU.mult, op1=ALU.mult),
                         r=[("hT", ti, c), "rstd"], w=[("yf", c)])
                blkc = 0
                for tile in REAL:
                    rmsnorm_tiles("norm_final", [tile], bufs, write_y)
                    ti, t0, n = tile
                    for b4 in range(4):
                        osl = blkc % 2
                        for half in range(2):
                            pb = 4 + (2 * blkc + half) % 4
                            for j in range(4):
                                c = half * 4 + j
                                s.op("pe", lambda e: e.transpose(ps[pb][:, j * 128:(j + 1) * 128], yf[:, c, b4 * 128:(b4 + 1) * 128], ident[:]),
                                     r=[("yf", c)], w=[("ps", pb)], inc=(j == 3))
                            s.op("act", lambda e: e.activation(out=ot[osl][:, half * 512:(half + 1) * 512], in_=ps[pb][:], func=AF.Copy),
                                 r=[("ps", pb)], w=[("ot", osl)])
                        blk = (t0 - 128) // 128 + b4
                        s.dma("sp", out[sq, blk * 128:(blk + 1) * 128, :], ot[osl][:], r=[("ot", osl)], key=("ot", osl))
                        blkc += 1
                s.barrier()
        s.finish()
    return nc


def kernel(**inputs):
    dbg = inputs.pop("_dbg", None)
    x = np.ascontiguousarray(np.asarray(inputs["x"], dtype=np.float32))
    nc = build_nc(dbg)
    common = {k: np.ascontiguousarray(np.asarray(v, dtype=np.float32)) for k, v in inputs.items() if k != "x"}
    common.update(host_consts())
    in_maps = []
    for c in range(NCORES):
        m = dict(common)
        m["x"] = x[c * SPC:(c + 1) * SPC]
        in_maps.append(m)
    res = run_bass_kernel_spmd(nc, in_maps, core_ids=list(range(NCORES)))
    if dbg:
        return [r["dbg"] for r in res.results]
    return np.concatenate([r["out"] for r in res.results], axis=0)
```

```python
import contextlib
import numpy as np
import concourse.bass as bass
import concourse.mybir as mybir
from concourse.bass_utils import run_bass_kernel_spmd

F32 = mybir.dt.float32
BF16 = mybir.dt.bfloat16
AF = mybir.ActivationFunctionType
ALU = mybir.AluOpType

NCORES = 8
B, SEQ, DM = 16, 2048, 1024
SPC = B // NCORES
T = SEQ + 128
NBLK = T // 128
KC = DM // 128
FF = 2816
FC = FF // 128
NQH, NKVH, HD = 16, 2, 64
LW, LA, LG = 64, 64, 160
RWC = 3 * 1024 + LW + LA + LG
INC = 1024 + 256 + RWC + 2048
RMS_EPS = 1e-6
GN_EPS = 64e-5


_SBN = [0]


def _sbt(nc, name, shape, dt):
    _SBN[0] += 1
    return nc.sbuf_tensor("%s_u%d" % (name, _SBN[0]), list(shape), dt)


class Tok:
    __slots__ = ("w", "r")

    def __init__(self):
        self.w = None
        self.r = {}


class Sched:
    def __init__(self, nc, stack):
        self.nc = nc
        self.stack = stack
        self.E = {"pe": nc.tensor, "act": nc.scalar, "dve": nc.vector, "pool": nc.gpsimd, "sp": nc.sync}
        self.sem = {k: stack.enter_context(nc.semaphore("s_" + k)) for k in ("pe", "act", "dve", "pool")}
        self.cnt = {k: 0 for k in self.E}
        self.seen = {k: {} for k in self.E}
        self.toks = {}
        self.dsem = {}
        self.dcnt = {}
        self.nwaits = 0

    def tok(self, key):
        t = self.toks.get(key)
        if t is None:
            t = self.toks[key] = Tok()
        return t

    def _semof(self, e):
        if isinstance(e, tuple):
            return self.dsem[e[1]], 16
        return self.sem[e], 1

    def _need(self, r, w):
        need = {}
        for k in r:
            t = self.tok(k)
            if t.w is not None:
                if need.get(t.w[0], 0) < t.w[1]:
                    need[t.w[0]] = t.w[1]
        for k in w:
            t = self.tok(k)
            if t.w is not None:
                if need.get(t.w[0], 0) < t.w[1]:
                    need[t.w[0]] = t.w[1]
            for e2, n in t.r.items():
                if need.get(e2, 0) < n:
                    need[e2] = n
        return need

    def _waits(self, eng, need, myidx, is_dma=False):
        E = self.E[eng]
        seen = self.seen[eng]
        for e2, n in need.items():
            if e2 == eng and not is_dma:
                if eng == "pe" or (eng != "pool" and n < myidx - 3):
                    continue
            if n <= seen.get(e2, 0):
                continue
            seen[e2] = n
            sem, mult = self._semof(e2)
            E.wait_ge(sem, n * mult)
            self.nwaits += 1

    def op(self, eng, fn, r=(), w=(), inc=True):
        psr = [k for k in r if isinstance(k, tuple) and k[0] == "ps"]
        if psr:
            w = list(w) + [k for k in psr if k not in w]
        idx = self.cnt[eng] + 1
        need = self._need(r, w)
        self._waits(eng, need, idx)
        ins = fn(self.E[eng])
        if inc:
            ins.then_inc(self.sem[eng], 1)
            self.cnt[eng] = idx
        for k in r:
            t = self.tok(k)
            if t.r.get(eng, 0) < idx:
                t.r[eng] = idx
        for k in w:
            t = self.tok(k)
            t.w = (eng, idx)
            t.r = {}
        return ins

    def dma(self, q, out, in_, r=(), w=(), key=None, **kw):
        if key not in self.dsem:
            self.dsem[key] = self.stack.enter_context(self.nc.semaphore("d_%d" % len(self.dsem)))
            self.dcnt[key] = 0
        need = self._need(r, w)
        self._waits(q, need, 0, is_dma=True)
        self.dcnt[key] += 1
        k = self.dcnt[key]
        pe = ("dma", key)
        ins = self.E[q].dma_start(out=out, in_=in_, **kw)
        ins.then_inc(self.dsem[key], 16)
        for kk in r:
            self.tok(kk).r[pe] = k
        for kk in w:
            t = self.tok(kk)
            t.w = (pe, k)
            t.r = {}
        return ins

    def barrier(self):
        for e in ("pe", "act", "dve", "pool", "sp"):
            need = {e2: self.cnt[e2] for e2 in ("pe", "act", "dve", "pool") if self.cnt[e2] > 0}
            for key, k in self.dcnt.items():
                need[("dma", key)] = k
            self._waits(e, need, 1 << 60, is_dma=True)

    def finish(self):
        self.barrier()


def _mm(s, out, lhsT, rhs, start, stop, r, w, inc=None):
    if inc is None:
        inc = stop
    return s.op("pe", lambda e: e.matmul(out, lhsT=lhsT, rhs=rhs, start=start, stop=stop), r=r, w=w, inc=inc)


def token_tiles(t0, t1, n=512):
    out = []
    while t0 < t1:
        m = min(n, t1 - t0)
        out.append((t0, m))
        t0 += m
    return out


Q0, K0, V0, R0, RK0, RV0, WD0, GD0, GA0, GR0 = 0, 1024, 1152, 1280, 2304, 3328, 4352, 4480, 4640, 5664
LWC = -0.6065306597126334


def mixer(nc, s, D, W_IN, ps, uT, att_sl, rwo_sl, vcol, vecs, vecs2, VROW, C, dbg, sq, dbg_t, REAL, hsl, hall, hspill):
    ident, ones_b, perm_b, bo_b, bo64_b = C["ident"], C["ones_b"], C["perm_b"], C["bo_b"], C["bo64_b"]
    mask4, maskl, mask3, mask3f, resetm = C["mask4"], C["maskl"], C["mask3"], C["mask3f"], C["resetm"]
    RT = [(4, 0, 128)] + list(REAL)
    MU = VROW["mu"]

    def mucol(j):
        return vecs[:, MU + j:MU + j + 1]

    def wchunk_src(col0, ncols):
        return W_IN[:, col0:col0 + ncols].rearrange("(k p) f -> p k f", p=128)

    def proj(wt, wtok, t0, n, pb, m=128):
        for k in range(KC):
            _mm(s, ps[pb][0:m, 0:n], wt[:, k, 0:m], uT[:, k, t0:t0 + n], k == 0, k == KC - 1, r=[wtok], w=[("ps", pb)])

    with contextlib.ExitStack() as sm:
        def sbm(name, shape, dt):
            return sm.enter_context(_sbt(nc, name, list(shape), dt))

        la = sbm("la", [128, T], BF16)
        lg1 = sbm("lg1", [128, T], BF16)
        lg2 = sbm("lg2", [128, T], BF16)
        identb = sbm("identb", [128, 128], BF16)
        s.op("act", lambda e: e.activation(out=identb[:], in_=ident[:], func=AF.Copy), w=["identb"])

        import os
        with contextlib.ExitStack() as s1:
            wl = [s1.enter_context(_sbt(nc, "wl%d" % i, [128, KC, 128], BF16)) for i in range(3)]
            pt = s1.enter_context(_sbt(nc, "pt1", [128, 513], F32))
            dtmp = s1.enter_context(_sbt(nc, "dtmp1", [128, 512], F32))
            shf = s1.enter_context(_sbt(nc, "shf1", [128, 512], F32))
            specs = [(WD0, 128, 24, "la"), (GD0, 128, 25, "lg1"), (GD0 + 128, 32, 26, "lg2")]
            s.op("dve", lambda e: e.memset(wl[2][:, :, :], 0.0), w=[("wl", 2)])
            for i, (col0, m, muj, nm) in enumerate(specs):
                s.dma("pool", wl[i][:, :, 0:m], wchunk_src(col0, m), w=[("wl", i)], key=("wl", i))
            if os.environ.get("K_SKIPM1"):
                specs = []
            for i, (col0, m, muj, nm) in enumerate(specs):
                m = 128
                s.op("dve", lambda e: e.memset(pt[:, 0:1], 0.0), r=["pt"], w=["pt"])
                for (ti, t0, n) in RT:
                    pb = i % 2
                    proj(wl[i], ("wl", i), t0, n, pb, m)
                    s.op("act", lambda e: e.activation(out=pt[0:m, 1:n + 1], in_=ps[pb][0:m, 0:n], func=AF.Copy),
                         r=[("ps", pb), "pt"], w=["pt"])
                    s.op("dve", lambda e: e.tensor_tensor(out=dtmp[0:m, 0:n], in0=pt[0:m, 0:n], in1=pt[0:m, 1:n + 1], op=ALU.subtract),
                         r=["pt"], w=["dtmp"])
                    s.op("dve", lambda e: e.scalar_tensor_tensor(out=shf[0:m, 0:n], in0=dtmp[0:m, 0:n], scalar=vecs[0:m, MU + muj:MU + muj + 1],
                                                               in1=pt[0:m, 1:n + 1], op0=ALU.mult, op1=ALU.add),
                         r=["dtmp", "pt"], w=["shf"])
                    s.op("dve", lambda e: e.tensor_copy(out=pt[0:m, 0:1], in_=pt[0:m, n:n + 1]), r=["pt", "shf"], w=["pt"])
                    if nm == "la":
                        s.op("act", lambda e: e.activation(out=la[0:64, t0:t0 + n], in_=shf[0:64, 0:n], func=AF.Tanh), r=["shf"], w=["la"])
                        s.op("act", lambda e: e.activation(out=la[64:128, t0:t0 + n], in_=shf[64:128, 0:n], func=AF.Copy), r=["shf"], w=["la"])
                    elif nm == "lg1":
                        s.op("act", lambda e: e.activation(out=lg1[:, t0:t0 + n], in_=shf[:, 0:n], func=AF.Sigmoid), r=["shf"], w=["lg1"])
                    else:
                        s.op("act", lambda e: e.activation(out=lg2[:, t0:t0 + n], in_=shf[:, 0:n], func=AF.Sigmoid), r=["shf"], w=["lg2"])
            s.barrier()

        if dbg == "M1":
            return
        with contextlib.ExitStack() as s2:
            def sb2(name, shape, dt):
                return s2.enter_context(_sbt(nc, name, list(shape), dt))
            t1 = sb2("ropet1", [128, 512], F32)
            t2 = sb2("ropet2", [128, 512], F32)
            cosT = sb2("cosT", [128, T], F32)
            sinT = sb2("sinT", [128, T], F32)
            kTd = [sb2("kTd%d" % g, [128, T], BF16) for g in range(2)]
            Vt = sb2("Vt", [128, NBLK, 128], BF16)
            qT = [sb2("qT%d" % i, [128, SEQ], BF16) for i in range(2)]
            wq = [sb2("wq%d" % i, [128, KC, 128], BF16) for i in range(3)]
            pre = [sb2("pre%d" % i, [128, 512], BF16) for i in range(2)]
            Eb = [sb2("Eb%d" % i, [128, 384], BF16) for i in range(4)]
            rd = sb2("rd", [128, 512], F32)
            print("SBUF remaining at M2:", nc.sbuf_bytes_remaining)
            s.dma("sp", cosT[:], D["c_cos"][:, :], w=["cos"], key="cos")
            s.dma("sp", sinT[:], D["c_sin"][:, :], w=["sin"], key="sin")

            def rope_tile(wt, wtok, t0, n, dst, unit):
                import os
                RS = int(os.environ.get("K_RSTEP", "6"))
                pa, pb = 2 * (unit % 2), 2 * (unit % 2) + 1
                pr = pre[unit % 2]
                proj(wt, wtok, t0, n, pa)
                if RS < 2:
                    return
                s.op("act", lambda e: e.activation(out=pr[:, 0:n], in_=ps[pa][:, 0:n], func=AF.Copy), r=[("ps", pa)], w=[("pre", unit % 2)])
                if RS < 3:
                    return
                _mm(s, ps[pb][:, 0:n], perm_b[:], pr[:, 0:n], True, True, r=[("pre", unit % 2)], w=[("ps", pb)])
                if RS < 4:
                    return
                VV = os.environ.get("K_VAR", "")
                if VV == "1":
                    s.op("dve", lambda e: e.tensor_tensor(out=t1[:, 0:n], in0=cosT[:, t0:t0 + n], in1=t2[:, 0:n], op=ALU.mult),
                         r=[("ps", pa), "cos"], w=["t1"])
                elif VV == "2":
                    s.op("dve", lambda e: e.tensor_tensor(out=t1[:, 0:n], in0=t2[:, 0:n], in1=ps[pa][:, 0:n], op=ALU.mult),
                         r=[("ps", pa), "cos"], w=["t1"])
                elif VV == "3":
                    s.op("dve", lambda e: e.tensor_copy(out=t1[:, 0:n], in_=ps[pa][:, 0:n]),
                         r=[("ps", pa), "cos"], w=["t1"])
                elif VV == "4":
                    s.op("dve", lambda e: e.tensor_tensor(out=t1[:, 0:n], in0=cosT[:, t0:t0 + n], in1=ps[pa][:, 0:n], op=ALU.mult),
                         r=[("ps", pa), "cos", ("pre", unit % 2)], w=["t1"])
                else:
                    s.op("dve", lambda e: e.tensor_tensor(out=t1[:, 0:n], in0=cosT[:, t0:t0 + n], in1=ps[pa][:, 0:n], op=ALU.mult),
                         r=[("ps", pa), "cos"], w=["t1"])
                if RS < 5:
                    return
                s.op("dve", lambda e: e.tensor_tensor(out=t2[:, 0:n], in0=sinT[:, t0:t0 + n], in1=ps[pb][:, 0:n], op=ALU.mult),
                     r=[("ps", pb), "sin"], w=["t2"])
                if RS < 6:
                    return
                s.op("dve", lambda e: e.tensor_tensor(out=dst, in0=t1[:, 0:n], in1=t2[:, 0:n], op=ALU.add),
                     r=["t1", "t2"], w=[])

            unit = 0
            for g in range(2):
                wt = wq[g]
                for hf in range(2):
                    s.dma("pool", wt[:, :, hf * 64:(hf + 1) * 64], wchunk_src(K0 + g * 64, 64), w=[("wq", g)], key=("wq", g, hf))
            s.dma("pool", wq[2][:, :, :], wchunk_src(V0, 128), w=[("wq", 2)], key=("wq", 2))
            s.barrier()
            if dbg == "M2a":
                return
            import os
            NR = int(os.environ.get("K_NROPE", "100"))
            for g in range(2):
                for (ti, t0, n) in RT:
                    if unit >= NR or (os.environ.get("K_SKIP128") and n == 128):
                        continue
                    rope_tile(wq[g], ("wq", g), t0, n, kTd[g][:, t0:t0 + n], unit)
                    unit += 1
            if dbg == "M2b":
                s.barrier()
                return
            for blk in range(NBLK):
                pb = 4 + (blk // 4) % 2
                j = blk % 4
                for k in range(KC):
                    _mm(s, ps[pb][:, j * 128:(j + 1) * 128], uT[:, k, blk * 128:(blk + 1) * 128], wq[2][:, k, :], k == 0, k == KC - 1,
                        r=[("wq", 2)], w=[("ps", pb)])
                if j == 3 or blk == NBLK - 1:
                    b0 = blk - j
                    s.op("act", lambda e: e.activation(out=Vt[:, b0:blk + 1, :], in_=ps[pb][:, 0:(j + 1) * 128].rearrange("p (j t) -> p j t", j=j + 1),
                                                       func=AF.Copy), r=[("ps", pb)], w=["Vt"])
            s.barrier()

            if dbg == "M2":
                return
            def rope_steps(c):
                ws = c % 3
                qt = qT[c % 2]
                steps = []

                def load():
                    s.dma("pool", wq[ws][:, :, :], wchunk_src(Q0 + c * 128, 128), w=[("wq", ws)], key=("wq", ws))
                steps.append(load)
                for (ti, t0, n) in REAL:
                    def stepA(t0=t0, n=n):
                        proj(wq[ws], ("wq", ws), t0, n, 6)
                        s.op("act", lambda e: e.activation(out=pre[0][:, 0:n], in_=ps[6][:, 0:n], func=AF.Copy), r=[("ps", 6)], w=[("pre", 0)])
                        _mm(s, ps[7][:, 0:n], perm_b[:], pre[0][:, 0:n], True, True, r=[("pre", 0)], w=[("ps", 7)])

                    def stepB(t0=t0, n=n, ti=ti):
                        s.op("dve", lambda e: e.tensor_tensor(out=t1[:, 0:n], in0=ps[6][:, 0:n], in1=cosT[:, t0:t0 + n], op=ALU.mult),
                             r=[("ps", 6), "cos"], w=["t1"])
                        s.op("dve", lambda e: e.tensor_tensor(out=t2[:, 0:n], in0=ps[7][:, 0:n], in1=sinT[:, t0:t0 + n], op=ALU.mult),
                             r=[("ps", 7), "sin"], w=["t2"])
                        s.op("dve", lambda e: e.tensor_tensor(out=qt[:, t0 - 128:t0 - 128 + n], in0=t1[:, 0:n], in1=t2[:, 0:n], op=ALU.add),
                             r=["t1", "t2"], w=[("qT", c % 2, ti)])
                    steps.append(stepA)
                    steps.append(stepB)
                return steps

            for st_ in rope_steps(0):
                st_()
            for c in range(KC):
                g = c // 4
                qt = qT[c % 2]
                pend = rope_steps(c + 1) if c + 1 < KC else []
                units = [(ti, b4, hh) for (ti, t0, n) in REAL for b4 in range(4) for hh in range(2)]

                def qk(u):
                    ti, b4, hh = units[u]
                    nblk = 1 + 4 * ti + b4
                    qc = (nblk - 1) * 128
                    R0_, R1_ = hh * 64, hh * 64 + 64
                    sbk = u % 4
                    E = Eb[sbk]
                    for j, kb in enumerate((0, nblk - 1, nblk)):
                        _mm(s, ps[sbk][:, j * 128:(j + 1) * 128], kTd[g][R0_:R1_, kb * 128:(kb + 1) * 128], qt[R0_:R1_, qc:qc + 128],
                            True, True, r=[("qT", c % 2, ti)], w=[("ps", sbk)], inc=(j == 2))
                    s.op("act", lambda e: e.activation(out=E[:, :], in_=ps[sbk][:, 0:384], func=AF.Exp, scale=0.125),
                         r=[("ps", sbk)], w=[("E", sbk)])
                    mk = mask3f if nblk == 1 else mask3
                    s.op("dve", lambda e: e.tensor_tensor(out=E[:, :], in0=E[:, :], in1=mk[:, :], op=ALU.mult),
                         r=[("E", sbk)], w=[("E", sbk)])

                def pv(u):
                    ti, b4, hh = units[u]
                    nblk = 1 + 4 * ti + b4
                    R0_, R1_ = hh * 64, hh * 64 + 64
                    sbk = u % 4
                    E = Eb[sbk]
                    for j, kb in enumerate((0, nblk - 1, nblk)):
                        _mm(s, ps[4][R0_:R1_, b4 * 128:(b4 + 1) * 128], Vt[:, kb, g * 64:(g + 1) * 64], E[:, j * 128:(j + 1) * 128],
                            j == 0, j == 2, r=[("E", sbk), "Vt"], w=[("ps", 4)])
                    for j in range(3):
                        _mm(s, ps[5][R0_:R1_, b4 * 128:(b4 + 1) * 128], ones_b[:, 0:64], E[:, j * 128:(j + 1) * 128],
                            j == 0, j == 2, r=[("E", sbk)], w=[("ps", 5)])
                    if b4 == 3 and hh == 1:
                        s.op("dve", lambda e: e.tensor_scalar_add(rd[:, :], ps[5][:, :], vecs2[:, 8 + c:9 + c]),
                             r=[("ps", 5)], w=["rd"])
                        s.op("dve", lambda e: e.reciprocal(out=rd[:, :], in_=rd[:, :]), r=["rd"], w=["rd"])
                        s.op("dve", lambda e: e.tensor_tensor(out=att_sl(ti, c), in0=ps[4][:, :], in1=rd[:, :], op=ALU.mult),
                             r=[("ps", 4), "rd"], w=[("att", ti, c)])

                LA = 3
                NU = len(units)
                for u in range(min(LA, NU)):
                    qk(u)
                for u in range(NU):
                    if u + LA < NU:
                        qk(u + LA)
                    pv(u)
                    if pend and u % 3 == 2:
                        pend.pop(0)()
                while pend:
                    pend.pop(0)()
            s.barrier()
        if dbg in ("B", "BC") and sq == 0:
            for (ti, t0, n) in REAL:
                for c in range(KC):
                    s.dma("pool", dbg_t[0, :, c, t0:t0 + n], att_sl(ti, c), key="dbg")
            s.barrier()
            if dbg == "B":
                return
        if dbg == "K" and sq == 0:
            return

        with contextlib.ExitStack() as s4:
            def sb4(name, shape, dt):
                return s4.enter_context(_sbt(nc, name, list(shape), dt))
            W4 = 256
            NSL = 2
            FN = ("r", "k", "v", "d", "lw", "a", "kk", "kmod", "b", "cw", "ep", "en", "epv", "g")
            ALIAS = {"rn": "d", "t": "d", "kkn": "kk", "kh": "epv", "bh": "en", "y": "lw", "yc": "cw", "ee": "k"}
            SL = []
            for sl in range(NSL):
                Bf = {}
                Bf["wr"] = [sb4("wr%d_%d" % (sl, j), [128, KC, 128], BF16) for j in range(3)]
                Bf["lwt"] = sb4("lwt%d" % sl, [128, 128], BF16)
                Bf["g2a"] = sb4("g2a%d" % sl, [128, 128], BF16)
                Bf["g2b"] = sb4("g2b%d" % sl, [128, 128], BF16)
                s.op("dve", lambda e: e.memset(Bf["g2b"][:, :], 0.0), w=[("g2b", sl)])
                Bf["ptx"] = [sb4("ptx%d_%d" % (sl, i), [128, W4 + 1], F32) for i in range(3)]
                F = {nm: sb4("f%d_%s" % (sl, nm), [128, W4], F32) for nm in FN}
                for a_, b_ in ALIAS.items():
                    F[a_] = F[b_]
                Bf["F"] = F
                Bf["sqk"] = sb4("sqk%d" % sl, [128, W4], BF16)
                Bf["AR"] = sb4("AR%d" % sl, [128, 2, 256], BF16)
                Bf["ktb"] = sb4("ktb%d" % sl, [128, W4], BF16)
                Bf["btb"] = sb4("btb%d" % sl, [128, W4], BF16)
                Bf["KBV"] = sb4("KBV%d" % sl, [128, 2, 384], BF16)
                Bf["MM"] = [sb4("MM%d_%d" % (sl, hh), [128, 2, 512], BF16) for hh in range(2)]
                Bf["NN"] = sb4("NN%d" % sl, [128, 2, W4], BF16)
                Bf["MNb"] = sb4("MNb%d" % sl, [128, 2, 2, 512], BF16)
                Bf["Zb"] = sb4("Zb%d" % sl, [128, 2, 2, 256], BF16)
                Bf["RH"] = sb4("RH%d" % sl, [128, 128], BF16)
                Bf["UU"] = sb4("UU%d" % sl, [128, 128], BF16)
                Bf["Sst"] = sb4("Sst%d" % sl, [128, 64], F32)
                Bf["Sb"] = sb4("Sb%d" % sl, [128, 128], BF16)
                Bf["yb"] = sb4("yb%d" % sl, [128, W4], BF16)
                SL.append(Bf)
            print("SBUF remaining at M4:", nc.sbuf_bytes_remaining)

            def v3(ap, nch):
                return ap.rearrange("p (j t) -> p j t", j=nch)

            RT4 = [(4, 0, 128, None)]
            for (ti, t0, n) in REAL:
                for h_ in range(2):
                    RT4.append((ti, t0 + W4 * h_, W4, W4 * h_))

            def m4_pair(c, sl):
                Bf = SL[sl]
                F = Bf["F"]
                wr, lwt, g2a, g2b, ptx = Bf["wr"], Bf["lwt"], Bf["g2a"], Bf["g2b"], Bf["ptx"]
                sqk, AR, ktb, btb, KBV, MM, NN, MNb, Zb = Bf["sqk"], Bf["AR"], Bf["ktb"], Bf["btb"], Bf["KBV"], Bf["MM"], Bf["NN"], Bf["MNb"], Bf["Zb"]
                RH, UU, Sst, Sb, yb = Bf["RH"], Bf["UU"], Bf["Sst"], Bf["Sb"], Bf["yb"]
                banks = [4 * sl + i for i in range(3)]
                YB = 4 * sl + 3
                rr = [0]

                def nb():
                    rr[0] = (rr[0] + 1) % 3
                    return banks[rr[0]]

                def T_(nm):
                    return (nm, sl)

                def fT(nm):
                    return ("f_" + ALIAS.get(nm, nm), sl)

                for j, col0 in enumerate((R0, RK0, RV0)):
                    s.dma("pool", wr[j][:, :, :], wchunk_src(col0 + c * 128, 128), w=[("wr", sl, j)], key=("wr", sl, j))
                s.dma("pool", lwt[0:64, :], D["w2"][0, :, c * 128:(c + 1) * 128], w=[("lwt", sl)], key=("lwt", sl, 0))
                s.dma("pool", lwt[64:128, :], D["a2"][0, :, c * 128:(c + 1) * 128], w=[("lwt", sl)], key=("lwt", sl, 1))
                s.dma("pool", g2a[:, :], D["g2"][0, 0:128, c * 128:(c + 1) * 128], w=[("g2a", sl)], key=("g2a", sl))
                s.dma("pool", g2b[0:32, :], D["g2"][0, 128:160, c * 128:(c + 1) * 128], w=[("g2b", sl)], key=("g2b", sl))
                s.op("dve", lambda e: e.memset(Sst[:], 0.0), r=[T_("Sst")], w=[T_("Sst")])
                s.op("dve", lambda e: e.memset(Sb[:], 0.0), r=[T_("Sb")], w=[T_("Sb")])
                for i in range(3):
                    s.op("dve", lambda e: e.memset(ptx[i][:, 0:1], 0.0), r=[("ptx", sl, i)], w=[("ptx", sl, i)])
                yield
                for (ti, t0, n, off) in RT4:
                    nch = n // 128
                    for i, nm in enumerate(("r", "k", "v")):
                        pb = nb()
                        proj(wr[i], ("wr", sl, i), t0, n, pb)
                        p_ = ptx[i]
                        s.op("act", lambda e: e.activation(out=p_[:, 1:n + 1], in_=ps[pb][:, 0:n], func=AF.Copy), r=[("ps", pb), ("ptx", sl, i)], w=[("ptx", sl, i)])
                        s.op("dve", lambda e: e.tensor_tensor(out=F["d"][:, 0:n], in0=p_[:, 0:n], in1=p_[:, 1:n + 1], op=ALU.subtract),
                             r=[("ptx", sl, i)], w=[fT("d")])
                        s.op("dve", lambda e: e.scalar_tensor_tensor(out=F[nm][:, 0:n], in0=F["d"][:, 0:n], scalar=mucol(8 * i + c),
                                                                   in1=p_[:, 1:n + 1], op0=ALU.mult, op1=ALU.add),
                             r=[fT("d"), ("ptx", sl, i)], w=[fT(nm)])
                        s.op("dve", lambda e: e.tensor_copy(out=p_[:, 0:1], in_=p_[:, n:n + 1]), r=[("ptx", sl, i), fT(nm)], w=[("ptx", sl, i)])
                        yield
                    pb = nb()
                    _mm(s, ps[pb][:, 0:n], lwt[0:64, :], la[0:64, t0:t0 + n], True, True, r=[("lwt", sl)], w=[("ps", pb)])
                    s.op("act", lambda e: e.activation(out=F["lw"][:, 0:n], in_=ps[pb][:, 0:n], func=AF.Sigmoid, bias=vcol("w0", c)),
                         r=[("ps", pb)], w=[fT("lw")])
                    pb = nb()
                    _mm(s, ps[pb][:, 0:n], lwt[64:128, :], la[64:128, t0:t0 + n], True, True, r=[("lwt", sl)], w=[("ps", pb)])
                    s.op("act", lambda e: e.activation(out=F["a"][:, 0:n], in_=ps[pb][:, 0:n], func=AF.Sigmoid, bias=vcol("a0", c)),
                         r=[("ps", pb)], w=[fT("a")])
                    pb = nb()
                    _mm(s, ps[pb][:, 0:n], g2a[:, :], lg1[:, t0:t0 + n], True, False, r=[("g2a", sl)], w=[("ps", pb)], inc=False)
                    _mm(s, ps[pb][:, 0:n], g2b[:, :], lg2[:, t0:t0 + n], False, True, r=[("g2b", sl)], w=[("ps", pb)])
                    s.op("act", lambda e: e.activation(out=F["g"][:, 0:n], in_=ps[pb][:, 0:n], func=AF.Copy), r=[("ps", pb)], w=[fT("g")])
                    yield
                    s.op("dve", lambda e: e.tensor_scalar_mul(F["kk"][:, 0:n], F["k"][:, 0:n], vcol("k_k", c)), r=[fT("k")], w=[fT("kk")])
                    s.op("act", lambda e: e.activation(out=sqk[:, 0:n], in_=F["kk"][:, 0:n], func=AF.Square), r=[fT("kk")], w=[T_("sqk")])
                    pb = nb()
                    _mm(s, ps[pb][:, 0:n], bo_b[:], sqk[:, 0:n], True, True, r=[T_("sqk")], w=[("ps", pb)])
                    s.op("act", lambda e: e.activation(out=F["rn"][:, 0:n], in_=ps[pb][:, 0:n], func=AF.Sqrt, bias=1e-24), r=[("ps", pb)], w=[fT("rn")])
                    s.op("dve", lambda e: e.reciprocal(out=F["rn"][:, 0:n], in_=F["rn"][:, 0:n]), r=[fT("rn")], w=[fT("rn")])
                    s.op("dve", lambda e: e.tensor_tensor(out=F["kkn"][:, 0:n], in0=F["kk"][:, 0:n], in1=F["rn"][:, 0:n], op=ALU.mult),
                         r=[fT("kk"), fT("rn")], w=[fT("kkn")])
                    yield
                    s.op("dve", lambda e: e.tensor_scalar(F["t"][:, 0:n], F["a"][:, 0:n], vcol("k_a", c), vecs2[:, c:c + 1], ALU.mult, ALU.add),
                         r=[fT("a")], w=[fT("t")])
                    s.op("dve", lambda e: e.tensor_tensor(out=F["kmod"][:, 0:n], in0=F["k"][:, 0:n], in1=F["t"][:, 0:n], op=ALU.mult),
                         r=[fT("k"), fT("t")], w=[fT("kmod")])
                    s.op("pool", lambda e: e.tensor_tensor(out=F["b"][:, 0:n], in0=F["kkn"][:, 0:n], in1=F["a"][:, 0:n], op=ALU.mult),
                         r=[fT("kkn"), fT("a")], w=[fT("b")])
                    s.op("dve", lambda e: e.tensor_scalar_mul(F["lw"][:, 0:n], F["lw"][:, 0:n], LWC), r=[fT("lw")], w=[fT("lw")])
                    s.op("dve", lambda e: e.tensor_tensor_scan(out=F["cw"][:, 0:n], data0=resetm[:, 0:n], data1=F["lw"][:, 0:n], initial=0.0,
                                                             op0=ALU.mult, op1=ALU.add), r=[fT("lw")], w=[fT("cw")])
                    s.op("act", lambda e: e.activation(out=F["ep"][:, 0:n], in_=F["cw"][:, 0:n], func=AF.Exp), r=[fT("cw")], w=[fT("ep")])
                    s.op("act", lambda e: e.activation(out=F["en"][:, 0:n], in_=F["cw"][:, 0:n], func=AF.Exp, scale=-1.0), r=[fT("cw")], w=[fT("en")])
                    s.op("pool", lambda e: e.tensor_tensor(out=F["t"][:, 0:n], in0=F["cw"][:, 0:n], in1=F["lw"][:, 0:n], op=ALU.subtract),
                         r=[fT("cw"), fT("lw"), fT("t")], w=[fT("t")])
                    s.op("act", lambda e: e.activation(out=F["epv"][:, 0:n], in_=F["t"][:, 0:n], func=AF.Exp), r=[fT("t")], w=[fT("epv")])
                    for j in range(nch):
                        s.op("act", lambda e: e.activation(out=F["ee"][:, j * 128:(j + 1) * 128], in_=F["cw"][:, j * 128:(j + 1) * 128], func=AF.Exp,
                                                           scale=-1.0, bias=F["cw"][:, j * 128 + 127:j * 128 + 128]), r=[fT("cw"), fT("k")], w=[fT("ee")])
                    yield
                    s.op("dve", lambda e: e.scalar_tensor_tensor(out=AR[:, 0:nch, 0:128], in0=v3(F["kkn"][:, 0:n], nch), scalar=-1.0,
                                                               in1=v3(F["epv"][:, 0:n], nch), op0=ALU.mult, op1=ALU.mult),
                         r=[fT("kkn"), fT("epv"), T_("AR")], w=[T_("AR")])
                    s.op("dve", lambda e: e.tensor_tensor(out=AR[:, 0:nch, 128:256], in0=v3(F["r"][:, 0:n], nch), in1=v3(F["ep"][:, 0:n], nch), op=ALU.mult),
                         r=[fT("r"), fT("ep"), T_("AR")], w=[T_("AR")])
                    s.op("pool", lambda e: e.tensor_tensor(out=ktb[:, 0:n], in0=F["kmod"][:, 0:n], in1=F["en"][:, 0:n], op=ALU.mult),
                         r=[fT("kmod"), fT("en")], w=[T_("ktb")])
                    s.op("pool", lambda e: e.tensor_tensor(out=btb[:, 0:n], in0=F["b"][:, 0:n], in1=F["en"][:, 0:n], op=ALU.mult),
                         r=[fT("b"), fT("en")], w=[T_("btb")])
                    s.op("pool", lambda e: e.tensor_tensor(out=F["kh"][:, 0:n], in0=F["kmod"][:, 0:n], in1=F["ee"][:, 0:n], op=ALU.mult),
                         r=[fT("kmod"), fT("ee"), fT("epv")], w=[fT("kh")])
                    s.op("pool", lambda e: e.tensor_tensor(out=F["bh"][:, 0:n], in0=F["b"][:, 0:n], in1=F["ee"][:, 0:n], op=ALU.mult),
                         r=[fT("b"), fT("ee"), fT("en")], w=[fT("bh")])
                    yield
                    for j in range(nch):
                        pb = nb()
                        cs = slice(j * 128, (j + 1) * 128)
                        for q_, nm in enumerate(("kh", "bh", "v")):
                            s.op("pe", lambda e: e.transpose(ps[pb][:, q_ * 128:(q_ + 1) * 128], F[nm][:, cs], ident[:]),
                                 r=[fT(nm)], w=[("ps", pb)], inc=(q_ == 2))
                        s.op("act", lambda e: e.activation(out=KBV[:, j, :], in_=ps[pb][:, 0:384], func=AF.Copy), r=[("ps", pb), T_("KBV")], w=[T_("KBV")])
                    yield
                    nbank = [None, None]
                    for hh in range(2):
                        nbank[hh] = nb()
                        R = slice(hh * 64, hh * 64 + 64)
                        for j in range(nch):
                            cs = slice(j * 128, (j + 1) * 128)
                            _mm(s, ps[nbank[hh]][:, cs], AR[R, j, 0:128], btb[R, cs], True, True, r=[T_("btb"), T_("AR")], w=[("ps", nbank[hh])])
                        s.op("dve", lambda e: e.tensor_tensor(out=NN[:, hh, 0:n], in0=ps[nbank[hh]][:, 0:n], in1=maskl[:, 0:n], op=ALU.mult),
                             r=[("ps", nbank[hh]), T_("NN")], w=[T_("NN")])
                    yield
                    for j in range(nch):
                        cs = slice(j * 128, (j + 1) * 128)
                        for hh in range(2):
                            R = slice(hh * 64, hh * 64 + 64)
                            px = nb()
                            _mm(s, ps[px][:, 0:256], btb[R, cs], AR[R, j, :], True, True, r=[T_("btb"), T_("AR")], w=[("ps", px)], inc=False)
                            _mm(s, ps[px][:, 256:512], ktb[R, cs], AR[R, j, :], True, True, r=[T_("ktb"), T_("AR")], w=[("ps", px)])
                            s.op("dve", lambda e: e.tensor_tensor(out=MM[hh][:, j, :], in0=ps[px][:, :], in1=mask4[:, :], op=ALU.mult),
                                 r=[("ps", px), ("MM", sl, hh)], w=[("MM", sl, hh)])
                            yield
                    Mc, Nc, Zc = {}, {}, {}
                    for j in range(nch):
                        for hh in range(2):
                            Mc[j, hh] = MM[hh][:, j, 0:128]
                            Nc[j, hh] = NN[:, hh, j * 128:(j + 1) * 128]
                            s.op("pool", lambda e: e.tensor_tensor(out=Zb[:, j, 0, hh * 128:(hh + 1) * 128], in0=MM[hh][:, j, 0:128], in1=identb[:, :], op=ALU.add),
                                 r=[("MM", sl, hh), ("Zb", sl, j)], w=[("Zb", sl, j)])
                            Zc[j, hh] = Zb[:, j, 0, hh * 128:(hh + 1) * 128]
                    for p in range(1, 7):
                        pp = p % 2
                        pmn = {}
                        for j in range(nch):
                            pmn[j] = nb()
                            for hh in range(2):
                                if p < 6:
                                    _mm(s, ps[pmn[j]][:, hh * 128:(hh + 1) * 128], Nc[j, hh], Mc[j, hh], True, True,
                                        r=[("MM", sl, hh), T_("NN"), ("MNb", sl, j)], w=[("ps", pmn[j])], inc=False)
                                _mm(s, ps[pmn[j]][:, 256 + hh * 128:256 + (hh + 1) * 128], Mc[j, hh], Nc[j, hh], True, True,
                                    r=[("MM", sl, hh), T_("NN"), ("MNb", sl, j)], w=[("ps", pmn[j])], inc=(hh == 1))
                            lo = 0 if p < 6 else 256
                            s.op("act", lambda e: e.activation(out=MNb[:, j, pp, lo:512], in_=ps[pmn[j]][:, lo:512], func=AF.Copy),
                                 r=[("ps", pmn[j]), ("MNb", sl, j)], w=[("MNb", sl, j)])
                            for hh in range(2):
                                Mc[j, hh] = MNb[:, j, pp, hh * 128:(hh + 1) * 128]
                                Nc[j, hh] = MNb[:, j, pp, 256 + hh * 128:256 + (hh + 1) * 128]
                        yield
                        pz = nb()
                        for j in range(nch):
                            zo = j * 256
                            for hh in range(2):
                                _mm(s, ps[pz][:, zo + hh * 128:zo + (hh + 1) * 128], Nc[j, hh], Zc[j, hh], True, True,
                                    r=[("MNb", sl, j), ("Zb", sl, j)], w=[("ps", pz)], inc=(hh == 1))
                        for j in range(nch):
                            zo = j * 256
                            s.op("dve", lambda e: e.tensor_tensor(out=Zb[:, j, pp, :], in0=ps[pz][:, zo:zo + 256], in1=Zb[:, j, 1 - pp, :], op=ALU.add),
                                 r=[("ps", pz), ("Zb", sl, j)], w=[("Zb", sl, j)])
                            for hh in range(2):
                                Zc[j, hh] = Zb[:, j, pp, hh * 128:(hh + 1) * 128]
                        yield
                    for j in range(nch):
                        cs = slice(j * 128, (j + 1) * 128)
                        Vh = [KBV[:, j, 256 + hh * 64:256 + (hh + 1) * 64] for hh in range(2)]
                        p1 = nb()
                        _mm(s, ps[p1][:, 0:128], AR[:, j, 0:128], Sb[:, :], True, False, r=[T_("AR"), T_("Sb")], w=[("ps", p1)], inc=False)
                        for hh in range(2):
                            _mm(s, ps[p1][:, hh * 64:(hh + 1) * 64], MM[hh][:, j, 256:384], Vh[hh], False, hh == 1, r=[("MM", sl, hh), T_("KBV")], w=[("ps", p1)],
                                inc=(hh == 1))
                        s.op("act", lambda e: e.activation(out=RH[:, :], in_=ps[p1][:, 0:128], func=AF.Copy), r=[("ps", p1), T_("RH")], w=[T_("RH")])
                        yield
                        p2 = nb()
                        for hh in range(2):
                            _mm(s, ps[p2][:, hh * 64:(hh + 1) * 64], Zc[j, hh], RH[:, hh * 64:(hh + 1) * 64], True, True,
                                r=[("Zb", sl, j), T_("RH")], w=[("ps", p2)], inc=(hh == 1))
                        s.op("act", lambda e: e.activation(out=UU[:, :], in_=ps[p2][:, 0:128], func=AF.Copy), r=[("ps", p2), T_("UU")], w=[T_("UU")])
                        yield
                        if ti != 4:
                            _mm(s, ps[YB][:, cs], Sb[:, :], AR[:, j, 128:256], True, False, r=[T_("Sb"), T_("AR")], w=[("ps", YB)], inc=False)
                            for hh in range(2):
                                R = slice(hh * 64, hh * 64 + 64)
                                _mm(s, ps[YB][R, cs], UU[:, hh * 64:(hh + 1) * 64], MM[hh][:, j, 128:256], False, False, r=[T_("UU"), ("MM", sl, hh)], w=[("ps", YB)], inc=False)
                                _mm(s, ps[YB][R, cs], Vh[hh], MM[hh][:, j, 384:512], False, True, r=[T_("KBV"), ("MM", sl, hh)], w=[("ps", YB)], inc=(hh == 1))
                        p3 = nb()
                        for hh in range(2):
                            R = slice(hh * 64, hh * 64 + 64)
                            _mm(s, ps[p3][R, 0:64], KBV[:, j, 128 + hh * 64:128 + (hh + 1) * 64], UU[:, hh * 64:(hh + 1) * 64], True, False,
                                r=[T_("KBV"), T_("UU")], w=[("ps", p3)], inc=False)
                            _mm(s, ps[p3][R, 0:64], KBV[:, j, hh * 64:(hh + 1) * 64], Vh[hh], False, True, r=[T_("KBV")], w=[("ps", p3)], inc=(hh == 1))
                        s.op("dve", lambda e: e.scalar_tensor_tensor(out=Sst[:, :], in0=Sst[:, :], scalar=F["ep"][:, j * 128 + 127:j * 128 + 128],
                                                                   in1=ps[p3][:, 0:64], op0=ALU.mult, op1=ALU.add),
                             r=[("ps", p3), T_("Sst"), fT("ep")], w=[T_("Sst")])
                        for hh in range(2):
                            R = slice(hh * 64, hh * 64 + 64)
                            s.op("act", lambda e: e.activation(out=Sb[R, hh * 64:(hh + 1) * 64], in_=Sst[R, :], func=AF.Copy), r=[T_("Sst"), T_("Sb")], w=[T_("Sb")])
                        yield
                    if ti == 4:
                        continue
                    s.op("act", lambda e: e.activation(out=F["y"][:, 0:n], in_=ps[YB][:, 0:n], func=AF.Copy), r=[("ps", YB), fT("y")], w=[fT("y")])
                    s.op("act", lambda e: e.activation(out=yb[:, 0:n], in_=ps[YB][:, 0:n], func=AF.Copy), r=[("ps", YB), T_("yb")], w=[T_("yb")])
                    pb = nb()
                    _mm(s, ps[pb][:, 0:n], bo64_b[:], yb[:, 0:n], True, True, r=[T_("yb")], w=[("ps", pb)])
                    s.op("dve", lambda e: e.tensor_tensor(out=F["yc"][:, 0:n], in0=F["y"][:, 0:n], in1=ps[pb][:, 0:n], op=ALU.subtract),
                         r=[fT("y"), ("ps", pb)], w=[fT("yc")])
                    s.op("act", lambda e: e.activation(out=yb[:, 0:n], in_=F["yc"][:, 0:n], func=AF.Square), r=[fT("yc"), T_("yb")], w=[T_("yb")])
                    pb = nb()
                    _mm(s, ps[pb][:, 0:n], bo64_b[:], yb[:, 0:n], True, True, r=[T_("yb")], w=[("ps", pb)])
                    s.op("act", lambda e: e.activation(out=F["y"][:, 0:n], in_=ps[pb][:, 0:n], func=AF.Sqrt, bias=GN_EPS), r=[("ps", pb), fT("y")], w=[fT("y")])
                    s.op("dve", lambda e: e.reciprocal(out=F["y"][:, 0:n], in_=F["y"][:, 0:n]), r=[fT("y")], w=[fT("y")])
                    s.op("dve", lambda e: e.tensor_tensor(out=F["yc"][:, 0:n], in0=F["yc"][:, 0:n], in1=F["y"][:, 0:n], op=ALU.mult),
                         r=[fT("yc"), fT("y")], w=[fT("yc")])
                    s.op("dve", lambda e: e.tensor_scalar(F["yc"][:, 0:n], F["yc"][:, 0:n], vcol("lnx_w", c), vcol("lnx_b", c), ALU.mult, ALU.add),
                         r=[fT("yc")], w=[fT("yc")])
                    yield
                    s.op("pool", lambda e: e.tensor_tensor(out=F["t"][:, 0:n], in0=F["r"][:, 0:n], in1=F["kmod"][:, 0:n], op=ALU.mult),
                         r=[fT("r"), fT("kmod"), fT("t")], w=[fT("t")])
                    s.op("dve", lambda e: e.tensor_scalar_mul(yb[:, 0:n], F["t"][:, 0:n], vcol("r_k", c)), r=[fT("t"), T_("yb")], w=[T_("yb")])
                    pb = nb()
                    _mm(s, ps[pb][:, 0:n], bo_b[:], yb[:, 0:n], True, True, r=[T_("yb")], w=[("ps", pb)])
                    s.op("dve", lambda e: e.tensor_tensor(out=F["y"][:, 0:n], in0=ps[pb][:, 0:n], in1=F["v"][:, 0:n], op=ALU.mult),
                         r=[("ps", pb), fT("v"), fT("y")], w=[fT("y")])
                    s.op("dve", lambda e: e.tensor_tensor(out=F["yc"][:, 0:n], in0=F["yc"][:, 0:n], in1=F["y"][:, 0:n], op=ALU.add),
                         r=[fT("yc"), fT("y")], w=[fT("yc")])
                    s.op("dve", lambda e: e.tensor_tensor(out=rwo_sl(ti, c, off, off + n), in0=F["yc"][:, 0:n], in1=F["g"][:, 0:n], op=ALU.mult),
                         r=[fT("yc"), fT("g")], w=[("rwo", ti, c)])
                    yield

            for c0 in range(0, KC, NSL):
                gens = [m4_pair(c0 + i, i) for i in range(NSL)]
                live = list(gens)
                lead = int(os.environ.get("K_LEAD", "0"))
                for _ in range(lead):
                    try:
                        next(gens[0])
                    except StopIteration:
                        live.remove(gens[0])
                        break
                while live:
                    for g_ in list(live):
                        try:
                            next(g_)
                        except StopIteration:
                            live.remove(g_)
            s.barrier()
        if dbg in ("C", "BC") and sq == 0:
            for (ti, t0, n) in REAL:
                for c in range(KC):
                    s.dma("pool", dbg_t[1, :, c, t0:t0 + n], rwo_sl(ti, c), key="dbg")
            s.barrier()
            return

        with contextlib.ExitStack() as s5:
            def sb5(name, shape, dt):
                return s5.enter_context(_sbt(nc, name, list(shape), dt))
            mg = sb5("mg", [128, KC, SEQ], BF16)
            w5 = [[sb5("w5_%d_%d" % (i, j), [128, KC, 128], BF16) for j in range(4)] for i in range(2)]
            sga = [sb5("sga%d" % i, [128, 512], F32) for i in range(2)]
            sgr = [sb5("sgr%d" % i, [128, 512], F32) for i in range(2)]
            WA, WR, WO = D["w_attn_branch"][0], D["w_rwkv_branch"][0], D["w_out"][0]
            unit = 0
            for oc in range(KC):
                wsl = oc % 2
                cs = slice(oc * 128, (oc + 1) * 128)
                srcs = [WA[:, cs].rearrange("(k p) f -> p k f", p=128), WR[:, cs].rearrange("(k p) f -> p k f", p=128),
                        wchunk_src(GA0 + oc * 128, 128), wchunk_src(GR0 + oc * 128, 128)]
                for j in range(4):
                    s.dma("pool", w5[wsl][j][:, :, :], srcs[j], w=[("w5", wsl, j)], key=("w5", wsl, j))
                for (ti, t0, n) in REAL:
                    b0 = 4 * (unit % 2)
                    u2 = unit % 2
                    for k in range(KC):
                        _mm(s, ps[b0][:, :], w5[wsl][0][:, k, :], att_sl(ti, k), k == 0, k == KC - 1, r=[("w5", wsl, 0), ("att", ti, k)], w=[("ps", b0)])
                    for k in range(KC):
                        _mm(s, ps[b0 + 1][:, :], w5[wsl][1][:, k, :], rwo_sl(ti, k), k == 0, k == KC - 1, r=[("w5", wsl, 1), ("rwo", ti, k)], w=[("ps", b0 + 1)])
                    proj(w5[wsl][2], ("w5", wsl, 2), t0, n, b0 + 2)
                    proj(w5[wsl][3], ("w5", wsl, 3), t0, n, b0 + 3)
                    s.op("act", lambda e: e.activation(out=sga[u2][:, :], in_=ps[b0 + 2][:, :], func=AF.Sigmoid), r=[("ps", b0 + 2)], w=[("sga", u2)])
                    s.op("act", lambda e: e.activation(out=sgr[u2][:, :], in_=ps[b0 + 3][:, :], func=AF.Sigmoid), r=[("ps", b0 + 3)], w=[("sgr", u2)])
                    s.op("dve", lambda e: e.tensor_tensor(out=sga[u2][:, :], in0=ps[b0][:, :], in1=sga[u2][:, :], op=ALU.mult),
                         r=[("ps", b0), ("sga", u2)], w=[("sga", u2)])
                    s.op("dve", lambda e: e.tensor_tensor(out=sgr[u2][:, :], in0=ps[b0 + 1][:, :], in1=sgr[u2][:, :], op=ALU.mult),
                         r=[("ps", b0 + 1), ("sgr", u2)], w=[("sgr", u2)])
                    s.op("pool", lambda e: e.tensor_tensor(out=mg[:, oc, ti * 512:(ti + 1) * 512], in0=sga[u2][:, :], in1=sgr[u2][:, :], op=ALU.add),
                         r=[("sga", u2), ("sgr", u2)], w=[("mg", ti, oc)])
                    unit += 1
            s.barrier()
            for (ti, t0, n) in REAL:
                s.dma("sp", hall(ti), hspill[:, ti * KC * 512:(ti + 1) * KC * 512].rearrange("p (c t) -> p c t", c=KC),
                      w=[("hT", ti, cc) for cc in range(KC)], key=("hre", ti))
            for oc in range(KC):
                wsl = oc % 2
                s.dma("pool", w5[wsl][0][:, :, :], WO[:, oc * 128:(oc + 1) * 128].rearrange("(k p) f -> p k f", p=128),
                      w=[("w5", wsl, 0)], key=("w5", wsl, 0))
                for (ti, t0, n) in REAL:
                    pb = unit % 4
                    for k in range(KC):
                        _mm(s, ps[pb][:, :], w5[wsl][0][:, k, :], mg[:, k, ti * 512:(ti + 1) * 512], k == 0, k == KC - 1,
                            r=[("w5", wsl, 0), ("mg", ti, k)], w=[("ps", pb)])
                    s.op("dve", lambda e: e.tensor_tensor(out=hsl(ti, oc), in0=ps[pb][:, :], in1=hsl(ti, oc), op=ALU.add),
                         r=[("ps", pb), ("hT", ti, oc)], w=[("hT", ti, oc)])
                    unit += 1
            s.barrier()

def host_consts():
    c = {}
    c["c_ident"] = np.eye(128, dtype=np.float32)
    p = np.arange(128)
    perm = np.zeros((128, 128), np.float32)
    partner = (p // 64) * 64 + ((p % 64) + 32) % 64
    perm[partner, p] = 1.0
    c["c_perm"] = perm
    i = np.arange(128)[:, None]
    t = np.arange(128)[None, :]
    su = (t > i).astype(np.float32)
    iu = (t >= i).astype(np.float32)
    c["c_mask4"] = np.concatenate([su, iu, su, iu], axis=1)
    sl = (t < i).astype(np.float32)
    c["c_maskl"] = np.concatenate([sl, sl, sl, sl], axis=1)
    meta = np.broadcast_to((i >= 112), (128, 128)).astype(np.float32)
    prev = (i > t).astype(np.float32)
    cur = (i <= t).astype(np.float32)
    c["c_mask3"] = np.concatenate([meta, prev, cur], axis=1)
    c["c_mask3f"] = np.concatenate([meta, np.zeros_like(prev), cur], axis=1)
    bo = np.zeros((128, 128), np.float32)
    bo[:64, :64] = 1.0
    bo[64:, 64:] = 1.0
    c["c_bo"] = bo
    rs = np.ones((128, 512), np.float32)
    rs[:, 0::128] = 0.0
    c["c_reset"] = rs
    half = 32
    inv = (10000.0 ** (-np.arange(half, dtype=np.float32) / half)).astype(np.float32)
    pos = (np.arange(T) - 112).astype(np.float32)
    ang = pos[None, :] * inv[(p % 32)][:, None]
    c["c_cos"] = np.cos(ang).astype(np.float32)
    sgn = np.where((p % 64) < 32, -1.0, 1.0).astype(np.float32)[:, None]
    c["c_sin"] = (np.sin(ang) * sgn).astype(np.float32)
    return c


CONST_SHAPES = {"c_ident": (128, 128), "c_perm": (128, 128), "c_mask4": (128, 512), "c_maskl": (128, 512),
                "c_mask3": (128, 384), "c_mask3f": (128, 384), "c_bo": (128, 128), "c_reset": (128, 512),
                "c_cos": (128, T), "c_sin": (128, T)}


def build_nc(dbg=None, nseq=SPC, skip=False):
    nc = bass.Bass("TRN2", target_bir_lowering=False)
    D = {}

    def inp(name, shape):
        D[name] = nc.dram_tensor(name, list(shape), F32, kind="ExternalInput").ap()

    inp("x", (SPC, SEQ, DM))
    inp("meta_tokens", (16, DM))
    inp("norm_ffn1", (1, DM))
    inp("ffn1_w_in", (1, DM, 2 * FF))
    inp("ffn1_w_out", (1, FF, DM))
    inp("norm_mix", (1, DM))
    inp("w_in", (1, DM, INC))
    inp("rwkv_mu", (1, RWC))
    inp("sinks", (1, NQH))
    inp("w0", (1, 1024))
    inp("w2", (1, LW, 1024))
    inp("a0", (1, 1024))
    inp("a2", (1, LA, 1024))
    inp("g2", (1, LG, 1024))
    inp("k_k", (1, 1024))
    inp("k_a", (1, 1024))
    inp("r_k", (1, 16, 64))
    inp("lnx_w", (1, 1024))
    inp("lnx_b", (1, 1024))
    inp("w_attn_branch", (1, 1024, DM))
    inp("w_rwkv_branch", (1, 1024, DM))
    inp("w_out", (1, DM, DM))
    inp("norm_ffn2", (1, DM))
    inp("ffn2_w_in", (1, DM, 2 * FF))
    inp("ffn2_w_out", (1, FF, DM))
    inp("norm_final", (DM,))
    for k, shp in CONST_SHAPES.items():
        inp(k, shp)
    out = nc.dram_tensor("out", [SPC, SEQ, DM], F32, kind="ExternalOutput").ap()
    hspill = nc.dram_tensor("hspill", [128, 4 * KC * 512], F32, kind="Internal").ap()
    if dbg:
        dbg_t = nc.dram_tensor("dbg", [2, 128, KC, T], F32, kind="ExternalOutput").ap()

    W_IN = D["w_in"][0]

    with contextlib.ExitStack() as st:
        s = Sched(nc, st)

        def sb(name, shape, dt):
            return st.enter_context(_sbt(nc, name, list(shape), dt))

        ident = sb("ident", [128, 128], F32)
        ones_b = sb("ones_b", [128, 128], BF16)
        perm_b = sb("perm_b", [128, 128], BF16)
        bo_b = sb("bo_b", [128, 128], BF16)
        bo64_b = sb("bo64_b", [128, 128], BF16)
        mask4 = sb("mask4", [128, 512], BF16)
        maskl = sb("maskl", [128, 512], BF16)
        mask3 = sb("mask3", [128, 384], BF16)
        mask3f = sb("mask3f", [128, 384], BF16)
        resetm = sb("resetm", [128, 512], F32)
        vecs = sb("vecs", [128, 128], F32)
        vecs2 = sb("vecs2", [128, 32], F32)
        vstage = sb("vstage", [128, 128], F32)
        ps = [st.enter_context(nc.psum_tensor("ps%d" % i, [128, 512], F32)) for i in range(8)]

        s.dma("sp", ident[:], D["c_ident"][:, :], key="c0")
        s.dma("sp", resetm[:], D["c_reset"][:, :], key="c0")
        s.dma("pool", perm_b[:], D["c_perm"][:, :], key="c1")
        s.dma("pool", bo_b[:], D["c_bo"][:, :], key="c1")
        s.dma("pool", mask4[:], D["c_mask4"][:, :], key="c1")
        s.dma("pool", maskl[:], D["c_maskl"][:, :], key="c1")
        s.dma("pool", mask3[:], D["c_mask3"][:, :], key="c1")
        s.dma("pool", mask3f[:], D["c_mask3f"][:, :], key="c1")
        s.op("dve", lambda e: e.memset(ones_b[:], 1.0))
        s.op("dve", lambda e: e.memset(vstage[:], 0.0))
        s.barrier()
        VROW = {}
        row = 0
        for nm in ("norm_ffn1", "norm_mix", "norm_ffn2", "norm_final", "w0", "a0", "k_k", "k_a",
                   "r_k", "lnx_w", "lnx_b"):
            ap = D[nm]
            if nm == "norm_final":
                src = ap.rearrange("(c p) -> c p", p=128)
            elif nm == "r_k":
                src = ap[0].rearrange("(c h) k -> c (h k)", h=2)
            else:
                src = ap[0].rearrange("(c p) -> c p", p=128)
            s.dma("sp", vstage[row:row + 8, :], src, key="c0")
            VROW[nm] = row
            row += 8
        mu = D["rwkv_mu"][0]
        s.dma("sp", vstage[row:row + 26, :], mu[0:26 * 128].rearrange("(c p) -> c p", p=128), key="c0")
        s.dma("sp", vstage[row + 26:row + 27, 0:32], mu[26 * 128:RWC].rearrange("(c p) -> c p", p=32), key="c0")
        VROW["mu"] = row
        row += 27
        assert row <= 128
        sk = D["sinks"][0].rearrange("(c h) -> h c", h=2)
        import os
        for hf in range(2):
            if os.environ.get("K_SKIPC"):
                continue
            s.dma("sp", vecs2[hf * 64:(hf + 1) * 64, 8:16], sk[hf:hf + 1, :].broadcast_to([64, 8]), key="c0",
                  allow_slow_non_contiguous=True)
        s.barrier()
        s.op("pe", lambda e: e.transpose(ps[0][:, 0:128], vstage[:], ident[:]))
        s.barrier()
        s.op("dve", lambda e: e.tensor_copy(out=vecs[:], in_=ps[0][:, 0:128]))
        s.op("act", lambda e: e.activation(out=vecs2[:, 8:16], in_=vecs2[:, 8:16], func=AF.Exp))
        s.op("act", lambda e: e.mul(out=bo64_b[:], in_=bo_b[:], mul=1.0 / 64))
        s.barrier()

        def vcol(nm, c):
            j = VROW[nm] + c
            return vecs[:, j:j + 1]

        s.op("dve", lambda e: e.tensor_scalar(vecs2[:, 0:8], vecs[:, VROW["k_a"]:VROW["k_a"] + 8], -1.0, 1.0, ALU.mult, ALU.add))
        s.barrier()

        hTf = sb("hTf", [128, 4 * KC * 512 + KC * 128], F32)
        hTb = hTf.bitcast(BF16)
        xnT = sb("xnT", [128, KC, T], BF16)

        REAL = [(i, 128 + 512 * i, 512) for i in range(4)]
        META = (4, 0, 128)
        ALLT = REAL + [META]

        def hsl(ti, c, a=0, b=None):
            if ti == 4:
                base = 4 * KC * 512 + c * 128
                n = 128
            else:
                base = (ti * KC + c) * 512
                n = 512
            if b is None:
                b = n
            return hTf[:, base + a:base + b]

        def hall(ti):
            if ti == 4:
                return hTf[:, 4 * KC * 512:4 * KC * 512 + KC * 128].rearrange("p (c t) -> p c t", c=KC)
            return hTf[:, ti * KC * 512:(ti + 1) * KC * 512].rearrange("p (c t) -> p c t", c=KC)

        def att_sl(ti, c, a=0, b=512):
            base = ti * 8192 + c * 512
            return hTb[:, base + a:base + b]

        def rwo_sl(ti, c, a=0, b=512):
            base = ti * 8192 + 4096 + c * 512
            return hTb[:, base + a:base + b]

        def hT_toks(ti):
            return [("hT", ti, c) for c in range(KC)]

        for sq in range(nseq):
            with contextlib.ExitStack() as st1:
                xs = [st1.enter_context(_sbt(nc, "xs%d" % i, [128, DM], F32)) for i in range(2)]
                import os
                for n in range(NBLK):
                    if os.environ.get("K_SKIPA"):
                        continue
                    slot = n % 2
                    xt = xs[slot]
                    if n == 0:
                        s.op("dve", lambda e: e.memset(xt[:], 0.0), w=[("xs", slot)])
                        s.dma("sp", xt[112:128, :], D["meta_tokens"][:, :], w=[("xs", slot)], key=("xs", slot))
                        ti, off = 4, 0
                    else:
                        s.dma("sp", xt[:], D["x"][sq, (n - 1) * 128:n * 128, :], w=[("xs", slot)], key=("xs", slot))
                        ti, off = (n - 1) // 4, ((n - 1) % 4) * 128
                    for half in range(2):
                        pb = (2 * n + half) % 4
                        for j in range(4):
                            c = half * 4 + j
                            s.op("pe", lambda e: e.transpose(ps[pb][:, j * 128:(j + 1) * 128], xt[:, c * 128:(c + 1) * 128], ident[:]),
                                 r=[("xs", slot)], w=[("ps", pb)], inc=(j == 3))
                        s.op("act", lambda e: e.activation(out=hall(ti)[:, half * 4:half * 4 + 4, off:off + 128],
                                                           in_=ps[pb][:].rearrange("p (j t) -> p j t", j=4), func=AF.Copy),
                             r=[("ps", pb)], w=[("hT", ti, half * 4 + jj) for jj in range(4)])
                s.barrier()

            def rmsnorm_tiles(gname, tiles, bufs, write):
                sqb, rstd = bufs
                for (ti, t0, n) in tiles:
                    s.op("act", lambda e: e.activation(out=sqb[:, :, 0:n], in_=hall(ti), func=AF.Square),
                         r=hT_toks(ti), w=["sqb"])
                    for c in range(KC):
                        _mm(s, ps[0][:, 0:n], ones_b[:], sqb[:, c, 0:n], c == 0, c == KC - 1, r=["sqb"], w=[("ps", 0)])
                    s.op("act", lambda e: e.activation(out=rstd[:, 0:n], in_=ps[0][:, 0:n], func=AF.Sqrt, bias=RMS_EPS, scale=1.0 / DM),
                         r=[("ps", 0)], w=["rstd"])
                    s.op("dve", lambda e: e.reciprocal(out=rstd[:, 0:n], in_=rstd[:, 0:n]), r=["rstd"], w=["rstd"])
                    for c in range(KC):
                        write(ti, t0, n, c, rstd, gname)

            def write_xn(ti, t0, n, c, rstd, gname):
                s.op("dve", lambda e: e.scalar_tensor_tensor(out=xnT[:, c, t0:t0 + n], in0=hsl(ti, c), scalar=vcol(gname, c),
                                                           in1=rstd[:, 0:n], op0=ALU.mult, op1=ALU.mult),
                     r=[("hT", ti, c), "rstd"], w=[("xn", ti, c)])

            def ffn(w_in_ap, w_out_ap, tiles):
                NF = 4
                blocks = []
                f0 = 0
                while f0 < FC:
                    nf = min(NF, FC - f0)
                    blocks.append((f0, nf))
                    f0 += nf
                with contextlib.ExitStack() as st2:
                    wg = [st2.enter_context(_sbt(nc, "wg%d" % i, [128, KC, NF * 128], BF16)) for i in range(2)]
                    wu = [st2.enter_context(_sbt(nc, "wu%d" % i, [128, KC, NF * 128], BF16)) for i in range(2)]
                    wd = [st2.enter_context(_sbt(nc, "wd%d" % i, [128, NF, DM], BF16)) for i in range(2)]
                    h1 = [st2.enter_context(_sbt(nc, "h1%d" % i, [128, NF, 512], BF16)) for i in range(2)]
                    sg = [st2.enter_context(_sbt(nc, "sg%d" % i, [128, 512], BF16)) for i in range(2)]
                    pending = [None]
                    unit = 0
                    for bi, (f0, nf) in enumerate(blocks):
                        ws = bi % 2
                        s.dma("pool", wg[ws][:, :, 0:nf * 128],
                              w_in_ap[:, f0 * 128:(f0 + nf) * 128].rearrange("(k p) f -> p k f", p=128),
                              w=[("wg", ws)], key=("wg", ws))
                        s.dma("pool", wu[ws][:, :, 0:nf * 128],
                              w_in_ap[:, FF + f0 * 128:FF + (f0 + nf) * 128].rearrange("(k p) f -> p k f", p=128),
                              w=[("wu", ws)], key=("wu", ws))
                        s.dma("pool", wd[ws][:, 0:nf, :],
                              w_out_ap[f0 * 128:(f0 + nf) * 128, :].rearrange("(c p) d -> p c d", p=128),
                              w=[("wd", ws)], key=("wd", ws))
                        for (ti, t0, n) in tiles:
                            hs = unit % 2
                            for j in range(nf):
                                pa = 2 * (j % 2)
                                for k in range(KC):
                                    _mm(s, ps[pa][:, 0:n], wg[ws][:, k, j * 128:(j + 1) * 128], xnT[:, k, t0:t0 + n],
                                        k == 0, k == KC - 1, r=[("wg", ws), ("xn", ti, k)], w=[("ps", pa)])
                                for k in range(KC):
                                    _mm(s, ps[pa + 1][:, 0:n], wu[ws][:, k, j * 128:(j + 1) * 128], xnT[:, k, t0:t0 + n],
                                        k == 0, k == KC - 1, r=[("wu", ws), ("xn", ti, k)], w=[("ps", pa + 1)])
                                sgj = sg[j % 2]
                                s.op("act", lambda e: e.activation(out=sgj[:, 0:n], in_=ps[pa][:, 0:n], func=AF.Silu),
                                     r=[("ps", pa)], w=[("sg", j % 2)])
                                s.op("dve", lambda e: e.tensor_tensor(out=h1[hs][:, j, 0:n], in0=ps[pa + 1][:, 0:n], in1=sgj[:, 0:n], op=ALU.mult),
                                     r=[("ps", pa + 1), ("sg", j % 2)], w=[("h1", hs)])

                            def down(ws=ws, hs=hs, nf=nf, ti=ti, n=n):
                                for dc in range(KC):
                                    pb = 4 + dc % 4
                                    for j in range(nf):
                                        _mm(s, ps[pb][:, 0:n], wd[ws][:, j, dc * 128:(dc + 1) * 128], h1[hs][:, j, 0:n],
                                            j == 0, j == nf - 1, r=[("wd", ws), ("h1", hs)], w=[("ps", pb)])
                                    s.op("dve", lambda e: e.scalar_tensor_tensor(out=hsl(ti, dc), in0=ps[pb][:, 0:n], scalar=0.5,
                                                                               in1=hsl(ti, dc), op0=ALU.mult, op1=ALU.add),
                                         r=[("ps", pb), ("hT", ti, dc)], w=[("hT", ti, dc)])
                            if pending[0] is not None:
                                pending[0]()
                            pending[0] = down
                            unit += 1
                    pending[0]()
                    s.barrier()

            with contextlib.ExitStack() as st3:
                bufs = (st3.enter_context(_sbt(nc, "sqb", [128, KC, 512], BF16)),
                        st3.enter_context(_sbt(nc, "rstd", [128, 512], F32)))
                if not os.environ.get("K_SKIPA"):
                    rmsnorm_tiles("norm_ffn1", ALLT, bufs, write_xn)
                if not skip:
                    ffn(D["ffn1_w_in"][0], D["ffn1_w_out"][0], ALLT)
                if not os.environ.get("K_SKIPA"):
                    rmsnorm_tiles("norm_mix", ALLT, bufs, write_xn)
                s.barrier()
            if dbg == "A" and sq == 0:
                for (ti, t0, n) in ALLT:
                    s.dma("sp", dbg_t[0, :, :, t0:t0 + n], hall(ti), key="dbg")
                s.barrier()
                break
            for ti_ in range(4):
                if os.environ.get("K_SKIPSP"):
                    continue
                s.dma("sp", hspill[:, ti_ * 4096:(ti_ + 1) * 4096], hTf[:, ti_ * 4096:(ti_ + 1) * 4096], key=("spill", ti_))
            s.barrier()

            uT = xnT
            mixer(nc, s, D, W_IN, ps, uT, att_sl, rwo_sl, vcol, vecs, vecs2, VROW,
                  dict(ident=ident, ones_b=ones_b, perm_b=perm_b, bo_b=bo_b, bo64_b=bo64_b, mask4=mask4, maskl=maskl,
                       mask3=mask3, mask3f=mask3f, resetm=resetm), dbg, sq, dbg_t if dbg else None, REAL, hsl, hall, hspill)
            if dbg in ("B", "C", "K", "BC", "M1", "M2", "M2b", "M2a") and sq == 0:
                break

            with contextlib.ExitStack() as st3:
                bufs = (st3.enter_context(_sbt(nc, "sqb", [128, KC, 512], BF16)),
                        st3.enter_context(_sbt(nc, "rstd", [128, 512], F32)))
                rmsnorm_tiles("norm_ffn2", REAL, bufs, write_xn)
                ffn(D["ffn2_w_in"][0], D["ffn2_w_out"][0], REAL)
                ot = [st3.enter_context(_sbt(nc, "ot%d" % i, [128, DM], F32)) for i in range(2)]
                yf = st3.enter_context(_sbt(nc, "yf", [128, KC, 512], F32))

                def write_y(ti, t0, n, c, rstd, gname):
                    s.op("dve", lambda e: e.scalar_tensor_tensor(out=yf[:, c, 0:n], in0=hsl(ti, c), scalar=vcol(gname, c),
                                                               in1=rstd[:, 0:n], op0=ALU.mult, op1=ALU.mult),
                         r=[("hT", ti, c), "rstd"], w=[("yf", c)])
                blkc = 0
                for tile in REAL:
                    rmsnorm_tiles("norm_final", [tile], bufs, write_y)
                    ti, t0, n = tile
                    for b4 in range(4):
                        osl = blkc % 2
                        for half in range(2):
                            pb = 4 + (2 * blkc + half) % 4
                            for j in range(4):
                                c = half * 4 + j
                                s.op("pe", lambda e: e.transpose(ps[pb][:, j * 128:(j + 1) * 128], yf[:, c, b4 * 128:(b4 + 1) * 128], ident[:]),
                                     r=[("yf", c)], w=[("ps", pb)], inc=(j == 3))
                            s.op("act", lambda e: e.activation(out=ot[osl][:, half * 512:(half + 1) * 512], in_=ps[pb][:], func=AF.Copy),
                                 r=[("ps", pb)], w=[("ot", osl)])
                        blk = (t0 - 128) // 128 + b4
                        s.dma("sp", out[sq, blk * 128:(blk + 1) * 128, :], ot[osl][:], r=[("ot", osl)], key=("ot", osl))
                        blkc += 1
                s.barrier()
        s.finish()
    return nc


def kernel(**inputs):
    dbg = inputs.pop("_dbg", None)
    x = np.ascontiguousarray(np.asarray(inputs["x"], dtype=np.float32))
    nc = build_nc(dbg)
    common = {k: np.ascontiguousarray(np.asarray(v, dtype=np.float32)) for k, v in inputs.items() if k != "x"}
    common.update(host_consts())
    in_maps = []
    for c in range(NCORES):
        m = dict(common)
        m["x"] = x[c * SPC:(c + 1) * SPC]
        in_maps.append(m)
    res = run_bass_kernel_spmd(nc, in_maps, core_ids=list(range(NCORES)))
    if dbg:
        return [r["dbg"] for r in res.results]
    return np.concatenate([r["out"] for r in res.results], axis=0)
```

```python
import contextlib
import numpy as np
import concourse.bass as bass
import concourse.mybir as mybir
from concourse.bass_utils import run_bass_kernel_spmd

F32 = mybir.dt.float32
BF16 = mybir.dt.bfloat16
AF = mybir.ActivationFunctionType
ALU = mybir.AluOpType

NCORES = 8
B, SEQ, DM = 16, 2048, 1024
SPC = B // NCORES
T = SEQ + 128
NBLK = T // 128
KC = DM // 128
FF = 2816
FC = FF // 128
NQH, NKVH, HD = 16, 2, 64
LW, LA, LG = 64, 64, 160
RWC = 3 * 1024 + LW + LA + LG
INC = 1024 + 256 + RWC + 2048
RMS_EPS = 1e-6
GN_EPS = 64e-5


_SBN = [0]


def _sbt(nc, name, shape, dt):
    _SBN[0] += 1
    return nc.sbuf_tensor("%s_u%d" % (name, _SBN[0]), list(shape), dt)


class Tok:
    __slots__ = ("w", "r")

    def __init__(self):
        self.w = None
        self.r = {}


class Sched:
    def __init__(self, nc, stack):
        self.nc = nc
        self.stack = stack
        self.E = {"pe": nc.tensor, "act": nc.scalar, "dve": nc.vector, "pool": nc.gpsimd, "sp": nc.sync}
        self.sem = {k: stack.enter_context(nc.semaphore("s_" + k)) for k in ("pe", "act", "dve", "pool")}
        self.cnt = {k: 0 for k in self.E}
        self.seen = {k: {} for k in self.E}
        self.toks = {}
        self.dsem = {}
        self.dcnt = {}
        self.nwaits = 0

    def tok(self, key):
        t = self.toks.get(key)
        if t is None:
            t = self.toks[key] = Tok()
        return t

    def _semof(self, e):
        if isinstance(e, tuple):
            return self.dsem[e[1]], 16
        return self.sem[e], 1

    def _need(self, r, w):
        need = {}
        for k in r:
            t = self.tok(k)
            if t.w is not None:
                if need.get(t.w[0], 0) < t.w[1]:
                    need[t.w[0]] = t.w[1]
        for k in w:
            t = self.tok(k)
            if t.w is not None:
                if need.get(t.w[0], 0) < t.w[1]:
                    need[t.w[0]] = t.w[1]
            for e2, n in t.r.items():
                if need.get(e2, 0) < n:
                    need[e2] = n
        return need

    def _waits(self, eng, need, myidx, is_dma=False):
        E = self.E[eng]
        seen = self.seen[eng]
        for e2, n in need.items():
            if e2 == eng and not is_dma:
                if eng == "pe" or (eng != "pool" and n < myidx - 3):
                    continue
            if n <= seen.get(e2, 0):
                continue
            seen[e2] = n
            sem, mult = self._semof(e2)
            E.wait_ge(sem, n * mult)
            self.nwaits += 1

    def op(self, eng, fn, r=(), w=(), inc=True):
        psr = [k for k in r if isinstance(k, tuple) and k[0] == "ps"]
        if psr:
            w = list(w) + [k for k in psr if k not in w]
        idx = self.cnt[eng] + 1
        need = self._need(r, w)
        self._waits(eng, need, idx)
        ins = fn(self.E[eng])
        if inc:
            ins.then_inc(self.sem[eng], 1)
            self.cnt[eng] = idx
        for k in r:
            t = self.tok(k)
            if t.r.get(eng, 0) < idx:
                t.r[eng] = idx
        for k in w:
            t = self.tok(k)
            t.w = (eng, idx)
            t.r = {}
        return ins

    def dma(self, q, out, in_, r=(), w=(), key=None, **kw):
        if key not in self.dsem:
            self.dsem[key] = self.stack.enter_context(self.nc.semaphore("d_%d" % len(self.dsem)))
            self.dcnt[key] = 0
        need = self._need(r, w)
        self._waits(q, need, 0, is_dma=True)
        self.dcnt[key] += 1
        k = self.dcnt[key]
        pe = ("dma", key)
        ins = self.E[q].dma_start(out=out, in_=in_, **kw)
        ins.then_inc(self.dsem[key], 16)
        for kk in r:
            self.tok(kk).r[pe] = k
        for kk in w:
            t = self.tok(kk)
            t.w = (pe, k)
            t.r = {}
        return ins

    def barrier(self):
        for e in ("pe", "act", "dve", "pool", "sp"):
            need = {e2: self.cnt[e2] for e2 in ("pe", "act", "dve", "pool") if self.cnt[e2] > 0}
            for key, k in self.dcnt.items():
                need[("dma", key)] = k
            self._waits(e, need, 1 << 60, is_dma=True)

    def finish(self):
        self.barrier()


def _mm(s, out, lhsT, rhs, start, stop, r, w, inc=None):
    if inc is None:
        inc = stop
    return s.op("pe", lambda e: e.matmul(out, lhsT=lhsT, rhs=rhs, start=start, stop=stop), r=r, w=w, inc=inc)


def token_tiles(t0, t1, n=512):
    out = []
    while t0 < t1:
        m = min(n, t1 - t0)
        out.append((t0, m))
        t0 += m
    return out


Q0, K0, V0, R0, RK0, RV0, WD0, GD0, GA0, GR0 = 0, 1024, 1152, 1280, 2304, 3328, 4352, 4480, 4640, 5664
LWC = -0.6065306597126334


def mixer(nc, s, D, W_IN, ps, uT, att_sl, rwo_sl, vcol, vecs, vecs2, VROW, C, dbg, sq, dbg_t, REAL, hsl, hall, hspill):
    ident, ones_b, perm_b, bo_b, bo64_b = C["ident"], C["ones_b"], C["perm_b"], C["bo_b"], C["bo64_b"]
    mask4, maskl, mask3, mask3f, resetm = C["mask4"], C["maskl"], C["mask3"], C["mask3f"], C["resetm"]
    RT = [(4, 0, 128)] + list(REAL)
    MU = VROW["mu"]

    def mucol(j):
        return vecs[:, MU + j:MU + j + 1]

    def wchunk_src(col0, ncols):
        return W_IN[:, col0:col0 + ncols].rearrange("(k p) f -> p k f", p=128)

    def proj(wt, wtok, t0, n, pb, m=128):
        for k in range(KC):
            _mm(s, ps[pb][0:m, 0:n], wt[:, k, 0:m], uT[:, k, t0:t0 + n], k == 0, k == KC - 1, r=[wtok], w=[("ps", pb)])

    with contextlib.ExitStack() as sm:
        def sbm(name, shape, dt):
            return sm.enter_context(_sbt(nc, name, list(shape), dt))

        la = sbm("la", [128, T], BF16)
        lg1 = sbm("lg1", [128, T], BF16)
        lg2 = sbm("lg2", [128, T], BF16)
        identb = sbm("identb", [128, 128], BF16)
        s.op("act", lambda e: e.activation(out=identb[:], in_=ident[:], func=AF.Copy), w=["identb"])

        import os
        with contextlib.ExitStack() as s1:
            wl = [s1.enter_context(_sbt(nc, "wl%d" % i, [128, KC, 128], BF16)) for i in range(3)]
            pt = s1.enter_context(_sbt(nc, "pt1", [128, 513], F32))
            dtmp = s1.enter_context(_sbt(nc, "dtmp1", [128, 512], F32))
            shf = s1.enter_context(_sbt(nc, "shf1", [128, 512], F32))
            specs = [(WD0, 128, 24, "la"), (GD0, 128, 25, "lg1"), (GD0 + 128, 32, 26, "lg2")]
            s.op("dve", lambda e: e.memset(wl[2][:, :, :], 0.0), w=[("wl", 2)])
            for i, (col0, m, muj, nm) in enumerate(specs):
                s.dma("pool", wl[i][:, :, 0:m], wchunk_src(col0, m), w=[("wl", i)], key=("wl", i))
            if os.environ.get("K_SKIPM1"):
                specs = []
            for i, (col0, m, muj, nm) in enumerate(specs):
                m = 128
                s.op("dve", lambda e: e.memset(pt[:, 0:1], 0.0), r=["pt"], w=["pt"])
                for (ti, t0, n) in RT:
                    pb = i % 2
                    proj(wl[i], ("wl", i), t0, n, pb, m)
                    s.op("act", lambda e: e.activation(out=pt[0:m, 1:n + 1], in_=ps[pb][0:m, 0:n], func=AF.Copy),
                         r=[("ps", pb), "pt"], w=["pt"])
                    s.op("dve", lambda e: e.tensor_tensor(out=dtmp[0:m, 0:n], in0=pt[0:m, 0:n], in1=pt[0:m, 1:n + 1], op=ALU.subtract),
                         r=["pt"], w=["dtmp"])
                    s.op("dve", lambda e: e.scalar_tensor_tensor(out=shf[0:m, 0:n], in0=dtmp[0:m, 0:n], scalar=vecs[0:m, MU + muj:MU + muj + 1],
                                                               in1=pt[0:m, 1:n + 1], op0=ALU.mult, op1=ALU.add),
                         r=["dtmp", "pt"], w=["shf"])
                    s.op("dve", lambda e: e.tensor_copy(out=pt[0:m, 0:1], in_=pt[0:m, n:n + 1]), r=["pt", "shf"], w=["pt"])
                    if nm == "la":
                        s.op("act", lambda e: e.activation(out=la[0:64, t0:t0 + n], in_=shf[0:64, 0:n], func=AF.Tanh), r=["shf"], w=["la"])
                        s.op("act", lambda e: e.activation(out=la[64:128, t0:t0 + n], in_=shf[64:128, 0:n], func=AF.Copy), r=["shf"], w=["la"])
                    elif nm == "lg1":
                        s.op("act", lambda e: e.activation(out=lg1[:, t0:t0 + n], in_=shf[:, 0:n], func=AF.Sigmoid), r=["shf"], w=["lg1"])
                    else:
                        s.op("act", lambda e: e.activation(out=lg2[:, t0:t0 + n], in_=shf[:, 0:n], func=AF.Sigmoid), r=["shf"], w=["lg2"])
            s.barrier()

        if dbg == "M1":
            return
        with contextlib.ExitStack() as s2:
            def sb2(name, shape, dt):
                return s2.enter_context(_sbt(nc, name, list(shape), dt))
            t1 = sb2("ropet1", [128, 512], F32)
            t2 = sb2("ropet2", [128, 512], F32)
            cosT = sb2("cosT", [128, T], F32)
            sinT = sb2("sinT", [128, T], F32)
            kTd = [sb2("kTd%d" % g, [128, T], BF16) for g in range(2)]
            Vt = sb2("Vt", [128, NBLK, 128], BF16)
            qT = [sb2("qT%d" % i, [128, SEQ], BF16) for i in range(2)]
            wq = [sb2("wq%d" % i, [128, KC, 128], BF16) for i in range(3)]
            pre = [sb2("pre%d" % i, [128, 512], BF16) for i in range(2)]
            Eb = [sb2("Eb%d" % i, [128, 384], BF16) for i in range(4)]
            rd = sb2("rd", [128, 512], F32)
            print("SBUF remaining at M2:", nc.sbuf_bytes_remaining)
            s.dma("sp", cosT[:], D["c_cos"][:, :], w=["cos"], key="cos")
            s.dma("sp", sinT[:], D["c_sin"][:, :], w=["sin"], key="sin")

            def rope_tile(wt, wtok, t0, n, dst, unit):
                import os
                RS = int(os.environ.get("K_RSTEP", "6"))
                pa, pb = 2 * (unit % 2), 2 * (unit % 2) + 1
                pr = pre[unit % 2]
                proj(wt, wtok, t0, n, pa)
                if RS < 2:
                    return
                s.op("act", lambda e: e.activation(out=pr[:, 0:n], in_=ps[pa][:, 0:n], func=AF.Copy), r=[("ps", pa)], w=[("pre", unit % 2)])
                if RS < 3:
                    return
                _mm(s, ps[pb][:, 0:n], perm_b[:], pr[:, 0:n], True, True, r=[("pre", unit % 2)], w=[("ps", pb)])
                if RS < 4:
                    return
                VV = os.environ.get("K_VAR", "")
                if VV == "1":
                    s.op("dve", lambda e: e.tensor_tensor(out=t1[:, 0:n], in0=cosT[:, t0:t0 + n], in1=t2[:, 0:n], op=ALU.mult),
                         r=[("ps", pa), "cos"], w=["t1"])
                elif VV == "2":
                    s.op("dve", lambda e: e.tensor_tensor(out=t1[:, 0:n], in0=t2[:, 0:n], in1=ps[pa][:, 0:n], op=ALU.mult),
                         r=[("ps", pa), "cos"], w=["t1"])
                elif VV == "3":
                    s.op("dve", lambda e: e.tensor_copy(out=t1[:, 0:n], in_=ps[pa][:, 0:n]),
                         r=[("ps", pa), "cos"], w=["t1"])
                elif VV == "4":
                    s.op("dve", lambda e: e.tensor_tensor(out=t1[:, 0:n], in0=cosT[:, t0:t0 + n], in1=ps[pa][:, 0:n], op=ALU.mult),
                         r=[("ps", pa), "cos", ("pre", unit % 2)], w=["t1"])
                else:
                    s.op("dve", lambda e: e.tensor_tensor(out=t1[:, 0:n], in0=cosT[:, t0:t0 + n], in1=ps[pa][:, 0:n], op=ALU.mult),
                         r=[("ps", pa), "cos"], w=["t1"])
                if RS < 5:
                    return
                s.op("dve", lambda e: e.tensor_tensor(out=t2[:, 0:n], in0=sinT[:, t0:t0 + n], in1=ps[pb][:, 0:n], op=ALU.mult),
                     r=[("ps", pb), "sin"], w=["t2"])
                if RS < 6:
                    return
                s.op("dve", lambda e: e.tensor_tensor(out=dst, in0=t1[:, 0:n], in1=t2[:, 0:n], op=ALU.add),
                     r=["t1", "t2"], w=[])

            unit = 0
            for g in range(2):
                wt = wq[g]
                for hf in range(2):
                    s.dma("pool", wt[:, :, hf * 64:(hf + 1) * 64], wchunk_src(K0 + g * 64, 64), w=[("wq", g)], key=("wq", g, hf))
            s.dma("pool", wq[2][:, :, :], wchunk_src(V0, 128), w=[("wq", 2)], key=("wq", 2))
            s.barrier()
            if dbg == "M2a":
                return
            import os
            NR = int(os.environ.get("K_NROPE", "100"))
            for g in range(2):
                for (ti, t0, n) in RT:
                    if unit >= NR or (os.environ.get("K_SKIP128") and n == 128):
                        continue
                    rope_tile(wq[g], ("wq", g), t0, n, kTd[g][:, t0:t0 + n], unit)
                    unit += 1
            if dbg == "M2b":
                s.barrier()
                return
            for blk in range(NBLK):
                pb = 4 + (blk // 4) % 2
                j = blk % 4
                for k in range(KC):
                    _mm(s, ps[pb][:, j * 128:(j + 1) * 128], uT[:, k, blk * 128:(blk + 1) * 128], wq[2][:, k, :], k == 0, k == KC - 1,
                        r=[("wq", 2)], w=[("ps", pb)])
                if j == 3 or blk == NBLK - 1:
                    b0 = blk - j
                    s.op("act", lambda e: e.activation(out=Vt[:, b0:blk + 1, :], in_=ps[pb][:, 0:(j + 1) * 128].rearrange("p (j t) -> p j t", j=j + 1),
                                                       func=AF.Copy), r=[("ps", pb)], w=["Vt"])
            s.barrier()

            if dbg == "M2":
                return
            def rope_steps(c):
                ws = c % 3
                qt = qT[c % 2]
                steps = []

                def load():
                    s.dma("pool", wq[ws][:, :, :], wchunk_src(Q0 + c * 128, 128), w=[("wq", ws)], key=("wq", ws))
                steps.append(load)
                for (ti, t0, n) in REAL:
                    def stepA(t0=t0, n=n):
                        proj(wq[ws], ("wq", ws), t0, n, 6)
                        s.op("act", lambda e: e.activation(out=pre[0][:, 0:n], in_=ps[6][:, 0:n], func=AF.Copy), r=[("ps", 6)], w=[("pre", 0)])
                        _mm(s, ps[7][:, 0:n], perm_b[:], pre[0][:, 0:n], True, True, r=[("pre", 0)], w=[("ps", 7)])

                    def stepB(t0=t0, n=n, ti=ti):
                        s.op("dve", lambda e: e.tensor_tensor(out=t1[:, 0:n], in0=ps[6][:, 0:n], in1=cosT[:, t0:t0 + n], op=ALU.mult),
                             r=[("ps", 6), "cos"], w=["t1"])
                        s.op("dve", lambda e: e.tensor_tensor(out=t2[:, 0:n], in0=ps[7][:, 0:n], in1=sinT[:, t0:t0 + n], op=ALU.mult),
                             r=[("ps", 7), "sin"], w=["t2"])
                        s.op("dve", lambda e: e.tensor_tensor(out=qt[:, t0 - 128:t0 - 128 + n], in0=t1[:, 0:n], in1=t2[:, 0:n], op=ALU.add),
                             r=["t1", "t2"], w=[("qT", c % 2, ti)])
                    steps.append(stepA)
                    steps.append(stepB)
                return steps

            for st_ in rope_steps(0):
                st_()
            for c in range(KC):
                g = c // 4
                qt = qT[c % 2]
                pend = rope_steps(c + 1) if c + 1 < KC else []
                units = [(ti, b4, hh) for (ti, t0, n) in REAL for b4 in range(4) for hh in range(2)]

                def qk(u):
                    ti, b4, hh = units[u]
                    nblk = 1 + 4 * ti + b4
                    qc = (nblk - 1) * 128
                    R0_, R1_ = hh * 64, hh * 64 + 64
                    sbk = u % 4
                    E = Eb[sbk]
                    for j, kb in enumerate((0, nblk - 1, nblk)):
                        _mm(s, ps[sbk][:, j * 128:(j + 1) * 128], kTd[g][R0_:R1_, kb * 128:(kb + 1) * 128], qt[R0_:R1_, qc:qc + 128],
                            True, True, r=[("qT", c % 2, ti)], w=[("ps", sbk)], inc=(j == 2))
                    s.op("act", lambda e: e.activation(out=E[:, :], in_=ps[sbk][:, 0:384], func=AF.Exp, scale=0.125),
                         r=[("ps", sbk)], w=[("E", sbk)])
                    mk = mask3f if nblk == 1 else mask3
                    s.op("dve", lambda e: e.tensor_tensor(out=E[:, :], in0=E[:, :], in1=mk[:, :], op=ALU.mult),
                         r=[("E", sbk)], w=[("E", sbk)])

                def pv(u):
                    ti, b4, hh = units[u]
                    nblk = 1 + 4 * ti + b4
                    R0_, R1_ = hh * 64, hh * 64 + 64
                    sbk = u % 4
                    E = Eb[sbk]
                    for j, kb in enumerate((0, nblk - 1, nblk)):
                        _mm(s, ps[4][R0_:R1_, b4 * 128:(b4 + 1) * 128], Vt[:, kb, g * 64:(g + 1) * 64], E[:, j * 128:(j + 1) * 128],
                            j == 0, j == 2, r=[("E", sbk), "Vt"], w=[("ps", 4)])
                    for j in range(3):
                        _mm(s, ps[5][R0_:R1_, b4 * 128:(b4 + 1) * 128], ones_b[:, 0:64], E[:, j * 128:(j + 1) * 128],
                            j == 0, j == 2, r=[("E", sbk)], w=[("ps", 5)])
                    if b4 == 3 and hh == 1:
                        s.op("dve", lambda e: e.tensor_scalar_add(rd[:, :], ps[5][:, :], vecs2[:, 8 + c:9 + c]),
                             r=[("ps", 5)], w=["rd"])
                        s.op("dve", lambda e: e.reciprocal(out=rd[:, :], in_=rd[:, :]), r=["rd"], w=["rd"])
                        s.op("dve", lambda e: e.tensor_tensor(out=att_sl(ti, c), in0=ps[4][:, :], in1=rd[:, :], op=ALU.mult),
                             r=[("ps", 4), "rd"], w=[("att", ti, c)])

                LA = 3
                NU = len(units)
                for u in range(min(LA, NU)):
                    qk(u)
                for u in range(NU):
                    if u + LA < NU:
                        qk(u + LA)
                    pv(u)
                    if pend and u % 3 == 2:
                        pend.pop(0)()
                while pend:
                    pend.pop(0)()
            s.barrier()
        if dbg in ("B", "BC") and sq == 0:
            for (ti, t0, n) in REAL:
                for c in range(KC):
                    s.dma("pool", dbg_t[0, :, c, t0:t0 + n], att_sl(ti, c), key="dbg")
            s.barrier()
            if dbg == "B":
                return
        if dbg == "K" and sq == 0:
            return

        with contextlib.ExitStack() as s4:
            def sb4(name, shape, dt):
                return s4.enter_context(_sbt(nc, name, list(shape), dt))
            W4 = 256
            NSL = 2
            FN = ("r", "k", "v", "d", "lw", "a", "kk", "kmod", "b", "cw", "ep", "en", "epv", "g")
            ALIAS = {"rn": "d", "t": "d", "kkn": "kk", "kh": "epv", "bh": "en", "y": "lw", "yc": "cw", "ee": "k"}
            SL = []
            for sl in range(NSL):
                Bf = {}
                Bf["wr"] = [sb4("wr%d_%d" % (sl, j), [128, KC, 128], BF16) for j in range(3)]
                Bf["lwt"] = sb4("lwt%d" % sl, [128, 128], BF16)
                Bf["g2a"] = sb4("g2a%d" % sl, [128, 128], BF16)
                Bf["g2b"] = sb4("g2b%d" % sl, [128, 128], BF16)
                s.op("dve", lambda e: e.memset(Bf["g2b"][:, :], 0.0), w=[("g2b", sl)])
                Bf["ptx"] = [sb4("ptx%d_%d" % (sl, i), [128, W4 + 1], F32) for i in range(3)]
                F = {nm: sb4("f%d_%s" % (sl, nm), [128, W4], F32) for nm in FN}
                for a_, b_ in ALIAS.items():
                    F[a_] = F[b_]
                Bf["F"] = F
                Bf["sqk"] = sb4("sqk%d" % sl, [128, W4], BF16)
                Bf["AR"] = sb4("AR%d" % sl, [128, 2, 256], BF16)
                Bf["ktb"] = sb4("ktb%d" % sl, [128, W4], BF16)
                Bf["btb"] = sb4("btb%d" % sl, [128, W4], BF16)
                Bf["KBV"] = sb4("KBV%d" % sl, [128, 2, 384], BF16)
                Bf["MM"] = [sb4("MM%d_%d" % (sl, hh), [128, 2, 512], BF16) for hh in range(2)]
                Bf["NN"] = sb4("NN%d" % sl, [128, 2, W4], BF16)
                Bf["MNb"] = sb4("MNb%d" % sl, [128, 2, 2, 512], BF16)
                Bf["Zb"] = sb4("Zb%d" % sl, [128, 2, 2, 256], BF16)
                Bf["RH"] = sb4("RH%d" % sl, [128, 128], BF16)
                Bf["UU"] = sb4("UU%d" % sl, [128, 128], BF16)
                Bf["Sst"] = sb4("Sst%d" % sl, [128, 64], F32)
                Bf["Sb"] = sb4("Sb%d" % sl, [128, 128], BF16)
                Bf["yb"] = sb4("yb%d" % sl, [128, W4], BF16)
                SL.append(Bf)
            print("SBUF remaining at M4:", nc.sbuf_bytes_remaining)

            def v3(ap, nch):
                return ap.rearrange("p (j t) -> p j t", j=nch)

            RT4 = [(4, 0, 128, None)]
            for (ti, t0, n) in REAL:
                for h_ in range(2):
                    RT4.append((ti, t0 + W4 * h_, W4, W4 * h_))

            def m4_pair(c, sl):
                Bf = SL[sl]
                F = Bf["F"]
                wr, lwt, g2a, g2b, ptx = Bf["wr"], Bf["lwt"], Bf["g2a"], Bf["g2b"], Bf["ptx"]
                sqk, AR, ktb, btb, KBV, MM, NN, MNb, Zb = Bf["sqk"], Bf["AR"], Bf["ktb"], Bf["btb"], Bf["KBV"], Bf["MM"], Bf["NN"], Bf["MNb"], Bf["Zb"]
                RH, UU, Sst, Sb, yb = Bf["RH"], Bf["UU"], Bf["Sst"], Bf["Sb"], Bf["yb"]
                banks = [4 * sl + i for i in range(3)]
                YB = 4 * sl + 3
                rr = [0]

                def nb():
                    rr[0] = (rr[0] + 1) % 3
                    return banks[rr[0]]

                def T_(nm):
                    return (nm, sl)

                def fT(nm):
                    return ("f_" + ALIAS.get(nm, nm), sl)

                for j, col0 in enumerate((R0, RK0, RV0)):
                    s.dma("pool", wr[j][:, :, :], wchunk_src(col0 + c * 128, 128), w=[("wr", sl, j)], key=("wr", sl, j))
                s.dma("pool", lwt[0:64, :], D["w2"][0, :, c * 128:(c + 1) * 128], w=[("lwt", sl)], key=("lwt", sl, 0))
                s.dma("pool", lwt[64:128, :], D["a2"][0, :, c * 128:(c + 1) * 128], w=[("lwt", sl)], key=("lwt", sl, 1))
                s.dma("pool", g2a[:, :], D["g2"][0, 0:128, c * 128:(c + 1) * 128], w=[("g2a", sl)], key=("g2a", sl))
                s.dma("pool", g2b[0:32, :], D["g2"][0, 128:160, c * 128:(c + 1) * 128], w=[("g2b", sl)], key=("g2b", sl))
                s.op("dve", lambda e: e.memset(Sst[:], 0.0), r=[T_("Sst")], w=[T_("Sst")])
                yield
                s.op("dve", lambda e: e.memset(Sb[:], 0.0), r=[T_("Sb")], w=[T_("Sb")])
                yield
                for i in range(3):
                    s.op("dve", lambda e: e.memset(ptx[i][:, 0:1], 0.0), r=[("ptx", sl, i)], w=[("ptx", sl, i)])
                    yield
                yield
                for (ti, t0, n, off) in RT4:
                    nch = n // 128
                    for i, nm in enumerate(("r", "k", "v")):
                        pb = nb()
                        proj(wr[i], ("wr", sl, i), t0, n, pb)
                        yield
                        p_ = ptx[i]
                        s.op("act", lambda e: e.activation(out=p_[:, 1:n + 1], in_=ps[pb][:, 0:n], func=AF.Copy), r=[("ps", pb), ("ptx", sl, i)], w=[("ptx", sl, i)])
                        yield
                        s.op("dve", lambda e: e.tensor_tensor(out=F["d"][:, 0:n], in0=p_[:, 0:n], in1=p_[:, 1:n + 1], op=ALU.subtract),
                             r=[("ptx", sl, i)], w=[fT("d")])
                        yield
                        s.op("dve", lambda e: e.scalar_tensor_tensor(out=F[nm][:, 0:n], in0=F["d"][:, 0:n], scalar=mucol(8 * i + c),
                                                                   in1=p_[:, 1:n + 1], op0=ALU.mult, op1=ALU.add),
                             r=[fT("d"), ("ptx", sl, i)], w=[fT(nm)])
                        yield
                        s.op("dve", lambda e: e.tensor_copy(out=p_[:, 0:1], in_=p_[:, n:n + 1]), r=[("ptx", sl, i), fT(nm)], w=[("ptx", sl, i)])
                        yield
                        yield
                    pb = nb()
                    _mm(s, ps[pb][:, 0:n], lwt[0:64, :], la[0:64, t0:t0 + n], True, True, r=[("lwt", sl)], w=[("ps", pb)])
                    yield
                    s.op("act", lambda e: e.activation(out=F["lw"][:, 0:n], in_=ps[pb][:, 0:n], func=AF.Sigmoid, bias=vcol("w0", c)),
                         r=[("ps", pb)], w=[fT("lw")])
                    yield
                    pb = nb()
                    _mm(s, ps[pb][:, 0:n], lwt[64:128, :], la[64:128, t0:t0 + n], True, True, r=[("lwt", sl)], w=[("ps", pb)])
                    yield
                    s.op("act", lambda e: e.activation(out=F["a"][:, 0:n], in_=ps[pb][:, 0:n], func=AF.Sigmoid, bias=vcol("a0", c)),
                         r=[("ps", pb)], w=[fT("a")])
                    yield
                    pb = nb()
                    _mm(s, ps[pb][:, 0:n], g2a[:, :], lg1[:, t0:t0 + n], True, False, r=[("g2a", sl)], w=[("ps", pb)], inc=False)
                    yield
                    _mm(s, ps[pb][:, 0:n], g2b[:, :], lg2[:, t0:t0 + n], False, True, r=[("g2b", sl)], w=[("ps", pb)])
                    yield
                    s.op("act", lambda e: e.activation(out=F["g"][:, 0:n], in_=ps[pb][:, 0:n], func=AF.Copy), r=[("ps", pb)], w=[fT("g")])
                    yield
                    yield
                    s.op("dve", lambda e: e.tensor_scalar_mul(F["kk"][:, 0:n], F["k"][:, 0:n], vcol("k_k", c)), r=[fT("k")], w=[fT("kk")])
                    yield
                    s.op("act", lambda e: e.activation(out=sqk[:, 0:n], in_=F["kk"][:, 0:n], func=AF.Square), r=[fT("kk")], w=[T_("sqk")])
                    yield
                    pb = nb()
                    _mm(s, ps[pb][:, 0:n], bo_b[:], sqk[:, 0:n], True, True, r=[T_("sqk")], w=[("ps", pb)])
                    yield
                    s.op("act", lambda e: e.activation(out=F["rn"][:, 0:n], in_=ps[pb][:, 0:n], func=AF.Sqrt, bias=1e-24), r=[("ps", pb)], w=[fT("rn")])
                    yield
                    s.op("dve", lambda e: e.reciprocal(out=F["rn"][:, 0:n], in_=F["rn"][:, 0:n]), r=[fT("rn")], w=[fT("rn")])
                    yield
                    s.op("dve", lambda e: e.tensor_tensor(out=F["kkn"][:, 0:n], in0=F["kk"][:, 0:n], in1=F["rn"][:, 0:n], op=ALU.mult),
                         r=[fT("kk"), fT("rn")], w=[fT("kkn")])
                    yield
                    yield
                    s.op("dve", lambda e: e.tensor_scalar(F["t"][:, 0:n], F["a"][:, 0:n], vcol("k_a", c), vecs2[:, c:c + 1], ALU.mult, ALU.add),
                         r=[fT("a")], w=[fT("t")])
                    yield
                    s.op("dve", lambda e: e.tensor_tensor(out=F["kmod"][:, 0:n], in0=F["k"][:, 0:n], in1=F["t"][:, 0:n], op=ALU.mult),
                         r=[fT("k"), fT("t")], w=[fT("kmod")])
                    yield
                    s.op("pool", lambda e: e.tensor_tensor(out=F["b"][:, 0:n], in0=F["kkn"][:, 0:n], in1=F["a"][:, 0:n], op=ALU.mult),
                         r=[fT("kkn"), fT("a")], w=[fT("b")])
                    yield
                    s.op("dve", lambda e: e.tensor_scalar_mul(F["lw"][:, 0:n], F["lw"][:, 0:n], LWC), r=[fT("lw")], w=[fT("lw")])
                    yield
                    s.op("dve", lambda e: e.tensor_tensor_scan(out=F["cw"][:, 0:n], data0=resetm[:, 0:n], data1=F["lw"][:, 0:n], initial=0.0,
                                                             op0=ALU.mult, op1=ALU.add), r=[fT("lw")], w=[fT("cw")])
                    yield
                    s.op("act", lambda e: e.activation(out=F["ep"][:, 0:n], in_=F["cw"][:, 0:n], func=AF.Exp), r=[fT("cw")], w=[fT("ep")])
                    yield
                    s.op("act", lambda e: e.activation(out=F["en"][:, 0:n], in_=F["cw"][:, 0:n], func=AF.Exp, scale=-1.0), r=[fT("cw")], w=[fT("en")])
                    yield
                    s.op("pool", lambda e: e.tensor_tensor(out=F["t"][:, 0:n], in0=F["cw"][:, 0:n], in1=F["lw"][:, 0:n], op=ALU.subtract),
                         r=[fT("cw"), fT("lw"), fT("t")], w=[fT("t")])
                    yield
                    s.op("act", lambda e: e.activation(out=F["epv"][:, 0:n], in_=F["t"][:, 0:n], func=AF.Exp), r=[fT("t")], w=[fT("epv")])
                    yield
                    for j in range(nch):
                        s.op("act", lambda e: e.activation(out=F["ee"][:, j * 128:(j + 1) * 128], in_=F["cw"][:, j * 128:(j + 1) * 128], func=AF.Exp,
                                                           scale=-1.0, bias=F["cw"][:, j * 128 + 127:j * 128 + 128]), r=[fT("cw"), fT("k")], w=[fT("ee")])
                        yield
                    yield
                    s.op("dve", lambda e: e.scalar_tensor_tensor(out=AR[:, 0:nch, 0:128], in0=v3(F["kkn"][:, 0:n], nch), scalar=-1.0,
                                                               in1=v3(F["epv"][:, 0:n], nch), op0=ALU.mult, op1=ALU.mult),
                         r=[fT("kkn"), fT("epv"), T_("AR")], w=[T_("AR")])
                    yield
                    s.op("dve", lambda e: e.tensor_tensor(out=AR[:, 0:nch, 128:256], in0=v3(F["r"][:, 0:n], nch), in1=v3(F["ep"][:, 0:n], nch), op=ALU.mult),
                         r=[fT("r"), fT("ep"), T_("AR")], w=[T_("AR")])
                    yield
                    s.op("pool", lambda e: e.tensor_tensor(out=ktb[:, 0:n], in0=F["kmod"][:, 0:n], in1=F["en"][:, 0:n], op=ALU.mult),
                         r=[fT("kmod"), fT("en")], w=[T_("ktb")])
                    yield
                    s.op("pool", lambda e: e.tensor_tensor(out=btb[:, 0:n], in0=F["b"][:, 0:n], in1=F["en"][:, 0:n], op=ALU.mult),
                         r=[fT("b"), fT("en")], w=[T_("btb")])
                    yield
                    s.op("pool", lambda e: e.tensor_tensor(out=F["kh"][:, 0:n], in0=F["kmod"][:, 0:n], in1=F["ee"][:, 0:n], op=ALU.mult),
                         r=[fT("kmod"), fT("ee"), fT("epv")], w=[fT("kh")])
                    yield
                    s.op("pool", lambda e: e.tensor_tensor(out=F["bh"][:, 0:n], in0=F["b"][:, 0:n], in1=F["ee"][:, 0:n], op=ALU.mult),
                         r=[fT("b"), fT("ee"), fT("en")], w=[fT("bh")])
                    yield
                    yield
                    for j in range(nch):
                        pb = nb()
                        cs = slice(j * 128, (j + 1) * 128)
                        for q_, nm in enumerate(("kh", "bh", "v")):
                            s.op("pe", lambda e: e.transpose(ps[pb][:, q_ * 128:(q_ + 1) * 128], F[nm][:, cs], ident[:]),
                                 r=[fT(nm)], w=[("ps", pb)], inc=(q_ == 2))
                            yield
                        s.op("act", lambda e: e.activation(out=KBV[:, j, :], in_=ps[pb][:, 0:384], func=AF.Copy), r=[("ps", pb), T_("KBV")], w=[T_("KBV")])
                        yield
                    yield
                    nbank = [None, None]
                    for hh in range(2):
                        nbank[hh] = nb()
                        R = slice(hh * 64, hh * 64 + 64)
                        for j in range(nch):
                            cs = slice(j * 128, (j + 1) * 128)
                            _mm(s, ps[nbank[hh]][:, cs], AR[R, j, 0:128], btb[R, cs], True, True, r=[T_("btb"), T_("AR")], w=[("ps", nbank[hh])])
                            yield
                        s.op("dve", lambda e: e.tensor_tensor(out=NN[:, hh, 0:n], in0=ps[nbank[hh]][:, 0:n], in1=maskl[:, 0:n], op=ALU.mult),
                             r=[("ps", nbank[hh]), T_("NN")], w=[T_("NN")])
                        yield
                    yield
                    for j in range(nch):
                        cs = slice(j * 128, (j + 1) * 128)
                        for hh in range(2):
                            R = slice(hh * 64, hh * 64 + 64)
                            px = nb()
                            _mm(s, ps[px][:, 0:256], btb[R, cs], AR[R, j, :], True, True, r=[T_("btb"), T_("AR")], w=[("ps", px)], inc=False)
                            yield
                            _mm(s, ps[px][:, 256:512], ktb[R, cs], AR[R, j, :], True, True, r=[T_("ktb"), T_("AR")], w=[("ps", px)])
                            yield
                            s.op("dve", lambda e: e.tensor_tensor(out=MM[hh][:, j, :], in0=ps[px][:, :], in1=mask4[:, :], op=ALU.mult),
                                 r=[("ps", px), ("MM", sl, hh)], w=[("MM", sl, hh)])
                            yield
                            yield
                    Mc, Nc, Zc = {}, {}, {}
                    for j in range(nch):
                        for hh in range(2):
                            Mc[j, hh] = MM[hh][:, j, 0:128]
                            Nc[j, hh] = NN[:, hh, j * 128:(j + 1) * 128]
                            s.op("pool", lambda e: e.tensor_tensor(out=Zb[:, j, 0, hh * 128:(hh + 1) * 128], in0=MM[hh][:, j, 0:128], in1=identb[:, :], op=ALU.add),
                                 r=[("MM", sl, hh), ("Zb", sl, j)], w=[("Zb", sl, j)])
                            yield
                            Zc[j, hh] = Zb[:, j, 0, hh * 128:(hh + 1) * 128]
                    for p in range(1, 7):
                        pp = p % 2
                        pmn = {}
                        for j in range(nch):
                            pmn[j] = nb()
                            for hh in range(2):
                                if p < 6:
                                    _mm(s, ps[pmn[j]][:, hh * 128:(hh + 1) * 128], Nc[j, hh], Mc[j, hh], True, True,
                                        r=[("MM", sl, hh), T_("NN"), ("MNb", sl, j)], w=[("ps", pmn[j])], inc=False)
                                    yield
                                _mm(s, ps[pmn[j]][:, 256 + hh * 128:256 + (hh + 1) * 128], Mc[j, hh], Nc[j, hh], True, True,
                                    r=[("MM", sl, hh), T_("NN"), ("MNb", sl, j)], w=[("ps", pmn[j])], inc=(hh == 1))
                                yield
                            lo = 0 if p < 6 else 256
                            s.op("act", lambda e: e.activation(out=MNb[:, j, pp, lo:512], in_=ps[pmn[j]][:, lo:512], func=AF.Copy),
                                 r=[("ps", pmn[j]), ("MNb", sl, j)], w=[("MNb", sl, j)])
                            yield
                            for hh in range(2):
                                Mc[j, hh] = MNb[:, j, pp, hh * 128:(hh + 1) * 128]
                                Nc[j, hh] = MNb[:, j, pp, 256 + hh * 128:256 + (hh + 1) * 128]
                        yield
                        pz = nb()
                        for j in range(nch):
                            zo = j * 256
                            for hh in range(2):
                                _mm(s, ps[pz][:, zo + hh * 128:zo + (hh + 1) * 128], Nc[j, hh], Zc[j, hh], True, True,
                                    r=[("MNb", sl, j), ("Zb", sl, j)], w=[("ps", pz)], inc=(hh == 1))
                                yield
                        for j in range(nch):
                            zo = j * 256
                            s.op("dve", lambda e: e.tensor_tensor(out=Zb[:, j, pp, :], in0=ps[pz][:, zo:zo + 256], in1=Zb[:, j, 1 - pp, :], op=ALU.add),
                                 r=[("ps", pz), ("Zb", sl, j)], w=[("Zb", sl, j)])
                            yield
                            for hh in range(2):
                                Zc[j, hh] = Zb[:, j, pp, hh * 128:(hh + 1) * 128]
                        yield
                    for j in range(nch):
                        cs = slice(j * 128, (j + 1) * 128)
                        Vh = [KBV[:, j, 256 + hh * 64:256 + (hh + 1) * 64] for hh in range(2)]
                        p1 = nb()
                        _mm(s, ps[p1][:, 0:128], AR[:, j, 0:128], Sb[:, :], True, False, r=[T_("AR"), T_("Sb")], w=[("ps", p1)], inc=False)
                        yield
                        for hh in range(2):
                            _mm(s, ps[p1][:, hh * 64:(hh + 1) * 64], MM[hh][:, j, 256:384], Vh[hh], False, hh == 1, r=[("MM", sl, hh), T_("KBV")], w=[("ps", p1)],
                                inc=(hh == 1))
                            yield
                        s.op("act", lambda e: e.activation(out=RH[:, :], in_=ps[p1][:, 0:128], func=AF.Copy), r=[("ps", p1), T_("RH")], w=[T_("RH")])
                        yield
                        yield
                        p2 = nb()
                        for hh in range(2):
                            _mm(s, ps[p2][:, hh * 64:(hh + 1) * 64], Zc[j, hh], RH[:, hh * 64:(hh + 1) * 64], True, True,
                                r=[("Zb", sl, j), T_("RH")], w=[("ps", p2)], inc=(hh == 1))
                            yield
                        s.op("act", lambda e: e.activation(out=UU[:, :], in_=ps[p2][:, 0:128], func=AF.Copy), r=[("ps", p2), T_("UU")], w=[T_("UU")])
                        yield
                        yield
                        if ti != 4:
                            _mm(s, ps[YB][:, cs], Sb[:, :], AR[:, j, 128:256], True, False, r=[T_("Sb"), T_("AR")], w=[("ps", YB)], inc=False)
                            yield
                            for hh in range(2):
                                R = slice(hh * 64, hh * 64 + 64)
                                _mm(s, ps[YB][R, cs], UU[:, hh * 64:(hh + 1) * 64], MM[hh][:, j, 128:256], False, False, r=[T_("UU"), ("MM", sl, hh)], w=[("ps", YB)], inc=False)
                                yield
                                _mm(s, ps[YB][R, cs], Vh[hh], MM[hh][:, j, 384:512], False, True, r=[T_("KBV"), ("MM", sl, hh)], w=[("ps", YB)], inc=(hh == 1))
                                yield
                        p3 = nb()
                        for hh in range(2):
                            R = slice(hh * 64, hh * 64 + 64)
                            _mm(s, ps[p3][R, 0:64], KBV[:, j, 128 + hh * 64:128 + (hh + 1) * 64], UU[:, hh * 64:(hh + 1) * 64], True, False,
                                r=[T_("KBV"), T_("UU")], w=[("ps", p3)], inc=False)
                            yield
                            _mm(s, ps[p3][R, 0:64], KBV[:, j, hh * 64:(hh + 1) * 64], Vh[hh], False, True, r=[T_("KBV")], w=[("ps", p3)], inc=(hh == 1))
                            yield
                        s.op("dve", lambda e: e.scalar_tensor_tensor(out=Sst[:, :], in0=Sst[:, :], scalar=F["ep"][:, j * 128 + 127:j * 128 + 128],
                                                                   in1=ps[p3][:, 0:64], op0=ALU.mult, op1=ALU.add),
                             r=[("ps", p3), T_("Sst"), fT("ep")], w=[T_("Sst")])
                        yield
                        for hh in range(2):
                            R = slice(hh * 64, hh * 64 + 64)
                            s.op("act", lambda e: e.activation(out=Sb[R, hh * 64:(hh + 1) * 64], in_=Sst[R, :], func=AF.Copy), r=[T_("Sst"), T_("Sb")], w=[T_("Sb")])
                            yield
                        yield
                    if ti == 4:
                        continue
                    s.op("act", lambda e: e.activation(out=F["y"][:, 0:n], in_=ps[YB][:, 0:n], func=AF.Copy), r=[("ps", YB), fT("y")], w=[fT("y")])
                    yield
                    s.op("act", lambda e: e.activation(out=yb[:, 0:n], in_=ps[YB][:, 0:n], func=AF.Copy), r=[("ps", YB), T_("yb")], w=[T_("yb")])
                    yield
                    pb = nb()
                    _mm(s, ps[pb][:, 0:n], bo64_b[:], yb[:, 0:n], True, True, r=[T_("yb")], w=[("ps", pb)])
                    yield
                    s.op("dve", lambda e: e.tensor_tensor(out=F["yc"][:, 0:n], in0=F["y"][:, 0:n], in1=ps[pb][:, 0:n], op=ALU.subtract),
                         r=[fT("y"), ("ps", pb)], w=[fT("yc")])
                    yield
                    s.op("act", lambda e: e.activation(out=yb[:, 0:n], in_=F["yc"][:, 0:n], func=AF.Square), r=[fT("yc"), T_("yb")], w=[T_("yb")])
                    yield
                    pb = nb()
                    _mm(s, ps[pb][:, 0:n], bo64_b[:], yb[:, 0:n], True, True, r=[T_("yb")], w=[("ps", pb)])
                    yield
                    s.op("act", lambda e: e.activation(out=F["y"][:, 0:n], in_=ps[pb][:, 0:n], func=AF.Sqrt, bias=GN_EPS), r=[("ps", pb), fT("y")], w=[fT("y")])
                    yield
                    s.op("dve", lambda e: e.reciprocal(out=F["y"][:, 0:n], in_=F["y"][:, 0:n]), r=[fT("y")], w=[fT("y")])
                    yield
                    s.op("dve", lambda e: e.tensor_tensor(out=F["yc"][:, 0:n], in0=F["yc"][:, 0:n], in1=F["y"][:, 0:n], op=ALU.mult),
                         r=[fT("yc"), fT("y")], w=[fT("yc")])
                    yield
                    s.op("dve", lambda e: e.tensor_scalar(F["yc"][:, 0:n], F["yc"][:, 0:n], vcol("lnx_w", c), vcol("lnx_b", c), ALU.mult, ALU.add),
                         r=[fT("yc")], w=[fT("yc")])
                    yield
                    yield
                    s.op("pool", lambda e: e.tensor_tensor(out=F["t"][:, 0:n], in0=F["r"][:, 0:n], in1=F["kmod"][:, 0:n], op=ALU.mult),
                         r=[fT("r"), fT("kmod"), fT("t")], w=[fT("t")])
                    yield
                    s.op("dve", lambda e: e.tensor_scalar_mul(yb[:, 0:n], F["t"][:, 0:n], vcol("r_k", c)), r=[fT("t"), T_("yb")], w=[T_("yb")])
                    yield
                    pb = nb()
                    _mm(s, ps[pb][:, 0:n], bo_b[:], yb[:, 0:n], True, True, r=[T_("yb")], w=[("ps", pb)])
                    yield
                    s.op("dve", lambda e: e.tensor_tensor(out=F["y"][:, 0:n], in0=ps[pb][:, 0:n], in1=F["v"][:, 0:n], op=ALU.mult),
                         r=[("ps", pb), fT("v"), fT("y")], w=[fT("y")])
                    yield
                    s.op("dve", lambda e: e.tensor_tensor(out=F["yc"][:, 0:n], in0=F["yc"][:, 0:n], in1=F["y"][:, 0:n], op=ALU.add),
                         r=[fT("yc"), fT("y")], w=[fT("yc")])
                    yield
                    s.op("dve", lambda e: e.tensor_tensor(out=rwo_sl(ti, c, off, off + n), in0=F["yc"][:, 0:n], in1=F["g"][:, 0:n], op=ALU.mult),
                         r=[fT("yc"), fT("g")], w=[("rwo", ti, c)])
                    yield
                    yield

            for c0 in range(0, KC, NSL):
                gens = [m4_pair(c0 + i, i) for i in range(NSL)]
                live = list(gens)
                lead = int(os.environ.get("K_LEAD", "0"))
                for _ in range(lead):
                    try:
                        next(gens[0])
                    except StopIteration:
                        live.remove(gens[0])
                        break
                while live:
                    for g_ in list(live):
                        try:
                            next(g_)
                        except StopIteration:
                            live.remove(g_)
            s.barrier()
        if dbg in ("C", "BC") and sq == 0:
            for (ti, t0, n) in REAL:
                for c in range(KC):
                    s.dma("pool", dbg_t[1, :, c, t0:t0 + n], rwo_sl(ti, c), key="dbg")
            s.barrier()
            return

        with contextlib.ExitStack() as s5:
            def sb5(name, shape, dt):
                return s5.enter_context(_sbt(nc, name, list(shape), dt))
            mg = sb5("mg", [128, KC, SEQ], BF16)
            w5 = [[sb5("w5_%d_%d" % (i, j), [128, KC, 128], BF16) for j in range(4)] for i in range(2)]
            sga = [sb5("sga%d" % i, [128, 512], F32) for i in range(2)]
            sgr = [sb5("sgr%d" % i, [128, 512], F32) for i in range(2)]
            WA, WR, WO = D["w_attn_branch"][0], D["w_rwkv_branch"][0], D["w_out"][0]
            unit = 0
            for oc in range(KC):
                wsl = oc % 2
                cs = slice(oc * 128, (oc + 1) * 128)
                srcs = [WA[:, cs].rearrange("(k p) f -> p k f", p=128), WR[:, cs].rearrange("(k p) f -> p k f", p=128),
                        wchunk_src(GA0 + oc * 128, 128), wchunk_src(GR0 + oc * 128, 128)]
                for j in range(4):
                    s.dma("pool", w5[wsl][j][:, :, :], srcs[j], w=[("w5", wsl, j)], key=("w5", wsl, j))
                for (ti, t0, n) in REAL:
                    b0 = 4 * (unit % 2)
                    u2 = unit % 2
                    for k in range(KC):
                        _mm(s, ps[b0][:, :], w5[wsl][0][:, k, :], att_sl(ti, k), k == 0, k == KC - 1, r=[("w5", wsl, 0), ("att", ti, k)], w=[("ps", b0)])
                    for k in range(KC):
                        _mm(s, ps[b0 + 1][:, :], w5[wsl][1][:, k, :], rwo_sl(ti, k), k == 0, k == KC - 1, r=[("w5", wsl, 1), ("rwo", ti, k)], w=[("ps", b0 + 1)])
                    proj(w5[wsl][2], ("w5", wsl, 2), t0, n, b0 + 2)
                    proj(w5[wsl][3], ("w5", wsl, 3), t0, n, b0 + 3)
                    s.op("act", lambda e: e.activation(out=sga[u2][:, :], in_=ps[b0 + 2][:, :], func=AF.Sigmoid), r=[("ps", b0 + 2)], w=[("sga", u2)])
                    s.op("act", lambda e: e.activation(out=sgr[u2][:, :], in_=ps[b0 + 3][:, :], func=AF.Sigmoid), r=[("ps", b0 + 3)], w=[("sgr", u2)])
                    s.op("dve", lambda e: e.tensor_tensor(out=sga[u2][:, :], in0=ps[b0][:, :], in1=sga[u2][:, :], op=ALU.mult),
                         r=[("ps", b0), ("sga", u2)], w=[("sga", u2)])
                    s.op("dve", lambda e: e.tensor_tensor(out=sgr[u2][:, :], in0=ps[b0 + 1][:, :], in1=sgr[u2][:, :], op=ALU.mult),
                         r=[("ps", b0 + 1), ("sgr", u2)], w=[("sgr", u2)])
                    s.op("pool", lambda e: e.tensor_tensor(out=mg[:, oc, ti * 512:(ti + 1) * 512], in0=sga[u2][:, :], in1=sgr[u2][:, :], op=ALU.add),
                         r=[("sga", u2), ("sgr", u2)], w=[("mg", ti, oc)])
                    unit += 1
            s.barrier()
            for (ti, t0, n) in REAL:
                s.dma("sp", hall(ti), hspill[:, ti * KC * 512:(ti + 1) * KC * 512].rearrange("p (c t) -> p c t", c=KC),
                      w=[("hT", ti, cc) for cc in range(KC)], key=("hre", ti))
            for oc in range(KC):
                wsl = oc % 2
                s.dma("pool", w5[wsl][0][:, :, :], WO[:, oc * 128:(oc + 1) * 128].rearrange("(k p) f -> p k f", p=128),
                      w=[("w5", wsl, 0)], key=("w5", wsl, 0))
                for (ti, t0, n) in REAL:
                    pb = unit % 4
                    for k in range(KC):
                        _mm(s, ps[pb][:, :], w5[wsl][0][:, k, :], mg[:, k, ti * 512:(ti + 1) * 512], k == 0, k == KC - 1,
                            r=[("w5", wsl, 0), ("mg", ti, k)], w=[("ps", pb)])
                    s.op("dve", lambda e: e.tensor_tensor(out=hsl(ti, oc), in0=ps[pb][:, :], in1=hsl(ti, oc), op=ALU.add),
                         r=[("ps", pb), ("hT", ti, oc)], w=[("hT", ti, oc)])
                    unit += 1
            s.barrier()

def host_consts():
    c = {}
    c["c_ident"] = np.eye(128, dtype=np.float32)
    p = np.arange(128)
    perm = np.zeros((128, 128), np.float32)
    partner = (p // 64) * 64 + ((p % 64) + 32) % 64
    perm[partner, p] = 1.0
    c["c_perm"] = perm
    i = np.arange(128)[:, None]
    t = np.arange(128)[None, :]
    su = (t > i).astype(np.float32)
    iu = (t >= i).astype(np.float32)
    c["c_mask4"] = np.concatenate([su, iu, su, iu], axis=1)
    sl = (t < i).astype(np.float32)
    c["c_maskl"] = np.concatenate([sl, sl, sl, sl], axis=1)
    meta = np.broadcast_to((i >= 112), (128, 128)).astype(np.float32)
    prev = (i > t).astype(np.float32)
    cur = (i <= t).astype(np.float32)
    c["c_mask3"] = np.concatenate([meta, prev, cur], axis=1)
    c["c_mask3f"] = np.concatenate([meta, np.zeros_like(prev), cur], axis=1)
    bo = np.zeros((128, 128), np.float32)
    bo[:64, :64] = 1.0
    bo[64:, 64:] = 1.0
    c["c_bo"] = bo
    rs = np.ones((128, 512), np.float32)
    rs[:, 0::128] = 0.0
    c["c_reset"] = rs
    half = 32
    inv = (10000.0 ** (-np.arange(half, dtype=np.float32) / half)).astype(np.float32)
    pos = (np.arange(T) - 112).astype(np.float32)
    ang = pos[None, :] * inv[(p % 32)][:, None]
    c["c_cos"] = np.cos(ang).astype(np.float32)
    sgn = np.where((p % 64) < 32, -1.0, 1.0).astype(np.float32)[:, None]
    c["c_sin"] = (np.sin(ang) * sgn).astype(np.float32)
    return c


CONST_SHAPES = {"c_ident": (128, 128), "c_perm": (128, 128), "c_mask4": (128, 512), "c_maskl": (128, 512),
                "c_mask3": (128, 384), "c_mask3f": (128, 384), "c_bo": (128, 128), "c_reset": (128, 512),
                "c_cos": (128, T), "c_sin": (128, T)}


def build_nc(dbg=None, nseq=SPC, skip=False):
    nc = bass.Bass("TRN2", target_bir_lowering=False)
    D = {}

    def inp(name, shape):
        D[name] = nc.dram_tensor(name, list(shape), F32, kind="ExternalInput").ap()

    inp("x", (SPC, SEQ, DM))
    inp("meta_tokens", (16, DM))
    inp("norm_ffn1", (1, DM))
    inp("ffn1_w_in", (1, DM, 2 * FF))
    inp("ffn1_w_out", (1, FF, DM))
    inp("norm_mix", (1, DM))
    inp("w_in", (1, DM, INC))
    inp("rwkv_mu", (1, RWC))
    inp("sinks", (1, NQH))
    inp("w0", (1, 1024))
    inp("w2", (1, LW, 1024))
    inp("a0", (1, 1024))
    inp("a2", (1, LA, 1024))
    inp("g2", (1, LG, 1024))
    inp("k_k", (1, 1024))
    inp("k_a", (1, 1024))
    inp("r_k", (1, 16, 64))
    inp("lnx_w", (1, 1024))
    inp("lnx_b", (1, 1024))
    inp("w_attn_branch", (1, 1024, DM))
    inp("w_rwkv_branch", (1, 1024, DM))
    inp("w_out", (1, DM, DM))
    inp("norm_ffn2", (1, DM))
    inp("ffn2_w_in", (1, DM, 2 * FF))
    inp("ffn2_w_out", (1, FF, DM))
    inp("norm_final", (DM,))
    for k, shp in CONST_SHAPES.items():
        inp(k, shp)
    out = nc.dram_tensor("out", [SPC, SEQ, DM], F32, kind="ExternalOutput").ap()
    hspill = nc.dram_tensor("hspill", [128, 4 * KC * 512], F32, kind="Internal").ap()
    if dbg:
        dbg_t = nc.dram_tensor("dbg", [2, 128, KC, T], F32, kind="ExternalOutput").ap()

    W_IN = D["w_in"][0]

    with contextlib.ExitStack() as st:
        s = Sched(nc, st)

        def sb(name, shape, dt):
            return st.enter_context(_sbt(nc, name, list(shape), dt))

        ident = sb("ident", [128, 128], F32)
        ones_b = sb("ones_b", [128, 128], BF16)
        perm_b = sb("perm_b", [128, 128], BF16)
        bo_b = sb("bo_b", [128, 128], BF16)
        bo64_b = sb("bo64_b", [128, 128], BF16)
        mask4 = sb("mask4", [128, 512], BF16)
        maskl = sb("maskl", [128, 512], BF16)
        mask3 = sb("mask3", [128, 384], BF16)
        mask3f = sb("mask3f", [128, 384], BF16)
        resetm = sb("resetm", [128, 512], F32)
        vecs = sb("vecs", [128, 128], F32)
        vecs2 = sb("vecs2", [128, 32], F32)
        vstage = sb("vstage", [128, 128], F32)
        ps = [st.enter_context(nc.psum_tensor("ps%d" % i, [128, 512], F32)) for i in range(8)]

        s.dma("sp", ident[:], D["c_ident"][:, :], key="c0")
        s.dma("sp", resetm[:], D["c_reset"][:, :], key="c0")
        s.dma("pool", perm_b[:], D["c_perm"][:, :], key="c1")
        s.dma("pool", bo_b[:], D["c_bo"][:, :], key="c1")
        s.dma("pool", mask4[:], D["c_mask4"][:, :], key="c1")
        s.dma("pool", maskl[:], D["c_maskl"][:, :], key="c1")
        s.dma("pool", mask3[:], D["c_mask3"][:, :], key="c1")
        s.dma("pool", mask3f[:], D["c_mask3f"][:, :], key="c1")
        s.op("dve", lambda e: e.memset(ones_b[:], 1.0))
        s.op("dve", lambda e: e.memset(vstage[:], 0.0))
        s.barrier()
        VROW = {}
        row = 0
        for nm in ("norm_ffn1", "norm_mix", "norm_ffn2", "norm_final", "w0", "a0", "k_k", "k_a",
                   "r_k", "lnx_w", "lnx_b"):
            ap = D[nm]
            if nm == "norm_final":
                src = ap.rearrange("(c p) -> c p", p=128)
            elif nm == "r_k":
                src = ap[0].rearrange("(c h) k -> c (h k)", h=2)
            else:
                src = ap[0].rearrange("(c p) -> c p", p=128)
            s.dma("sp", vstage[row:row + 8, :], src, key="c0")
            VROW[nm] = row
            row += 8
        mu = D["rwkv_mu"][0]
        s.dma("sp", vstage[row:row + 26, :], mu[0:26 * 128].rearrange("(c p) -> c p", p=128), key="c0")
        s.dma("sp", vstage[row + 26:row + 27, 0:32], mu[26 * 128:RWC].rearrange("(c p) -> c p", p=32), key="c0")
        VROW["mu"] = row
        row += 27
        assert row <= 128
        sk = D["sinks"][0].rearrange("(c h) -> h c", h=2)
        import os
        for hf in range(2):
            if os.environ.get("K_SKIPC"):
                continue
            s.dma("sp", vecs2[hf * 64:(hf + 1) * 64, 8:16], sk[hf:hf + 1, :].broadcast_to([64, 8]), key="c0",
                  allow_slow_non_contiguous=True)
        s.barrier()
        s.op("pe", lambda e: e.transpose(ps[0][:, 0:128], vstage[:], ident[:]))
        s.barrier()
        s.op("dve", lambda e: e.tensor_copy(out=vecs[:], in_=ps[0][:, 0:128]))
        s.op("act", lambda e: e.activation(out=vecs2[:, 8:16], in_=vecs2[:, 8:16], func=AF.Exp))
        s.op("act", lambda e: e.mul(out=bo64_b[:], in_=bo_b[:], mul=1.0 / 64))
        s.barrier()

        def vcol(nm, c):
            j = VROW[nm] + c
            return vecs[:, j:j + 1]

        s.op("dve", lambda e: e.tensor_scalar(vecs2[:, 0:8], vecs[:, VROW["k_a"]:VROW["k_a"] + 8], -1.0, 1.0, ALU.mult, ALU.add))
        s.barrier()

        hTf = sb("hTf", [128, 4 * KC * 512 + KC * 128], F32)
        hTb = hTf.bitcast(BF16)
        xnT = sb("xnT", [128, KC, T], BF16)

        REAL = [(i, 128 + 512 * i, 512) for i in range(4)]
        META = (4, 0, 128)
        ALLT = REAL + [META]

        def hsl(ti, c, a=0, b=None):
            if ti == 4:
                base = 4 * KC * 512 + c * 128
                n = 128
            else:
                base = (ti * KC + c) * 512
                n = 512
            if b is None:
                b = n
            return hTf[:, base + a:base + b]

        def hall(ti):
            if ti == 4:
                return hTf[:, 4 * KC * 512:4 * KC * 512 + KC * 128].rearrange("p (c t) -> p c t", c=KC)
            return hTf[:, ti * KC * 512:(ti + 1) * KC * 512].rearrange("p (c t) -> p c t", c=KC)

        def att_sl(ti, c, a=0, b=512):
            base = ti * 8192 + c * 512
            return hTb[:, base + a:base + b]

        def rwo_sl(ti, c, a=0, b=512):
            base = ti * 8192 + 4096 + c * 512
            return hTb[:, base + a:base + b]

        def hT_toks(ti):
            return [("hT", ti, c) for c in range(KC)]

        for sq in range(nseq):
            with contextlib.ExitStack() as st1:
                xs = [st1.enter_context(_sbt(nc, "xs%d" % i, [128, DM], F32)) for i in range(2)]
                import os
                for n in range(NBLK):
                    if os.environ.get("K_SKIPA"):
                        continue
                    slot = n % 2
                    xt = xs[slot]
                    if n == 0:
                        s.op("dve", lambda e: e.memset(xt[:], 0.0), w=[("xs", slot)])
                        s.dma("sp", xt[112:128, :], D["meta_tokens"][:, :], w=[("xs", slot)], key=("xs", slot))
                        ti, off = 4, 0
                    else:
                        s.dma("sp", xt[:], D["x"][sq, (n - 1) * 128:n * 128, :], w=[("xs", slot)], key=("xs", slot))
                        ti, off = (n - 1) // 4, ((n - 1) % 4) * 128
                    for half in range(2):
                        pb = (2 * n + half) % 4
                        for j in range(4):
                            c = half * 4 + j
                            s.op("pe", lambda e: e.transpose(ps[pb][:, j * 128:(j + 1) * 128], xt[:, c * 128:(c + 1) * 128], ident[:]),
                                 r=[("xs", slot)], w=[("ps", pb)], inc=(j == 3))
                        s.op("act", lambda e: e.activation(out=hall(ti)[:, half * 4:half * 4 + 4, off:off + 128],
                                                           in_=ps[pb][:].rearrange("p (j t) -> p j t", j=4), func=AF.Copy),
                             r=[("ps", pb)], w=[("hT", ti, half * 4 + jj) for jj in range(4)])
                s.barrier()

            def rmsnorm_tiles(gname, tiles, bufs, write):
                sqb, rstd = bufs
                for (ti, t0, n) in tiles:
                    s.op("act", lambda e: e.activation(out=sqb[:, :, 0:n], in_=hall(ti), func=AF.Square),
                         r=hT_toks(ti), w=["sqb"])
                    for c in range(KC):
                        _mm(s, ps[0][:, 0:n], ones_b[:], sqb[:, c, 0:n], c == 0, c == KC - 1, r=["sqb"], w=[("ps", 0)])
                    s.op("act", lambda e: e.activation(out=rstd[:, 0:n], in_=ps[0][:, 0:n], func=AF.Sqrt, bias=RMS_EPS, scale=1.0 / DM),
                         r=[("ps", 0)], w=["rstd"])
                    s.op("dve", lambda e: e.reciprocal(out=rstd[:, 0:n], in_=rstd[:, 0:n]), r=["rstd"], w=["rstd"])
                    for c in range(KC):
                        write(ti, t0, n, c, rstd, gname)

            def write_xn(ti, t0, n, c, rstd, gname):
                s.op("dve", lambda e: e.scalar_tensor_tensor(out=xnT[:, c, t0:t0 + n], in0=hsl(ti, c), scalar=vcol(gname, c),
                                                           in1=rstd[:, 0:n], op0=ALU.mult, op1=ALU.mult),
                     r=[("hT", ti, c), "rstd"], w=[("xn", ti, c)])

            def ffn(w_in_ap, w_out_ap, tiles):
                NF = 4
                blocks = []
                f0 = 0
                while f0 < FC:
                    nf = min(NF, FC - f0)
                    blocks.append((f0, nf))
                    f0 += nf
                with contextlib.ExitStack() as st2:
                    wg = [st2.enter_context(_sbt(nc, "wg%d" % i, [128, KC, NF * 128], BF16)) for i in range(2)]
                    wu = [st2.enter_context(_sbt(nc, "wu%d" % i, [128, KC, NF * 128], BF16)) for i in range(2)]
                    wd = [st2.enter_context(_sbt(nc, "wd%d" % i, [128, NF, DM], BF16)) for i in range(2)]
                    h1 = [st2.enter_context(_sbt(nc, "h1%d" % i, [128, NF, 512], BF16)) for i in range(2)]
                    sg = [st2.enter_context(_sbt(nc, "sg%d" % i, [128, 512], BF16)) for i in range(2)]
                    pending = [None]
                    unit = 0
                    for bi, (f0, nf) in enumerate(blocks):
                        ws = bi % 2
                        s.dma("pool", wg[ws][:, :, 0:nf * 128],
                              w_in_ap[:, f0 * 128:(f0 + nf) * 128].rearrange("(k p) f -> p k f", p=128),
                              w=[("wg", ws)], key=("wg", ws))
                        s.dma("pool", wu[ws][:, :, 0:nf * 128],
                              w_in_ap[:, FF + f0 * 128:FF + (f0 + nf) * 128].rearrange("(k p) f -> p k f", p=128),
                              w=[("wu", ws)], key=("wu", ws))
                        s.dma("pool", wd[ws][:, 0:nf, :],
                              w_out_ap[f0 * 128:(f0 + nf) * 128, :].rearrange("(c p) d -> p c d", p=128),
                              w=[("wd", ws)], key=("wd", ws))
                        for (ti, t0, n) in tiles:
                            hs = unit % 2
                            for j in range(nf):
                                pa = 2 * (j % 2)
                                for k in range(KC):
                                    _mm(s, ps[pa][:, 0:n], wg[ws][:, k, j * 128:(j + 1) * 128], xnT[:, k, t0:t0 + n],
                                        k == 0, k == KC - 1, r=[("wg", ws), ("xn", ti, k)], w=[("ps", pa)])
                                for k in range(KC):
                                    _mm(s, ps[pa + 1][:, 0:n], wu[ws][:, k, j * 128:(j + 1) * 128], xnT[:, k, t0:t0 + n],
                                        k == 0, k == KC - 1, r=[("wu", ws), ("xn", ti, k)], w=[("ps", pa + 1)])
                                sgj = sg[j % 2]
                                s.op("act", lambda e: e.activation(out=sgj[:, 0:n], in_=ps[pa][:, 0:n], func=AF.Silu),
                                     r=[("ps", pa)], w=[("sg", j % 2)])
                                s.op("dve", lambda e: e.tensor_tensor(out=h1[hs][:, j, 0:n], in0=ps[pa + 1][:, 0:n], in1=sgj[:, 0:n], op=ALU.mult),
                                     r=[("ps", pa + 1), ("sg", j % 2)], w=[("h1", hs)])

                            def down(ws=ws, hs=hs, nf=nf, ti=ti, n=n):
                                for dc in range(KC):
                                    pb = 4 + dc % 4
                                    for j in range(nf):
                                        _mm(s, ps[pb][:, 0:n], wd[ws][:, j, dc * 128:(dc + 1) * 128], h1[hs][:, j, 0:n],
                                            j == 0, j == nf - 1, r=[("wd", ws), ("h1", hs)], w=[("ps", pb)])
                                    s.op("dve", lambda e: e.scalar_tensor_tensor(out=hsl(ti, dc), in0=ps[pb][:, 0:n], scalar=0.5,
                                                                               in1=hsl(ti, dc), op0=ALU.mult, op1=ALU.add),
                                         r=[("ps", pb), ("hT", ti, dc)], w=[("hT", ti, dc)])
                            if pending[0] is not None:
                                pending[0]()
                            pending[0] = down
                            unit += 1
                    pending[0]()
                    s.barrier()

            with contextlib.ExitStack() as st3:
                bufs = (st3.enter_context(_sbt(nc, "sqb", [128, KC, 512], BF16)),
                        st3.enter_context(_sbt(nc, "rstd", [128, 512], F32)))
                if not os.environ.get("K_SKIPA"):
                    rmsnorm_tiles("norm_ffn1", ALLT, bufs, write_xn)
                if not skip:
                    ffn(D["ffn1_w_in"][0], D["ffn1_w_out"][0], ALLT)
                if not os.environ.get("K_SKIPA"):
                    rmsnorm_tiles("norm_mix", ALLT, bufs, write_xn)
                s.barrier()
            if dbg == "A" and sq == 0:
                for (ti, t0, n) in ALLT:
                    s.dma("sp", dbg_t[0, :, :, t0:t0 + n], hall(ti), key="dbg")
                s.barrier()
                break
            for ti_ in range(4):
                if os.environ.get("K_SKIPSP"):
                    continue
                s.dma("sp", hspill[:, ti_ * 4096:(ti_ + 1) * 4096], hTf[:, ti_ * 4096:(ti_ + 1) * 4096], key=("spill", ti_))
            s.barrier()

            uT = xnT
            mixer(nc, s, D, W_IN, ps, uT, att_sl, rwo_sl, vcol, vecs, vecs2, VROW,
                  dict(ident=ident, ones_b=ones_b, perm_b=perm_b, bo_b=bo_b, bo64_b=bo64_b, mask4=mask4, maskl=maskl,
                       mask3=mask3, mask3f=mask3f, resetm=resetm), dbg, sq, dbg_t if dbg else None, REAL, hsl, hall, hspill)
            if dbg in ("B", "C", "K", "BC", "M1", "M2", "M2b", "M2a") and sq == 0:
                break

            with contextlib.ExitStack() as st3:
                bufs = (st3.enter_context(_sbt(nc, "sqb", [128, KC, 512], BF16)),
                        st3.enter_context(_sbt(nc, "rstd", [128, 512], F32)))
                rmsnorm_tiles("norm_ffn2", REAL, bufs, write_xn)
                ffn(D["ffn2_w_in"][0], D["ffn2_w_out"][0], REAL)
                ot = [st3.enter_context(_sbt(nc, "ot%d" % i, [128, DM], F32)) for i in range(2)]
                yf = st3.enter_context(_sbt(nc, "yf", [128, KC, 512], F32))

                def write_y(ti, t0, n, c, rstd, gname):
                    s.op("dve", lambda e: e.scalar_tensor_tensor(out=yf[:, c, 0:n], in0=hsl(ti, c), scalar=vcol(gname, c),
                                                               in1=rstd[:, 0:n], op0=ALU.mult, op1=ALU.mult),
                         r=[("hT", ti, c), "rstd"], w=[("yf", c)])
                blkc = 0
                for tile in REAL:
                    rmsnorm_tiles("norm_final", [tile], bufs, write_y)
                    ti, t0, n = tile
                    for b4 in range(4):
                        osl = blkc % 2
                        for half in range(2):
                            pb = 4 + (2 * blkc + half) % 4
                            for j in range(4):
                                c = half * 4 + j
                                s.op("pe", lambda e: e.transpose(ps[pb][:, j * 128:(j + 1) * 128], yf[:, c, b4 * 128:(b4 + 1) * 128], ident[:]),
                                     r=[("yf", c)], w=[("ps", pb)], inc=(j == 3))
                            s.op("act", lambda e: e.activation(out=ot[osl][:, half * 512:(half + 1) * 512], in_=ps[pb][:], func=AF.Copy),
                                 r=[("ps", pb)], w=[("ot", osl)])
                        blk = (t0 - 128) // 128 + b4
                        s.dma("sp", out[sq, blk * 128:(blk + 1) * 128, :], ot[osl][:], r=[("ot", osl)], key=("ot", osl))
                        blkc += 1
                s.barrier()
        s.finish()
    return nc


def kernel(**inputs):
    dbg = inputs.pop("_dbg", None)
    x = np.ascontiguousarray(np.asarray(inputs["x"], dtype=np.float32))
    nc = build_nc(dbg)
    common = {k: np.ascontiguousarray(np.asarray(v, dtype=np.float32)) for k, v in inputs.items() if k != "x"}
    common.update(host_consts())
    in_maps = []
    for c in range(NCORES):
        m = dict(common)
        m["x"] = x[c * SPC:(c + 1) * SPC]
        in_maps.append(m)
    res = run_bass_kernel_spmd(nc, in_maps, core_ids=list(range(NCORES)))
    if dbg:
        return [r["dbg"] for r in res.results]
    return np.concatenate([r["out"] for r in res.results], axis=0)
```
